# Optimizing a Trainium2 kernel written in Bass

```python
import jax, jax.numpy as jnp
from jax import lax
import numpy as np

D_MODEL = 2048
BATCH = 4
SEQ = 2048
DEPTH = 1
DEC_BATCH = 128
DEC_SEQ = 8
PAST_LEN = 16384
PAGE_SIZE = 128

META_TOKENS = 16
D_INNER = 2 * D_MODEL
SSD_HEAD_DIM = 64
SSD_HEADS = D_INNER // SSD_HEAD_DIM
SSD_STATE = 128
SSD_GROUPS = 8
SSD_CONV_W = 4
SSD_CHUNK = 128
SSD_CONV_DIM = D_INNER + 2 * SSD_GROUPS * SSD_STATE
SC_WIDTH = D_MODEL
SC_CONV_W = 3
PROJ_COLS = D_INNER + SSD_CONV_DIM + SSD_HEADS + 4 * SC_WIDTH + 2 * D_MODEL
EPS = 1e-6

kernel_name = "hybrid_ssd_shortconv_gated_decode_step"


def rms_norm(x, w):
    xf = x.astype(jnp.float32)
    xf = xf * lax.rsqrt(jnp.mean(xf * xf, axis=-1, keepdims=True) + EPS)
    return (xf * w.astype(jnp.float32)).astype(x.dtype)


def gated_group_rms_norm(y, z, w):
    g = y.astype(jnp.float32) * jax.nn.silu(z.astype(jnp.float32))
    shp = g.shape
    g = g.reshape(shp[:-1] + (SSD_GROUPS, shp[-1] // SSD_GROUPS))
    g = g * lax.rsqrt(jnp.mean(g * g, axis=-1, keepdims=True) + EPS)
    return (g.reshape(shp) * w.astype(jnp.float32)).astype(y.dtype)


def causal_dwconv(x, prev, w):
    width = w.shape[0]
    length = x.shape[1]
    xp = jnp.concatenate([prev.astype(x.dtype), x], axis=1)
    out = xp[:, 0:length] * w[0]
    for k in range(1, width):
        out = out + xp[:, k:k + length] * w[k]
    return out, xp[:, length:]


def ssd_chunked(xs, dt, log_a, bm, cm, h0, chunk):
    f32 = jnp.float32
    bsz, length, nh, hp = xs.shape
    ng, ns = bm.shape[2], bm.shape[3]
    hr = nh // ng
    nc = length // chunk
    xc = xs.astype(f32).reshape(bsz, nc, chunk, ng, hr, hp)
    dtc = dt.astype(f32).reshape(bsz, nc, chunk, ng, hr)
    acum = jnp.cumsum(log_a.astype(f32).reshape(bsz, nc, chunk, ng, hr), axis=2)
    bc = bm.astype(f32).reshape(bsz, nc, chunk, ng, ns)
    cc = cm.astype(f32).reshape(bsz, nc, chunk, ng, ns)
    causal = jnp.tril(jnp.ones((chunk, chunk), dtype=bool))[:, :, None, None]
    seg = acum[:, :, :, None] - acum[:, :, None, :]
    decay = jnp.exp(jnp.where(causal, seg, -jnp.inf))
    cb = jnp.einsum('bcign,bcjgn->bcijg', cc, bc)
    wts = cb[..., None] * decay * dtc[:, :, None]
    y_diag = jnp.einsum('bcijgr,bcjgrp->bcigrp', wts, xc)
    to_end = jnp.exp(acum[:, :, -1:] - acum) * dtc
    chunk_states = jnp.einsum('bcjgr,bcjgrp,bcjgn->bcgrpn', to_end, xc, bc)
    chunk_decay = jnp.exp(acum[:, :, -1])

    def step(h, inp):
        dec, st = inp
        return dec[..., None, None] * h + st, h

    h_last, h_in = lax.scan(step, h0.astype(f32).reshape(bsz, ng, hr, hp, ns),
                            (jnp.moveaxis(chunk_decay, 1, 0), jnp.moveaxis(chunk_states, 1, 0)))
    h_in = jnp.moveaxis(h_in, 0, 1)
    y_off = jnp.einsum('bcign,bcgrpn->bcigrp', cc, h_in) * jnp.exp(acum)[..., None]
    y = (y_diag + y_off).reshape(bsz, length, nh, hp).astype(xs.dtype)
    return y, h_last.reshape(bsz, nh, hp, ns).astype(h0.dtype)


def split_proj(proj):
    sizes = [D_INNER, SSD_CONV_DIM, SSD_HEADS, SC_WIDTH, SC_WIDTH, SC_WIDTH, SC_WIDTH, D_MODEL]
    idx = []
    acc = 0
    for s in sizes:
        acc += s
        idx.append(acc)
    return jnp.split(proj, idx, axis=-1)


def mixer_layer(x, conv_prev, ssm_prev, sconv_prev, segments, norm_w, w_in, ssd_conv_w,
                ssd_conv_b, dt_bias, a_log, d_skip, ssd_norm_w, w_ssd_out, sconv_w,
                w_sconv_out, w_o):
    bsz, length = x.shape[0], x.shape[1]
    h = rms_norm(x, norm_w)
    proj = h @ w_in
    z, xbc, dt_raw, sc_b, sc_c, sc_h, sc_z, g_a, g_b = split_proj(proj)

    xbc_c, conv_new = causal_dwconv(xbc, conv_prev, ssd_conv_w)
    xbc = jax.nn.silu(xbc_c + ssd_conv_b)
    xs, bm, cm = jnp.split(xbc, [D_INNER, D_INNER + SSD_GROUPS * SSD_STATE], axis=-1)
    xs = xs.reshape(bsz, length, SSD_HEADS, SSD_HEAD_DIM)
    bm = bm.reshape(bsz, length, SSD_GROUPS, SSD_STATE)
    cm = cm.reshape(bsz, length, SSD_GROUPS, SSD_STATE)
    dt = jax.nn.softplus(dt_raw.astype(jnp.float32) + dt_bias.astype(jnp.float32))
    log_a = dt * (-jnp.exp(a_log.astype(jnp.float32)))
    state = ssm_prev
    ys = []
    start = 0
    for seg_len, chunk in segments:
        sl = slice(start, start + seg_len)
        y_seg, state = ssd_chunked(xs[:, sl], dt[:, sl], log_a[:, sl], bm[:, sl],
                                   cm[:, sl], state, chunk)
        ys.append(y_seg)
        start += seg_len
    y = jnp.concatenate(ys, axis=1) + d_skip[:, None] * xs
    y = gated_group_rms_norm(y.reshape(bsz, length, D_INNER), z, ssd_norm_w)
    y_a = y @ w_ssd_out

    u = sc_c * sc_h
    u_c, sconv_new = causal_dwconv(u, sconv_prev, sconv_w)
    y_b = (sc_b * u_c * jax.nn.silu(sc_z)) @ w_sconv_out

    merged = jax.nn.sigmoid(g_a) * y_a + jax.nn.sigmoid(g_b) * y_b
    return x + merged @ w_o, conv_new, state, sconv_new


def setup_inputs(seed: int = 0) -> dict:
    key = jax.random.key(seed)
    ks = jax.random.split(key, 24)
    f32 = jnp.float32
    nrm = lambda k, shp, s: jax.random.normal(k, shp, f32) * s
    dt0 = jnp.exp(jax.random.uniform(ks[10], (DEPTH, SSD_HEADS), f32,
                                     np.log(1e-3).astype(np.float32), np.log(1e-1).astype(np.float32)))
    dt_bias = dt0 + jnp.log(-jnp.expm1(-dt0))
    return {
        "x_prompt": nrm(ks[0], (BATCH, SEQ, D_MODEL), 1.0),
        "x_sample": nrm(ks[1], (DEC_BATCH, DEC_SEQ, D_MODEL), 1.0),
        "state_ssd_conv": nrm(ks[2], (DEPTH, DEC_BATCH, SSD_CONV_W - 1, SSD_CONV_DIM), 1.0),
        "state_ssm": nrm(ks[3], (DEPTH, DEC_BATCH, SSD_HEADS, SSD_HEAD_DIM, SSD_STATE), 0.5),
        "state_sconv": nrm(ks[4], (DEPTH, DEC_BATCH, SC_CONV_W - 1, SC_WIDTH), 1.0),
        "meta_tokens": nrm(ks[5], (META_TOKENS, D_MODEL), 1.0),
        "norm_w": 1.0 + nrm(ks[6], (DEPTH, D_MODEL), 0.02),
        "w_in": nrm(ks[7], (DEPTH, D_MODEL, PROJ_COLS), D_MODEL ** -0.5),
        "ssd_conv_w": nrm(ks[8], (DEPTH, SSD_CONV_W, SSD_CONV_DIM), SSD_CONV_W ** -0.5),
        "ssd_conv_b": nrm(ks[9], (DEPTH, SSD_CONV_DIM), 0.02),
        "dt_bias": dt_bias,
        "a_log": jnp.log(jax.random.uniform(ks[11], (DEPTH, SSD_HEADS), f32, 1.0, 16.0)),
        "d_skip": 1.0 + nrm(ks[12], (DEPTH, SSD_HEADS), 0.1),
        "ssd_norm_w": 1.0 + nrm(ks[13], (DEPTH, D_INNER), 0.02),
        "w_ssd_out": nrm(ks[14], (DEPTH, D_INNER, D_MODEL), D_INNER ** -0.5),
        "sconv_w": nrm(ks[15], (DEPTH, SC_CONV_W, SC_WIDTH), SC_CONV_W ** -0.5),
        "w_sconv_out": nrm(ks[16], (DEPTH, SC_WIDTH, D_MODEL), SC_WIDTH ** -0.5),
        "w_o": nrm(ks[17], (DEPTH, D_MODEL, D_MODEL), D_MODEL ** -0.5),
        "final_norm_w": 1.0 + nrm(ks[18], (D_MODEL,), 0.02),
    }


def reference(x_prompt, x_sample, state_ssd_conv, state_ssm, state_sconv, meta_tokens, norm_w,
              w_in, ssd_conv_w, ssd_conv_b, dt_bias, a_log, d_skip, ssd_norm_w, w_ssd_out,
              sconv_w, w_sconv_out, w_o, final_norm_w):
    bp, seq = x_prompt.shape[0], x_prompt.shape[1]
    dec_seq = x_sample.shape[1]
    dt_ = x_prompt.dtype
    meta = jnp.broadcast_to(meta_tokens[None].astype(dt_), (bp, META_TOKENS, D_MODEL))
    xp = jnp.concatenate([meta, x_prompt], axis=1)
    xs = x_sample
    prompt_segments = ((META_TOKENS, META_TOKENS), (seq, min(SSD_CHUNK, seq)))
    sample_segments = ((dec_seq, dec_seq),)
    conv0 = jnp.zeros((bp, SSD_CONV_W - 1, SSD_CONV_DIM), dt_)
    ssm0 = jnp.zeros((bp, SSD_HEADS, SSD_HEAD_DIM, SSD_STATE), dt_)
    sconv0 = jnp.zeros((bp, SC_CONV_W - 1, SC_WIDTH), dt_)
    cp_l, sp_l, scp_l, cs_l, ss_l, scs_l = [], [], [], [], [], []
    for layer in range(DEPTH):
        lw = (norm_w[layer], w_in[layer], ssd_conv_w[layer], ssd_conv_b[layer], dt_bias[layer],
              a_log[layer], d_skip[layer], ssd_norm_w[layer], w_ssd_out[layer], sconv_w[layer],
              w_sconv_out[layer], w_o[layer])
        xp, cp, sp, scp = mixer_layer(xp, conv0, ssm0, sconv0, prompt_segments, *lw)
        xs, cs, ss, scs = mixer_layer(xs, state_ssd_conv[layer], state_ssm[layer],
                                      state_sconv[layer], sample_segments, *lw)
        cp_l.append(cp); sp_l.append(sp); scp_l.append(scp)
        cs_l.append(cs); ss_l.append(ss); scs_l.append(scs)
    y_prompt = rms_norm(xp, final_norm_w)[:, META_TOKENS:]
    y_sample = rms_norm(xs, final_norm_w)
    return (y_prompt, y_sample, jnp.stack(cp_l), jnp.stack(sp_l), jnp.stack(scp_l),
            jnp.stack(cs_l), jnp.stack(ss_l), jnp.stack(scs_l))
```

```python
import numpy as np
from contextlib import ExitStack
import concourse.bass as bass
import concourse.mybir as mybir
from concourse.bass_utils import run_bass_kernel_spmd

F32 = mybir.dt.float32
BF16 = mybir.dt.bfloat16
ALU = mybir.AluOpType
AF = mybir.ActivationFunctionType

D = 2048
DI = 4096
NH = 64
HP = 64
NS = 128
NG = 8
CONVD = 6144
PROJ = 22592
EPS = 1e-6
META = 16
SEQ = 2048
NCORES = 8
Q = 115
NCHW = 9
WIN = Q * NCHW
HALF = 1032
BLK = 3
NBLK = NCHW // BLK
TMAX = BLK * Q
NSEQ = 16
DSEQ = 8
C_Z = 0
C_X = DI
C_B = DI + DI
C_C = DI + DI + 1024
C_DT = DI + CONVD
C_SB = C_DT + NH
C_SC = C_SB + D
C_SH = C_SC + D
C_SZ = C_SH + D
C_GA = C_SZ + D
C_GB = C_GA + D
K_ID, K_TM, K_UM, K_TMS, K_UMS, K_ONE, K_EM, K_OM, K_BLK, K_ROLE = 0, 128, 256, 384, 512, 640, 768, 896, 1024, 1040
NCST = 1041
PV_CW, PV_CB, PV_SCW, PV_SNW, PV_NW = 0, 192, 240, 288, 320
NPV = 336
BV_DTB, BV_ALOG, BV_DSK, BV_FNW = 0, 64, 128, 192
NBV = 192 + D


class Res:
    __slots__ = ("w", "r")

    def __init__(self):
        self.w = None
        self.r = []


class TT:
    def __init__(self, t):
        self.t = t
        self.res = Res()

    def __getitem__(self, k):
        return self.t[k]


class _Rec:
    def __init__(self):
        self.call = None

    def __getattr__(self, name):
        def f(*a, **k):
            self.call = (name, a, k)
            return self
        return f


def _record(fn):
    r = _Rec()
    fn(r)
    name, a, k = r.call
    return lambda e: getattr(e, name)(*a, **k)


class Prog:
    ENG = ("pe", "act", "dve", "pool", "sp")

    def __init__(self, nc, stack, n_dma_sems=56):
        self.nc = nc
        self.dry = False
        self.sem = {e: stack.enter_context(nc.semaphore("c_" + e)) for e in self.ENG}
        self.dsem = [stack.enter_context(nc.semaphore("d%d" % i)) for i in range(n_dma_sems)]
        self.dpool = {"pool": list(range(0, 32)), "sp": list(range(32, n_dma_sems))}
        self.reset()

    def reset(self):
        self.q = {e: [] for e in self.ENG}
        self.cnt = {e: 0 for e in self.ENG}
        self.seen = {e: {} for e in self.ENG}
        self.dval = [0] * len(self.dsem)
        self.dnext = {"pool": 0, "sp": 0}

    def _deps(self, reads, writes):
        deps = []
        for r in reads:
            if r.res.w is not None:
                deps.append(r.res.w)
        for w in writes:
            if w.res.w is not None:
                deps.append(w.res.w)
            deps.extend(w.res.r)
        return deps

    def _waits(self, eng, deps, skip_self):
        need = {}
        seen = self.seen[eng]
        for (k, v) in deps:
            if skip_self and k == eng:
                continue
            if seen.get(k, 0) >= v:
                continue
            if need.get(k, 0) < v:
                need[k] = v
        for k, v in need.items():
            seen[k] = v
        return list(need.items())

    def _mark(self, tok, reads, writes):
        for r in reads:
            r.res.r.append(tok)
        for w in writes:
            w.res.w = tok
            w.res.r = []

    def op(self, eng, fn, r=(), w=()):
        if self.dry:
            return
        waits = self._waits(eng, self._deps(r, w), eng == "pe")
        self.cnt[eng] += 1
        tok = (eng, self.cnt[eng])
        self.q[eng].append((waits, _record(fn), (eng, 1)))
        self._mark(tok, r, w)

    def dma(self, eng, fn, r=(), w=()):
        if self.dry:
            return
        deps = self._deps(r, w)
        pool = self.dpool[eng]
        i = pool[self.dnext[eng] % len(pool)]
        self.dnext[eng] += 1
        if self.dval[i] > 0:
            deps.append((i, self.dval[i]))
        waits = self._waits(eng, deps, False)
        self.dval[i] += 16
        tok = (i, self.dval[i])
        self.q[eng].append((waits, _record(fn), (i, 16)))
        self._mark(tok, r, w)

    def finish(self, eng, tts):
        deps = []
        for t in tts:
            if t.res.w is not None:
                deps.append(t.res.w)
            deps.extend(t.res.r)
        self.q[eng].append((self._waits(eng, deps, False), None, None))

    def _semobj(self, k):
        return self.sem[k] if isinstance(k, str) else self.dsem[k]

    def emit(self):
        import bisect
        sig = {e: set() for e in self.ENG}
        for name in self.ENG:
            for waits, fn, inc in self.q[name]:
                for k, v in waits:
                    if isinstance(k, str):
                        sig[k].add(v)
        sigl = {e: sorted(sig[e]) for e in self.ENG}

        def remap(k, v):
            if isinstance(k, str):
                return bisect.bisect_right(sigl[k], v)
            return v

        def run(name):
            def body(e):
                n = 0
                for waits, fn, inc in self.q[name]:
                    for k, v in waits:
                        e.wait_ge(self._semobj(k), remap(k, v))
                    if fn is not None:
                        ins = fn(e)
                        if isinstance(inc[0], str):
                            n += 1
                            if n in sig[name]:
                                ins.then_inc(self._semobj(inc[0]), 1)
                        else:
                            ins.then_inc(self._semobj(inc[0]), inc[1])
            return body
        with self.nc.Block() as block:
            block.tensor(run("pe"))
            block.scalar(run("act"))
            block.vector(run("dve"))
            block.gpsimd(run("pool"))
            block.sync(run("sp"))


class WStream:
    def __init__(self, P, slots, depth, nc):
        self.P = P
        self.slots = slots
        self.depth = depth
        self.nc = nc
        self.specs = []
        self.i = 0
        self.issued = 0

    def start(self):
        self.i = 0
        self.issued = 0
        keys = {}
        self.kidx = []
        self.firstuse = []
        for (src, k, n) in self.specs:
            key = (src.name, str(src.offset), k, n)
            self.firstuse.append(key not in keys)
            if key not in keys:
                keys[key] = len(keys)
            self.kidx.append(keys[key])
        self.scr = self.nc.dram_tensor("wscr", [len(keys), 128, 4096], BF16).ap()
        self.scr_res = [TT(None) for _ in keys]
        self.conv_ptr = 0

    def _view(self, slot, kcs, ncols):
        return slot.t[:, 0:kcs * ncols].rearrange("p (k n) -> p k n", n=ncols)

    def next(self, src, kcs, ncols):
        if self.P.dry:
            self.specs.append((src, kcs, ncols))
            return self.slots[0], self._view(self.slots[0], kcs, ncols)
        i = self.i
        assert self.specs[i][1:] == (kcs, ncols)
        while self.issued < len(self.specs) and self.issued <= i + self.depth:
            j = self.issued
            s, k, n = self.specs[j]
            sl = self.slots[j % len(self.slots)]
            ki = self.kidx[j]
            flat = sl.t[:, 0:k * n]
            if self.firstuse[j]:
                v = self._view(sl, k, n)
                self.P.dma("pool", lambda e: e.dma_start(out=v, in_=s), w=[sl])
                self.P.dma("sp", lambda e: e.dma_start(out=self.scr[ki, :, 0:k * n], in_=flat), r=[sl], w=[self.scr_res[ki]])
            else:
                self.P.dma("sp", lambda e: e.dma_start(out=flat, in_=self.scr[ki, :, 0:k * n]), r=[self.scr_res[ki]], w=[sl])
            self.issued += 1
        jc = max(self.conv_ptr, self.issued + 6)
        while jc < len(self.specs) and not self.firstuse[jc]:
            jc += 1
        if jc < len(self.specs):
            s2, k2, n2 = self.specs[jc]
            ki2 = self.kidx[jc]
            dst = self.scr[ki2, :, 0:k2 * n2].rearrange("p (k n) -> p k n", n=n2)
            self.P.dma("pool", lambda e: e.dma_start(out=dst, in_=s2), w=[self.scr_res[ki2]])
            self.firstuse[jc] = False
            self.conv_ptr = jc + 1
        self.i += 1
        sl = self.slots[i % len(self.slots)]
        return sl, self._view(sl, kcs, ncols)


def bcl(ap2, n):
    return ap2.unsqueeze(2).broadcast_to([ap2.shape[0], ap2.shape[1], n])


def bcm(ap2, n):
    return ap2.unsqueeze(1).broadcast_to([ap2.shape[0], n, ap2.shape[1]])


def build_program():
    nc = bass.Bass("TRN2", target_bir_lowering=False)
    dt_in = lambda n, s: nc.dram_tensor(n, s, F32, kind="ExternalInput").ap()
    dt_out = lambda n, s: nc.dram_tensor(n, s, F32, kind="ExternalOutput").ap()
    xw_main = dt_in("xw_main", [WIN, D])
    xw_warm = dt_in("xw_warm", [WIN, D])
    xsmp = dt_in("xsmp", [128, D])
    consts = dt_in("consts", [128, NCST])
    pvec = dt_in("pvec", [128, NPV])
    bvec = dt_in("bvec", [1, NBV])
    cst_conv = dt_in("cst_conv", [NSEQ * 3, CONVD])
    cst_ssm = dt_in("cst_ssm", [NSEQ, NH, HP, NS])
    cst_sconv = dt_in("cst_sconv", [NSEQ * 2, D])
    w_in = dt_in("w_in", [D, PROJ])
    w_ssd_out = dt_in("w_ssd_out", [DI, D])
    w_sconv_out = dt_in("w_sconv_out", [D, D])
    w_o = dt_in("w_o", [D, D])
    o_yp = dt_out("o_yp", [WIN, D])
    o_ys = dt_out("o_ys", [128, D])
    o_convp = dt_out("o_convp", [Q, CONVD])
    o_ssmTp = dt_out("o_ssmTp", [128, DI])
    o_sconvp = dt_out("o_sconvp", [Q, D])
    o_convs = dt_out("o_convs", [128, CONVD])
    o_ssms = dt_out("o_ssms", [NSEQ, NH, HP, NS])
    o_sconvs = dt_out("o_sconvs", [128, D])

    w_in_v = w_in.rearrange("(kc p) n -> p kc n", p=128)
    w_so_v = w_ssd_out.rearrange("(kc p) n -> p kc n", p=128)
    w_sc_v = w_sconv_out.rearrange("(kc p) n -> p kc n", p=128)
    w_o_v = w_o.rearrange("(kc p) n -> p kc n", p=128)

    with ExitStack() as st:
        P = Prog(nc, st)
        sb = lambda n, s, d: TT(st.enter_context(nc.sbuf_tensor(n, s, d)))
        psb = lambda n: TT(st.enter_context(nc.psum_tensor(n, [128, 512], F32)))
        cst = sb("cst", [128, NCST], F32)
        identb = sb("identb", [128, 128], BF16)
        pv = sb("pv", [128, NPV], F32)
        pvn = sb("pvn", [128, 48], F32)
        bv = sb("bv", [128, NBV], F32)
        negA = sb("negA", [128, NH], F32)
        hT = sb("hT", [128, 16, TMAX], BF16)
        ynT = sb("ynT", [128, 32, TMAX], BF16)
        mrg = sb("mrg", [128, 16, TMAX], BF16)
        xio = [sb("xio%d" % i, [128, D], F32) for i in range(3)]
        hb = sb("hb", [128, D], BF16)
        x1 = xio[1].t
        segr1 = TT(x1[:, 0:1024].rearrange("p (a b) -> p a b", b=128))
        wT1 = TT(x1[:, 1024:1536].bitcast(BF16).rearrange("p (a b) -> p a b", b=128))
        xdt1 = TT(x1[:, 1536:1792].bitcast(BF16))
        xs1 = TT(x1[:, 1792:2048].bitcast(BF16))
        xio1_alias = [segr1, wT1, xdt1, xs1]
        x2 = xio[2].t
        sst_extra = [TT(x2[:, i * 512:(i + 1) * 512].rearrange("p (a n) -> p a n", n=128)) for i in range(4)]
        stt = [sb("stt%d" % i, [128, 4], F32) for i in range(4)]
        wsl = [sb("wsl%d" % i, [128, 4096], BF16) for i in range(3)]
        dtt = sb("dtt", [128, BLK, NH], F32)
        dtmp = sb("dtmp", [128, NH], F32)
        la = sb("la", [128, BLK, NH], F32)
        toend = sb("toend", [128, BLK, NH], F32)
        eacum = sb("eacum", [128, BLK, NH], F32)
        cdb = sb("cdb", [128, BLK, NH], F32)
        raw = sb("raw", [128, 6, 3 + TMAX], BF16)
        acc = [sb("acc%d" % i, [128, TMAX], F32) for i in range(2)]
        xbcTs = [sb("xbcT%d" % i, [128, 6, TMAX], BF16) for i in range(2)]
        xbtoks = [sb("xbtok%d" % i, [128, 640], BF16) for i in range(3)]
        szS = [sb("sz%d" % i, [128, BLK, 512], BF16) for i in range(2)]
        carx = sb("carx", [128, 48, 3], BF16)
        caru = sb("caru", [128, 16, 2], F32)
        cbms = [sb("cbm%d" % i, [128, 128], BF16) for i in range(2)]
        segr0 = sb("segr0", [128, 8, 128], F32)
        dec = sb("dec", [128, 8, 128], BF16)
        wT0 = sb("wT0", [128, 8, 128], BF16)
        xdt0 = sb("xdt0", [128, 512], BF16)
        xs0 = sb("xs0", [128, 512], BF16)
        t1 = sb("t1", [128, 512], F32)
        t3 = sb("t3", [128, 512], F32)
        ydg = t3
        gns = [sb("gn%d" % i, [128, 512], BF16) for i in range(2)]
        segrs = [segr0, segr1]
        wTs = [wT0, wT1]
        xdts = [xdt0, xdt1]
        xss = [xs0, xs1]
        big = sb("big", [128, 4096], F32)
        Sst = [TT(big.t[:, g * 512:(g + 1) * 512]) for g in range(NG)]
        Sbf = sb("Sbf", [128, 512], BF16)
        sgts = [sb("sgt%d" % i, [128, TMAX], F32) for i in range(2)]
        tcc = [sb("tcc%d" % i, [128, TMAX], F32) for i in range(2)]
        szz = [sb("szz%d" % i, [128, TMAX], F32) for i in range(2)]
        rawu = [sb("rawu%d" % i, [128, 2 + TMAX], F32) for i in range(2)]
        crow = [sb("crow%d" % i, [128, 256], F32) for i in range(2)]
        tcr = sb("tcr", [128, 256], F32)
        cvst = sb("cvst", [48, 768], BF16)
        scst = sb("scst", [32, D], BF16)
        CTm = TT(big.t[:, 0:1024].bitcast(BF16).rearrange("p (s n) -> p s n", n=128))
        Btm = TT(big.t[:, 1024:2048].bitcast(BF16).rearrange("p (s n) -> p s n", n=128))
        Rv = TT(big.t[:, 2048:3072].rearrange("p (a n) -> p a n", n=512))
        cds = TT(big.t[:, 3072:3584].rearrange("p (s h) -> p s h", h=32))
        smp_alias = [CTm, Btm, Rv, cds]
        sst = [sb("sst%d" % i, [128, 4, 128], F32) for i in range(4)] + sst_extra
        sstb = [TT(big.t[:, 3584 + 256 * i:3584 + 256 * (i + 1)].bitcast(BF16).rearrange("p (a n) -> p a n", n=128)) for i in range(2)]
        smp_alias += sstb
        stb = [sb("stb%d" % i, [128, 512], BF16) for i in range(2)]
        A = [psb("pA0"), psb("pA1")]
        Sg = [psb("pS0"), psb("pS1")]
        Yb = psb("pY")
        YOb = psb("pYO")
        STb = psb("pST")
        Mb = psb("pM")
        tp_ap = Mb.t[:, 192:512].bitcast(BF16).rearrange("p (a b) -> p a b", b=128)
        Cb = Mb

        W = WStream(P, wsl, 2, nc)

        def op(eng, fn, r=(), w=()):
            P.op(eng, fn, r, w)

        def cK(k, n=128, rows=128):
            return cst[0:rows, k:k + n]

        def setup():
            P.dma("pool", lambda e: e.dma_start(out=cst[:], in_=consts), w=[cst])
            P.dma("pool", lambda e: e.dma_start(out=pv[:], in_=pvec), w=[pv])
            P.dma("pool", lambda e: e.dma_start(out=bv[:], in_=bvec.partition_broadcast(128)), w=[bv])
            op("dve", lambda e: e.tensor_copy(out=identb[:], in_=cst[:, K_ID:K_ID + 128]), r=[cst], w=[identb])
            op("dve", lambda e: e.tensor_scalar(out=pvn[:], in0=pv[:, PV_CB:PV_CB + 48], scalar1=-1.0, scalar2=None, op0=ALU.mult),
               r=[pv], w=[pvn])
            op("act", lambda e: e.activation(out=negA[:], in_=bv[:, BV_ALOG:BV_ALOG + NH], func=AF.Exp), r=[bv], w=[negA])
            op("dve", lambda e: e.tensor_scalar(out=negA[:], in0=negA[:], scalar1=-1.0, scalar2=None, op0=ALU.mult),
               r=[negA], w=[negA])

        def rms_rstd(src_tt, rows, sti, n_inv_sqrt, junk):
            s = stt[sti]
            op("act", lambda e: e.activation(out=junk[0:rows, :], in_=src_tt[0:rows, :], func=AF.Square,
                                             scale=float(n_inv_sqrt), accum_out=s[0:rows, 0:1]),
               r=[src_tt], w=[junk, s])
            op("act", lambda e: e.activation(out=s[0:rows, 1:2], in_=s[0:rows, 0:1], func=AF.Ln, bias=EPS, scale=1.0),
               r=[s], w=[s])
            op("act", lambda e: e.activation(out=s[0:rows, 2:3], in_=s[0:rows, 1:2], func=AF.Exp, scale=-0.5), r=[s], w=[s])
            return s

        def sig_chain(dst_tt, dst_ap, src_tt, src_ap, negb=None):
            if negb is None:
                op("act", lambda e: e.activation(out=dst_ap, in_=src_ap, func=AF.Exp, scale=-1.0), r=[src_tt], w=[dst_tt])
            else:
                op("act", lambda e: e.activation(out=dst_ap, in_=src_ap, func=AF.Exp, scale=-1.0, bias=negb), r=[src_tt, pvn], w=[dst_tt])
            op("act", lambda e: e.activation(out=dst_ap, in_=dst_ap, func=AF.Ln, bias=1.0, scale=1.0), r=[dst_tt], w=[dst_tt])
            op("act", lambda e: e.activation(out=dst_ap, in_=dst_ap, func=AF.Exp, scale=-1.0), r=[dst_tt], w=[dst_tt])

        def block(xsrc, nch, q, kind, first, last, ysink):
            T = nch * q
            smp = kind == "sample"
            warm = kind == "warm"
            nseq, L = (NSEQ, DSEQ) if smp else (1, T)
            kTM, kUM = (K_TMS, K_UMS) if smp else (K_TM, K_UM)
            emit_rows = last or smp
            rawv = raw.t[:, :, 0:nseq * (3 + L)].rearrange("p j (s l) -> p j s l", l=3 + L)
            rawuvs = [ru.t[:, 0:nseq * (2 + L)].rearrange("p (s l) -> p s l", l=2 + L) for ru in rawu]

            def v3(ap2):
                return ap2.rearrange("p (s l) -> p s l", l=L)

            for c in range(nch):
                xt = xio[c]
                P.dma("pool", lambda e, xt=xt, c=c: e.dma_start(out=xt[0:q, :], in_=xsrc[c * q:(c + 1) * q, :]), w=[xt])
                s = rms_rstd(xt, q, c, D ** -0.5, hb)
                op("dve", lambda e, xt=xt, s=s: e.tensor_scalar(out=hb[0:q, :], in0=xt[0:q, :], scalar1=s[0:q, 2:3],
                                                                scalar2=None, op0=ALU.mult), r=[xt, s], w=[hb])
                for g4 in range(4):
                    for j in range(4):
                        kc = g4 * 4 + j
                        op("pe", lambda e, kc=kc, j=j: e.transpose(out=tp_ap[:, j, 0:q], in_=hb[0:q, kc * 128:(kc + 1) * 128],
                                                                   identity=identb[0:q, 0:q]), r=[hb, identb], w=[Mb])
                    op("dve", lambda e, g4=g4, c=c: e.tensor_tensor(
                        out=hT[:, g4 * 4:(g4 + 1) * 4, c * q:(c + 1) * q], in0=tp_ap[:, 0:4, 0:q],
                        in1=bcl(pv[:, PV_NW + g4 * 4:PV_NW + (g4 + 1) * 4], q), op=ALU.mult), r=[Mb, pv], w=[hT])
            sl, wv = W.next(w_in_v[:, :, C_DT:C_DT + NH], 16, NH)
            for c in range(nch):
                for kc in range(16):
                    op("pe", lambda e, kc=kc, c=c, wv=wv: e.matmul(Mb[0:q, 128:192], lhsT=hT[:, kc, c * q:(c + 1) * q],
                                                                   rhs=wv[:, kc, :], start=(kc == 0), stop=(kc == 15)),
                       r=[hT, sl], w=[Mb])
                op("dve", lambda e: e.tensor_tensor(out=dtmp[0:q, :], in0=Mb[0:q, 128:192], in1=bv[0:q, BV_DTB:BV_DTB + NH],
                                                    op=ALU.add), r=[Mb, bv], w=[dtmp])
                op("act", lambda e: e.activation(out=dtmp[0:q, :], in_=dtmp[0:q, :], func=AF.Exp), r=[dtmp], w=[dtmp])
                op("act", lambda e, c=c: e.activation(out=dtt[0:q, c, :], in_=dtmp[0:q, :], func=AF.Ln, bias=1.0, scale=1.0),
                   r=[dtmp], w=[dtt])
                if first and c == 0:
                    op("dve", lambda e: e.memset(dtt[0:3, 0, :], 0.0), w=[dtt])
                op("dve", lambda e, c=c: e.tensor_tensor(out=la[0:q, c, :], in0=dtt[0:q, c, :], in1=negA[0:q, :], op=ALU.mult),
                   r=[dtt, negA], w=[la])
                pa = A[c % 2]
                op("pe", lambda e, c=c, pa=pa: e.matmul(pa[0:q, 0:64], lhsT=cK(kUM, q, q), rhs=la[0:q, c, :], start=True, stop=True),
                   r=[cst, la], w=[pa])
                op("pe", lambda e, c=c, pa=pa: e.matmul(pa[0:q, 64:128], lhsT=cK(kTM, q, q), rhs=la[0:q, c, :], start=True, stop=True),
                   r=[cst, la], w=[pa])
                op("pe", lambda e, c=c, pa=pa: e.matmul(pa[:, 128:192], lhsT=cK(K_ONE, 128, q), rhs=la[0:q, c, :], start=True, stop=True),
                   r=[cst, la], w=[pa])
                op("act", lambda e, c=c, pa=pa: e.activation(out=toend[0:q, c, :], in_=pa[0:q, 0:64], func=AF.Exp), r=[pa], w=[toend])
                op("act", lambda e, c=c, pa=pa: e.activation(out=eacum[0:q, c, :], in_=pa[0:q, 64:128], func=AF.Exp), r=[pa], w=[eacum])
                op("act", lambda e, c=c, pa=pa: e.activation(out=cdb[:, c, :], in_=pa[:, 128:192], func=AF.Exp), r=[pa], w=[cdb])

            if smp:
                for al in smp_alias:
                    for S_ in Sst:
                        if S_.res.w is not None:
                            al.res.r.append(S_.res.w)
                        al.res.r.extend(S_.res.r)
                op("dve", lambda e: e.memset(CTm[:], 0.0), w=[CTm])
                P.dma("pool", lambda e: e.dma_start(out=scst[:], in_=cst_sconv), w=[scst])
                lav = la.t[:, 0, :].rearrange("p (hh two) -> p hh two", two=2)
                for par in range(2):
                    op("dve", lambda e, par=par: e.tensor_tensor(
                        out=Rv.t[:, par, :].rearrange("p (s h) -> p s h", h=32), in0=bcl(cst[:, K_BLK:K_BLK + 16], 32),
                        in1=bcm(lav[:, :, par], 16), op=ALU.mult), r=[cst, la], w=[Rv])
                op("pe", lambda e: e.matmul(A[0][:, 0:512], lhsT=cK(K_EM), rhs=Rv[:, 0, :], start=True, stop=False), r=[cst, Rv], w=[A[0]])
                op("pe", lambda e: e.matmul(A[0][:, 0:512], lhsT=cK(K_OM), rhs=Rv[:, 1, :], start=False, stop=True), r=[cst, Rv], w=[A[0]])
                op("act", lambda e: e.activation(out=cds.t[:].rearrange("p s h -> p (s h)"), in_=A[0][:, 0:512], func=AF.Exp),
                   r=[A[0]], w=[cds])

            bank_i = [0]
            banks = [A[0], A[1]]
            slabref = [None]
            raws = [TT(raw.t[:, jj, :]) for jj in range(6)]

            def nbank():
                pa = banks[bank_i[0] % len(banks)]
                bank_i[0] += 1
                return pa

            def inproj_fm(wv, sl, j0, ncols_used=128):
                pa = nbank()
                for kc in range(16):
                    op("pe", lambda e, kc=kc, pa=pa: e.matmul(pa[:, 0:T], lhsT=wv[:, kc, j0:j0 + 128], rhs=hT[:, kc, 0:T],
                                                              start=(kc == 0), stop=(kc == 15)), r=[sl, hT], w=[pa])
                return pa

            def rows_tm(wv, sl, ncols, colbase, out_dram):
                pa = nbank()
                c = nch - 1
                for kc in range(16):
                    op("pe", lambda e, kc=kc, pa=pa: e.matmul(pa[0:q, 0:ncols], lhsT=hT[:, kc, c * q:(c + 1) * q], rhs=wv[:, kc, 0:ncols],
                                                              start=(kc == 0), stop=(kc == 15)), r=[sl, hT], w=[pa])
                cr = crow[bank_i[0] % 2]
                op("act", lambda e, pa=pa, cr=cr: e.activation(out=cr[0:q, 0:ncols], in_=pa[0:q, 0:ncols], func=AF.Copy), r=[pa], w=[cr])
                P.dma("pool", lambda e, cr=cr: e.dma_start(out=out_dram[0:q, colbase:colbase + ncols], in_=cr[0:q, 0:ncols]), r=[cr], w=[cr])

            def stage1(g, si):
                xb = xbcTs[si]
                szs = szS[si]
                pend = [None, None, None]

                def step(job):
                    if pend[2] is not None and pend[2][3] is not None:
                        pend[2][3]()
                    if pend[0] is not None:
                        pend[0][1]()
                    if pend[1] is not None and pend[1][2] is not None:
                        pend[1][2]()
                    if job is not None:
                        job[0]()
                    pend[2], pend[1], pend[0] = pend[1], pend[0], job

                if smp:
                    P.dma("pool", lambda e: e.dma_start(out=cvst[:, 0:512], in_=cst_conv[:, g * 512:(g + 1) * 512]), w=[cvst])
                    P.dma("pool", lambda e: e.dma_start(out=cvst[:, 512:640], in_=cst_conv[:, DI + g * 128:DI + (g + 1) * 128]), w=[cvst])
                    P.dma("pool", lambda e: e.dma_start(out=cvst[:, 640:768], in_=cst_conv[:, DI + 1024 + g * 128:DI + 1024 + (g + 1) * 128]), w=[cvst])
                specs = [(C_X + g * 512, 256, [0, 1]), (C_X + g * 512 + 256, 256, [2, 3]), (C_B + g * 128, 128, [4])]
                if not warm:
                    specs.append((C_C + g * 128, 128, [5]))
                njob = [0]
                for (col0, ncols, jjs) in specs:
                    for jn, jj in enumerate(jjs):
                        cc = (col0 - C_X) // 128 + jn
                        n = njob[0]
                        njob[0] += 1
                        st_ = {}
                        rw = raws[jj]
                        rv = rawv[:, jj]
                        ac = acc[n % 2]
                        acv = v3(ac[:, 0:T])
                        sg_ = sgts[n % 2]

                        def pe(col0=col0, ncols=ncols, jn=jn, jj=jj, st_=st_, last_of_slab=(jn == len(jjs) - 1)):
                            if jn == 0:
                                st_["slab"] = W.next(w_in_v[:, :, col0:col0 + ncols], 16, ncols)
                                slabref[0] = st_["slab"]
                            sl, wv = slabref[0]
                            st_["pa"] = inproj_fm(wv, sl, jn * 128)
                            if smp:
                                op("pe", lambda e: e.transpose(out=tp_ap[:, 0, 0:48], in_=cvst[0:48, jj * 128:(jj + 1) * 128],
                                                               identity=identb[0:48, 0:48]), r=[cvst, identb], w=[Mb])
                                op("act", lambda e: e.activation(out=rv[:, :, 0:3], in_=tp_ap[:, 0, 0:48].rearrange("p (s k) -> p s k", k=3),
                                                                 func=AF.Copy), r=[Mb], w=[rw])
                            if last_of_slab and emit_rows and not warm:
                                rows_tm(wv, sl, ncols, col0 - C_X, o_convs if smp else o_convp)

                        def pa_(jj=jj, cc=cc, st_=st_, rw=rw, rv=rv, ac=ac, acv=acv):
                            pa = st_["pa"]
                            if smp:
                                pass
                            elif first:
                                op("dve", lambda e: e.memset(rv[:, :, 0:3], 0.0), w=[rw])
                            else:
                                op("act", lambda e: e.activation(out=rv[:, 0, 0:3], in_=carx[:, cc, :], func=AF.Copy), r=[carx], w=[rw])
                            op("act", lambda e: e.activation(out=rv[:, :, 3:3 + L], in_=v3(pa[:, 0:T]), func=AF.Copy), r=[pa], w=[rw])
                            if not smp:
                                op("act", lambda e: e.activation(out=carx[:, cc, :], in_=rv[:, 0, L:L + 3], func=AF.Copy), r=[rw], w=[carx])
                            op("act", lambda e: e.activation(out=acv, in_=rv[:, :, 0:L], func=AF.Copy, scale=pv[:, PV_CW + cc * 4:PV_CW + cc * 4 + 1]),
                               r=[rw, pv], w=[ac])
                            for k in range(1, 4):
                                op("dve", lambda e, k=k: e.scalar_tensor_tensor(
                                    out=acv, in0=rv[:, :, k:k + L], scalar=pv[:, PV_CW + cc * 4 + k:PV_CW + cc * 4 + k + 1], in1=acv,
                                    op0=ALU.mult, op1=ALU.add), r=[rw, pv, ac], w=[ac])

                        def pb_(cc=cc, ac=ac, sg_=sg_):
                            sig_chain(sg_, sg_[:, 0:T], ac, ac[:, 0:T], negb=pvn[:, cc:cc + 1])

                        def pc_(jj=jj, cc=cc, ac=ac, sg_=sg_):
                            op("dve", lambda e: e.scalar_tensor_tensor(out=xb[:, jj, 0:T], in0=ac[:, 0:T], scalar=pv[:, PV_CB + cc:PV_CB + cc + 1],
                                                                       in1=sg_[:, 0:T], op0=ALU.add, op1=ALU.mult), r=[ac, pv, sg_], w=[xb])

                        step((pe, pa_, pb_, pc_))
                        yield
                if not warm:
                    for zs in range(2):
                        for c in range(nch):
                            n = njob[0]
                            njob[0] += 1
                            st_ = {}
                            sg_ = sgts[n % 2]

                            def pe(zs=zs, c=c, st_=st_):
                                if c == 0:
                                    slabref[0] = W.next(w_in_v[:, :, C_Z + g * 512 + zs * 256:C_Z + g * 512 + (zs + 1) * 256], 16, 256)
                                sl, wv = slabref[0]
                                pa = nbank()
                                st_["pa"] = pa
                                for kc in range(16):
                                    op("pe", lambda e, kc=kc: e.matmul(pa[0:q, 0:256], lhsT=hT[:, kc, c * q:(c + 1) * q], rhs=wv[:, kc, :],
                                                                       start=(kc == 0), stop=(kc == 15)), r=[sl, hT], w=[pa])

                            def pa_(st_=st_, sg_=sg_):
                                pa = st_["pa"]
                                sig_chain(sg_, sg_[0:q, 0:256], pa, pa[0:q, 0:256])

                            def pb_(zs=zs, c=c, st_=st_, sg_=sg_):
                                pa = st_["pa"]
                                op("dve", lambda e: e.tensor_tensor(out=szs[0:q, c, zs * 256:(zs + 1) * 256], in0=sg_[0:q, 0:256], in1=pa[0:q, 0:256],
                                                                    op=ALU.mult), r=[sg_, pa], w=[szs])

                            step((pe, pa_, pb_, None))
                            yield
                for _ in range(3):
                    step(None)
                    yield

            def pieceA(g, c):
                si = g % 2
                xb = xbcTs[si]
                k = g * nch + c
                xbt, xd, xs2, cb2, sr2 = xbtoks[k % 3], xdts[k % 2], xss[k % 2], cbms[k % 2], segrs[k % 2]
                tk = slice(c * q, (c + 1) * q)
                hs = slice(g * 8, g * 8 + 8)
                S = Sst[g]
                if c == 0 and not smp:
                    if first and warm:
                        op("dve", lambda e: e.memset(S[:], 0.0), w=[S])
                    if first and kind == "main":
                        op("dve", lambda e: e.tensor_scalar(out=S[:], in0=S[:], scalar1=cst[:, K_ROLE:K_ROLE + 1], scalar2=None,
                                                            op0=ALU.mult), r=[S, cst], w=[S])
                for j in range(5):
                    op("pe", lambda e, j=j: e.transpose(out=tp_ap[0:q, j, :], in_=xb[:, j, tk], identity=identb[:]),
                       r=[xb, identb], w=[Mb])
                op("act", lambda e: e.activation(out=xbt[0:q, :].rearrange("p (a b) -> p a b", b=128), in_=tp_ap[0:q, 0:5, :], func=AF.Copy),
                   r=[Mb], w=[xbt])
                xv = xbt[0:q, 0:512].rearrange("p (r d) -> p r d", d=HP)
                op("dve", lambda e: e.tensor_tensor(out=xd[0:q, :].rearrange("p (r d) -> p r d", d=HP), in0=xv,
                                                    in1=bcl(dtt[0:q, c, hs], HP), op=ALU.mult), r=[xbt, dtt], w=[xd])
                op("dve", lambda e: e.tensor_tensor(out=xs2[0:q, :].rearrange("p (r d) -> p r d", d=HP),
                                                    in0=xd[0:q, :].rearrange("p (r d) -> p r d", d=HP),
                                                    in1=bcl(toend[0:q, c, hs], HP), op=ALU.mult), r=[xd, toend], w=[xs2])
                if not warm:
                    op("pe", lambda e: e.matmul(Cb[0:q, 0:q], lhsT=xb[:, 4, tk], rhs=xb[:, 5, tk], start=True, stop=True), r=[xb], w=[Cb])
                    op("dve", lambda e: e.tensor_tensor(out=cb2[0:q, 0:q], in0=Cb[0:q, 0:q], in1=cK(kTM, q, q), op=ALU.mult),
                       r=[Cb, cst], w=[cb2])
                    op("dve", lambda e: e.tensor_tensor(out=sr2[0:q, :, 0:q], in0=bcm(cK(kTM, q, q), 8),
                                                        in1=bcl(la[0:q, c, hs], q), op=ALU.mult), r=[cst, la], w=[sr2])

            def pieceB(g, c):
                if warm:
                    return
                k = g * nch + c
                cb2, sr2, w2 = cbms[k % 2], segrs[k % 2], wTs[k % 2]
                for r in range(8):
                    sgb = Sg[r // 4]
                    op("pe", lambda e, r=r, sgb=sgb: e.matmul(sgb[0:q, (r % 4) * 128:(r % 4) * 128 + q], lhsT=cK(kUM, q, q),
                                                              rhs=sr2[0:q, r, 0:q], start=True, stop=True), r=[cst, sr2], w=[sgb])
                for h2 in range(2):
                    sgb = Sg[h2]
                    op("act", lambda e, h2=h2, sgb=sgb: e.activation(
                        out=dec[0:q, h2 * 4:(h2 + 1) * 4, 0:q], in_=sgb[0:q, :].rearrange("p (a b) -> p a b", b=128)[:, :, 0:q],
                        func=AF.Exp), r=[sgb], w=[dec])
                op("dve", lambda e: e.tensor_tensor(out=w2[0:q, :, 0:q], in0=dec[0:q, :, 0:q], in1=bcm(cb2[0:q, 0:q], 8), op=ALU.mult),
                   r=[dec, cb2], w=[w2])

            def pieceC(g, c):
                si = g % 2
                xb = xbcTs[si]
                szs = szS[si]
                k = g * nch + c
                xbt, xd, xs2, w2, gn2 = xbtoks[k % 3], xdts[k % 2], xss[k % 2], wTs[k % 2], gns[k % 2]
                tk = slice(c * q, (c + 1) * q)
                hs = slice(g * 8, g * 8 + 8)
                S = Sst[g]
                xv = xbt[0:q, 0:512].rearrange("p (r d) -> p r d", d=HP)
                if not warm:
                    for r in range(8):
                        op("pe", lambda e, r=r: e.matmul(Yb[0:q, r * HP:(r + 1) * HP], lhsT=w2[0:q, r, 0:q], rhs=xd[0:q, r * HP:(r + 1) * HP],
                                                         start=True, stop=True), r=[w2, xd], w=[Yb])
                    if not smp:
                        if c == 0 and kind == "main":
                            op("act", lambda e: e.activation(out=Sbf[:], in_=S[:], func=AF.Copy), r=[S], w=[Sbf])
                        op("pe", lambda e: e.matmul(YOb[0:q, :], lhsT=xb[:, 5, tk], rhs=Sbf[:], start=True, stop=True),
                           r=[xb, Sbf], w=[YOb])
                if not smp:
                    op("pe", lambda e: e.matmul(STb[:, :], lhsT=xbt[0:q, 512:640], rhs=xs2[0:q, :], start=True, stop=True),
                       r=[xbt, xs2], w=[STb])
                    Sv = S[:].rearrange("p (r d) -> p r d", d=HP)
                    op("dve", lambda e: e.tensor_tensor(out=Sv, in0=Sv, in1=bcl(cdb[:, c, hs], HP), op=ALU.mult), r=[S, cdb], w=[S])
                    op("dve", lambda e: e.tensor_tensor(out=S[:], in0=S[:], in1=STb[:, :], op=ALU.add), r=[S, STb], w=[S])
                    need_sbf = (not warm) and c + 1 < nch
                else:
                    op("act", lambda e: e.activation(out=ydg[:, :], in_=Yb[:, :], func=AF.Copy), r=[Yb], w=[ydg])
                    ctsrc = xb[:, 5, 0:128].rearrange("p (s t) -> p s t", t=DSEQ)
                    base = CTm.t[:, 0, 0:DSEQ]
                    ctdst = bass.AP(base.tensor, base.offset, [list(base.ap[0]), [128 + DSEQ, NSEQ], [1, DSEQ]])
                    op("dve", lambda e: e.tensor_copy(out=ctdst, in_=ctsrc), r=[xb], w=[CTm])
                    op("dve", lambda e: e.tensor_tensor(out=Btm[:], in0=bcm(xbt[:, 512:640], NSEQ), in1=bcl(cst[:, K_BLK:K_BLK + 16], 128),
                                                        op=ALU.mult), r=[xbt, cst], w=[Btm])

                    def sD(s):
                        it = g * NSEQ + s
                        so = sst[it % 8]
                        srcv = cst_ssm[s, g * 8:(g + 1) * 8].rearrange("(q two) p n -> (two p) q n", two=2)
                        P.dma("sp", lambda e: e.dma_start(out=so[:], in_=srcv), w=[so])

                    def sA(s):
                        it = g * NSEQ + s
                        so, sbb, stt_, tpb = sst[it % 8], sstb[it % 2], stb[it % 2], Sg[it % 2]
                        tpv = tpb.t[:, 0:256].bitcast(BF16).rearrange("p (a b) -> p a b", b=128)
                        op("act", lambda e: e.activation(out=sbb[:], in_=so[:], func=AF.Copy), r=[so], w=[sbb])
                        for qq in range(4):
                            op("pe", lambda e, qq=qq: e.transpose(out=tpv[:, qq, :], in_=sbb[:, qq, :], identity=identb[:]),
                               r=[sbb, identb], w=[tpb])
                        op("act", lambda e: e.activation(out=stt_[:].rearrange("p (a b) -> p a b", b=128), in_=tpv[:, 0:4, :], func=AF.Copy),
                           r=[tpb], w=[stt_])

                    def sB(s):
                        it = g * NSEQ + s
                        so, stt_ = sst[it % 8], stb[it % 2]
                        stp = (STb, Yb)[it % 2]
                        dstv = o_ssms[s, g * 8:(g + 1) * 8].rearrange("(q two) p n -> (two p) q n", two=2)
                        op("pe", lambda e: e.matmul(YOb[:, :], lhsT=CTm[:, s, :], rhs=stt_[:], start=(s == 0), stop=(s == NSEQ - 1)),
                           r=[CTm, stt_], w=[YOb])
                        for qq in range(4):
                            op("pe", lambda e, qq=qq: e.matmul(stp[:, qq * 128:(qq + 1) * 128], lhsT=xs2[:, qq * 128:(qq + 1) * 128],
                                                               rhs=Btm[:, s, :], start=True, stop=True), r=[xs2, Btm], w=[stp])
                        op("dve", lambda e: e.tensor_tensor(out=so[:], in0=so[:], in1=bcl(cds[:, s, g * 4:(g + 1) * 4], 128),
                                                            op=ALU.mult), r=[so, cds], w=[so])
                        op("dve", lambda e: e.tensor_tensor(out=so[:], in0=so[:], in1=stp[:, :].rearrange("p (a b) -> p a b", b=128),
                                                            op=ALU.add), r=[so, stp], w=[so])
                        P.dma("sp", lambda e: e.dma_start(out=dstv, in_=so[:]), r=[so], w=[so])

                    for s0 in range(6):
                        sD(s0)
                    sA(0)
                    for s in range(NSEQ):
                        if s + 6 < NSEQ:
                            sD(s + 6)
                        if s + 1 < NSEQ:
                            sA(s + 1)
                        sB(s)
                        if s % 4 == 3:
                            yield
                if not warm:
                    t1v = t1[0:q, :].rearrange("p (r d) -> p r d", d=HP)
                    op("dve", lambda e: e.tensor_tensor(out=t1v, in0=YOb[0:q, :].rearrange("p (r d) -> p r d", d=HP),
                                                        in1=bcl(eacum[0:q, c, hs], HP), op=ALU.mult), r=[YOb, eacum], w=[t1])
                    ysrc = ydg if smp else Yb
                    op("dve", lambda e: e.tensor_tensor(out=t1[0:q, :], in0=t1[0:q, :], in1=ysrc[0:q, :], op=ALU.add), r=[t1, ysrc], w=[t1])
                    op("dve", lambda e: e.tensor_tensor(out=t3[0:q, :].rearrange("p (r d) -> p r d", d=HP), in0=xv,
                                                        in1=bcl(bv[0:q, BV_DSK + g * 8:BV_DSK + g * 8 + 8], HP), op=ALU.mult),
                       r=[xbt, bv], w=[t3])
                    op("dve", lambda e: e.tensor_tensor(out=t1[0:q, :], in0=t1[0:q, :], in1=t3[0:q, :], op=ALU.add), r=[t1, t3], w=[t1])
                    op("dve", lambda e: e.tensor_tensor(out=t1[0:q, :], in0=t1[0:q, :], in1=szs[0:q, c, :], op=ALU.mult), r=[t1, szs], w=[t1])
                    yield "C2"
                    if (not smp) and need_sbf:
                        op("act", lambda e: e.activation(out=Sbf[:], in_=S[:], func=AF.Copy), r=[S], w=[Sbf])
                    s4 = rms_rstd(t1, q, 3, 512 ** -0.5, t3)
                    op("act", lambda e: e.activation(out=gn2[0:q, :], in_=t1[0:q, :], func=AF.Copy, scale=s4[0:q, 2:3]),
                       r=[t1, s4], w=[gn2])
                if last and kind == "main" and c == nch - 1:
                    P.dma("pool", lambda e: e.dma_start(out=o_ssmTp[:, g * 512:(g + 1) * 512], in_=S[:]), r=[S], w=[S])

            def pieceD(g, c):
                if warm:
                    return
                k = g * nch + c
                gn2 = gns[k % 2]
                tk = slice(c * q, (c + 1) * q)
                for j in range(4):
                    op("pe", lambda e, j=j: e.transpose(out=tp_ap[:, j, 0:q], in_=gn2[0:q, j * 128:(j + 1) * 128], identity=identb[0:q, 0:q]),
                       r=[gn2, identb], w=[Mb])
                op("dve", lambda e: e.tensor_tensor(out=ynT[:, g * 4:(g + 1) * 4, tk], in0=tp_ap[:, 0:4, 0:q],
                                                    in1=bcl(pv[:, PV_SNW + g * 4:PV_SNW + (g + 1) * 4], q), op=ALU.mult),
                   r=[Mb, pv], w=[ynT])

            if smp:
                for al in sst_extra:
                    if xio[2].res.w is not None:
                        al.res.r.append(xio[2].res.w)
                    al.res.r.extend(xio[2].res.r)
            for al in xio1_alias:
                if xio[1].res.w is not None:
                    al.res.r.append(xio[1].res.w)
                al.res.r.extend(xio[1].res.r)
            its = [(g, c) for g in range(NG) for c in range(nch)]
            NIT = len(its)
            s1 = {0: stage1(0, 0)}
            for _ in s1[0]:
                pass
            s1done = {0}

            def unit(gcur, n=1, flush_g=None):
                gg = flush_g if flush_g is not None else gcur + 1
                if gg >= NG or gg in s1done:
                    return
                if gg not in s1:
                    s1[gg] = stage1(gg, gg % 2)
                cnt = 0
                while flush_g is not None or cnt < n:
                    try:
                        next(s1[gg])
                    except StopIteration:
                        s1done.add(gg)
                        return
                    cnt += 1

            def drive(gen_or_none, gcur):
                if gen_or_none is None:
                    return
                for _ in gen_or_none:
                    unit(gcur)

            pieceA(*its[0])
            for t in range(NIT):
                g, c = its[t]
                if t + 1 < NIT:
                    gn_, cn_ = its[t + 1]
                    if gn_ != g:
                        unit(g, flush_g=gn_)
                    pieceA(gn_, cn_)
                pieceB(g, c)
                unit(g)
                if t >= 1:
                    pieceD(*its[t - 1])
                    unit(g)
                drive(pieceC(g, c), g)
                unit(g)
            pieceD(*its[NIT - 1])
            if smp:
                for al in sst_extra:
                    if al.res.w is not None:
                        xio[2].res.r.append(al.res.w)
                    xio[2].res.r.extend(al.res.r)
            for al in xio1_alias:
                if al.res.w is not None:
                    xio[1].res.r.append(al.res.w)
                xio[1].res.r.extend(al.res.r)
            if warm:
                return
            banks[:] = [A[0], A[1], Sg[0], Sg[1], STb]
            for m2 in range(8):
                sl2, wv2 = W.next(w_in_v[:, :, C_GA + m2 * 256:C_GA + (m2 + 1) * 256], 16, 256)
                for hf in range(2):
                    p1 = nbank()
                    for kc in range(16):
                        op("pe", lambda e, kc=kc: e.matmul(p1[:, 0:T], lhsT=wv2[:, kc, hf * 128:(hf + 1) * 128], rhs=hT[:, kc, 0:T],
                                                           start=(kc == 0), stop=(kc == 15)), r=[sl2, hT], w=[p1])
                    sig_chain(sgts[hf], sgts[hf][:, 0:T], p1, p1[:, 0:T])
                for hf in range(2):
                    m = m2 * 2 + hf
                    sl1, wv1 = W.next(w_so_v[:, :, m * 128:(m + 1) * 128], 32, 128)
                    p0 = nbank()
                    for kc in range(32):
                        op("pe", lambda e, kc=kc: e.matmul(p0[:, 0:T], lhsT=wv1[:, kc, :], rhs=ynT[:, kc, 0:T], start=(kc == 0), stop=(kc == 31)),
                           r=[sl1, ynT], w=[p0])
                    op("dve", lambda e: e.tensor_tensor(out=mrg[:, m, 0:T], in0=sgts[hf][:, 0:T], in1=p0[:, 0:T], op=ALU.mult),
                       r=[sgts[hf], p0], w=[mrg])
            urow = xio[2]
            for j2 in range(8):
                c = nch - 1
                slc, wvc = W.next(w_in_v[:, :, C_SC + j2 * 256:C_SC + (j2 + 1) * 256], 16, 256)
                for hf in range(2):
                    pa = inproj_fm(wvc, slc, hf * 128)
                    op("act", lambda e, pa=pa, hf=hf: e.activation(out=tcc[hf][:, 0:T], in_=pa[:, 0:T], func=AF.Copy), r=[pa], w=[tcc[hf]])
                if emit_rows:
                    for kc in range(16):
                        op("pe", lambda e, kc=kc: e.matmul(Yb[0:q, 0:256], lhsT=hT[:, kc, c * q:(c + 1) * q], rhs=wvc[:, kc, :],
                                                           start=(kc == 0), stop=(kc == 15)), r=[slc, hT], w=[Yb])
                    op("act", lambda e: e.activation(out=tcr[0:q, :], in_=Yb[0:q, 0:256], func=AF.Copy), r=[Yb], w=[tcr])
                slh, wvh = W.next(w_in_v[:, :, C_SH + j2 * 256:C_SH + (j2 + 1) * 256], 16, 256)
                for hf in range(2):
                    j = j2 * 2 + hf
                    ru = rawu[hf]
                    rawuv = rawuvs[hf]
                    pa = inproj_fm(wvh, slh, hf * 128)
                    if smp:
                        op("pe", lambda e, j=j: e.transpose(out=tp_ap[:, 0, 0:32], in_=scst[0:32, j * 128:(j + 1) * 128], identity=identb[0:32, 0:32]),
                           r=[scst, identb], w=[Mb])
                        op("dve", lambda e: e.tensor_copy(out=rawuv[:, :, 0:2], in_=tp_ap[:, 0, 0:32].rearrange("p (s k) -> p s k", k=2)),
                           r=[Mb], w=[ru])
                    elif first:
                        op("dve", lambda e: e.memset(rawuv[:, :, 0:2], 0.0), w=[ru])
                    else:
                        op("dve", lambda e, j=j: e.tensor_copy(out=rawuv[:, 0, 0:2], in_=caru[:, j, :]), r=[caru], w=[ru])
                    op("dve", lambda e, pa=pa: e.tensor_tensor(out=rawuv[:, :, 2:2 + L], in0=v3(tcc[hf][:, 0:T]), in1=v3(pa[:, 0:T]), op=ALU.mult),
                       r=[tcc[hf], pa], w=[ru])
                    if not smp:
                        op("dve", lambda e, j=j: e.tensor_copy(out=caru[:, j, :], in_=rawuv[:, 0, L:L + 2]), r=[ru], w=[caru])
                    ac = acc[hf]
                    acv = v3(ac[:, 0:T])
                    op("dve", lambda e, j=j, acv=acv: e.tensor_scalar(out=acv, in0=rawuv[:, :, 0:L], scalar1=pv[:, PV_SCW + j * 3:PV_SCW + j * 3 + 1],
                                                                      scalar2=None, op0=ALU.mult), r=[ru, pv], w=[ac])
                    for k in range(1, 3):
                        op("dve", lambda e, j=j, acv=acv, k=k: e.scalar_tensor_tensor(
                            out=acv, in0=rawuv[:, :, k:k + L], scalar=pv[:, PV_SCW + j * 3 + k:PV_SCW + j * 3 + k + 1], in1=acv,
                            op0=ALU.mult, op1=ALU.add), r=[ru, pv, ac], w=[ac])
                if emit_rows:
                    for kc in range(16):
                        op("pe", lambda e, kc=kc: e.matmul(YOb[0:q, 0:256], lhsT=hT[:, kc, c * q:(c + 1) * q], rhs=wvh[:, kc, :],
                                                           start=(kc == 0), stop=(kc == 15)), r=[slh, hT], w=[YOb])
                    op("dve", lambda e: e.tensor_tensor(out=urow[0:q, j2 * 256:(j2 + 1) * 256], in0=tcr[0:q, :], in1=YOb[0:q, 0:256], op=ALU.mult),
                       r=[tcr, YOb], w=[urow])
                slz, wvz = W.next(w_in_v[:, :, C_SZ + j2 * 256:C_SZ + (j2 + 1) * 256], 16, 256)
                for hf in range(2):
                    pa = inproj_fm(wvz, slz, hf * 128)
                    sig_chain(szz[hf], szz[hf][:, 0:T], pa, pa[:, 0:T])
                    op("dve", lambda e, pa=pa: e.tensor_tensor(out=szz[hf][:, 0:T], in0=szz[hf][:, 0:T], in1=pa[:, 0:T], op=ALU.mult),
                       r=[szz[hf], pa], w=[szz[hf]])
                slb, wvb = W.next(w_in_v[:, :, C_SB + j2 * 256:C_SB + (j2 + 1) * 256], 16, 256)
                for hf in range(2):
                    j = j2 * 2 + hf
                    ac = acc[hf]
                    pa = inproj_fm(wvb, slb, hf * 128)
                    op("dve", lambda e, pa=pa, ac=ac: e.tensor_tensor(out=ac[:, 0:T], in0=ac[:, 0:T], in1=pa[:, 0:T], op=ALU.mult), r=[ac, pa], w=[ac])
                    op("dve", lambda e, j=j, ac=ac: e.tensor_tensor(out=ynT[:, j, 0:T], in0=ac[:, 0:T], in1=szz[hf][:, 0:T], op=ALU.mult),
                       r=[ac, szz[hf]], w=[ynT])
            if emit_rows:
                od = o_sconvs if smp else o_sconvp
                P.dma("pool", lambda e: e.dma_start(out=od[0:q, :], in_=urow[0:q, :]), r=[urow], w=[urow])
            for m2 in range(8):
                sl2, wv2 = W.next(w_in_v[:, :, C_GB + m2 * 256:C_GB + (m2 + 1) * 256], 16, 256)
                for hf in range(2):
                    p1 = nbank()
                    for kc in range(16):
                        op("pe", lambda e, kc=kc: e.matmul(p1[:, 0:T], lhsT=wv2[:, kc, hf * 128:(hf + 1) * 128], rhs=hT[:, kc, 0:T],
                                                           start=(kc == 0), stop=(kc == 15)), r=[sl2, hT], w=[p1])
                    sig_chain(sgts[hf], sgts[hf][:, 0:T], p1, p1[:, 0:T])
                sl1, wv1 = W.next(w_sc_v[:, :, m2 * 256:(m2 + 1) * 256], 16, 256)
                for hf in range(2):
                    m = m2 * 2 + hf
                    p0 = nbank()
                    for kc in range(16):
                        op("pe", lambda e, kc=kc: e.matmul(p0[:, 0:T], lhsT=wv1[:, kc, hf * 128:(hf + 1) * 128], rhs=ynT[:, kc, 0:T],
                                                           start=(kc == 0), stop=(kc == 15)), r=[sl1, ynT], w=[p0])
                    op("dve", lambda e: e.tensor_tensor(out=sgts[hf][:, 0:T], in0=sgts[hf][:, 0:T], in1=p0[:, 0:T], op=ALU.mult),
                       r=[sgts[hf], p0], w=[sgts[hf]])
                    op("dve", lambda e: e.tensor_tensor(out=mrg[:, m, 0:T], in0=mrg[:, m, 0:T], in1=sgts[hf][:, 0:T], op=ALU.add),
                       r=[mrg, sgts[hf]], w=[mrg])
            for c in range(nch):
                xt = xio[c]
                P.dma("pool", lambda e, xt=xt, c=c: e.dma_start(out=xt[0:q, :], in_=xsrc[c * q:(c + 1) * q, :]), w=[xt])
            for s8 in range(8):
                sl, wv = W.next(w_o_v[:, :, s8 * 256:(s8 + 1) * 256], 16, 256)
                for c in range(nch):
                    pa = nbank()
                    for kc in range(16):
                        op("pe", lambda e, kc=kc, c=c, pa=pa, wv=wv: e.matmul(pa[0:q, 0:256], lhsT=mrg[:, kc, c * q:(c + 1) * q], rhs=wv[:, kc, :],
                                                                              start=(kc == 0), stop=(kc == 15)), r=[sl, mrg], w=[pa])
                    xt = xio[c]
                    op("dve", lambda e, xt=xt, pa=pa, s8=s8: e.tensor_tensor(out=xt[0:q, s8 * 256:(s8 + 1) * 256], in0=xt[0:q, s8 * 256:(s8 + 1) * 256],
                                                                             in1=pa[0:q, 0:256], op=ALU.add), r=[xt, pa], w=[xt])
            for c in range(nch):
                xt = xio[c]
                s = rms_rstd(xt, q, c, D ** -0.5, hb)
                op("dve", lambda e, xt=xt, s=s: e.scalar_tensor_tensor(out=xt[0:q, :], in0=xt[0:q, :], scalar=s[0:q, 2:3], in1=bv[0:q, BV_FNW:BV_FNW + D],
                                                                       op0=ALU.mult, op1=ALU.mult), r=[xt, s, bv], w=[xt])
                P.dma("pool", lambda e, xt=xt, c=c: e.dma_start(out=ysink[c * q:(c + 1) * q, :], in_=xt[0:q, :]), r=[xt], w=[xt])

        def emit_all():
            setup()
            for b in range(NBLK):
                block(xw_warm[b * TMAX:(b + 1) * TMAX, :], BLK, Q, "warm", b == 0, b == NBLK - 1, None)
            for b in range(NBLK):
                block(xw_main[b * TMAX:(b + 1) * TMAX, :], BLK, Q, "main", b == 0, b == NBLK - 1, o_yp[b * TMAX:(b + 1) * TMAX, :])
            block(xsmp, 1, 128, "sample", False, False, o_ys)

        P.dry = True
        emit_all()
        P.dry = False
        W.start()
        emit_all()
        allt = xio + sst + crow + Sst
        P.finish("pool", allt)
        P.finish("sp", W.scr_res + sst)
        P.emit()
    return nc


_CACHE = {}


def _consts():
    k = np.arange(128)
    c = np.zeros((128, NCST), np.float32)
    c[:, K_ID:K_ID + 128] = np.eye(128)
    c[:, K_TM:K_TM + 128] = (k[:, None] <= k[None, :])
    c[:, K_UM:K_UM + 128] = (k[:, None] > k[None, :])
    same = (k[:, None] // DSEQ) == (k[None, :] // DSEQ)
    c[:, K_TMS:K_TMS + 128] = (k[:, None] <= k[None, :]) & same
    c[:, K_UMS:K_UMS + 128] = (k[:, None] > k[None, :]) & same
    c[:, K_ONE:K_ONE + 128] = 1.0
    c[:, K_EM:K_EM + 128] = (k[None, :] < 64)
    c[:, K_OM:K_OM + 128] = (k[None, :] >= 64)
    c[:, K_BLK:K_BLK + 16] = (k[:, None] // DSEQ) == np.arange(16)[None, :]
    return c


def kernel(x_prompt, x_sample, state_ssd_conv, state_ssm, state_sconv, meta_tokens, norm_w, w_in, ssd_conv_w,
           ssd_conv_b, dt_bias, a_log, d_skip, ssd_norm_w, w_ssd_out, sconv_w, w_sconv_out, w_o, final_norm_w):
    f = lambda a: np.ascontiguousarray(np.asarray(a, dtype=np.float32))
    x_prompt, x_sample = f(x_prompt), f(x_sample)
    if "nc" not in _CACHE:
        _CACHE["nc"] = build_program()
    nc = _CACHE["nc"]
    pvec = np.zeros((128, NPV), np.float32)
    cw = f(ssd_conv_w)[0]
    pvec[:, PV_CW:PV_CW + 192] = cw.reshape(4, 48, 128).transpose(2, 1, 0).reshape(128, 192)
    pvec[:, PV_CB:PV_CB + 48] = f(ssd_conv_b)[0].reshape(48, 128).T
    pvec[:, PV_SCW:PV_SCW + 48] = f(sconv_w)[0].reshape(3, 16, 128).transpose(2, 1, 0).reshape(128, 48)
    pvec[:, PV_SNW:PV_SNW + 32] = f(ssd_norm_w)[0].reshape(32, 128).T
    pvec[:, PV_NW:PV_NW + 16] = f(norm_w)[0].reshape(16, 128).T
    bvec = np.zeros((1, NBV), np.float32)
    bvec[0, BV_DTB:BV_DTB + 64] = f(dt_bias)[0]
    bvec[0, BV_ALOG:BV_ALOG + 64] = f(a_log)[0]
    bvec[0, BV_DSK:BV_DSK + 64] = f(d_skip)[0]
    bvec[0, BV_FNW:BV_FNW + D] = f(final_norm_w)
    cbase = _consts()
    w_in0, w_so0, w_sc0, w_o0 = f(w_in)[0], f(w_ssd_out)[0], f(w_sconv_out)[0], f(w_o)[0]
    meta = f(meta_tokens)
    zeros3 = np.zeros((3, D), np.float32)
    in_maps = []
    for c in range(NCORES):
        b, half = c // 2, c % 2
        full = np.concatenate([zeros3, meta, x_prompt[b]], axis=0)
        win0 = full[0:WIN]
        win1 = full[HALF:HALF + WIN]
        cc = cbase.copy()
        cc[:, K_ROLE] = float(half)
        sl = slice(c * NSEQ, (c + 1) * NSEQ)
        in_maps.append({
            "xw_main": np.ascontiguousarray(win1 if half else win0),
            "xw_warm": np.ascontiguousarray(win0) if half else np.zeros((WIN, D), np.float32),
            "xsmp": np.ascontiguousarray(x_sample[sl].reshape(128, D)),
            "consts": cc, "pvec": pvec, "bvec": bvec,
            "cst_conv": np.ascontiguousarray(f(state_ssd_conv)[0, sl].reshape(NSEQ * 3, CONVD)),
            "cst_ssm": np.ascontiguousarray(f(state_ssm)[0, sl]),
            "cst_sconv": np.ascontiguousarray(f(state_sconv)[0, sl].reshape(NSEQ * 2, D)),
            "w_in": w_in0, "w_ssd_out": w_so0, "w_sconv_out": w_sc0, "w_o": w_o0,
        })
    res = run_bass_kernel_spmd(nc, in_maps, core_ids=list(range(NCORES)))
    R = res.results
    nb = x_prompt.shape[0]
    y_prompt = np.zeros((nb, SEQ, D), np.float32)
    conv_p = np.zeros((1, nb, 3, CONVD), np.float32)
    ssm_p = np.zeros((1, nb, NH, HP, NS), np.float32)
    sconv_p = np.zeros((1, nb, 2, D), np.float32)
    for b in range(nb):
        y_prompt[b, 0:HALF - META] = R[2 * b]["o_yp"][3 + META:WIN]
        y_prompt[b, HALF - META:] = R[2 * b + 1]["o_yp"][3:WIN]
        conv_p[0, b] = R[2 * b + 1]["o_convp"][Q - 3:Q]
        ssm_p[0, b] = R[2 * b + 1]["o_ssmTp"].T.reshape(NH, HP, NS)
        sconv_p[0, b] = R[2 * b + 1]["o_sconvp"][Q - 2:Q]
    y_sample = np.concatenate([R[c]["o_ys"].reshape(NSEQ, DSEQ, D) for c in range(NCORES)], axis=0)
    conv_s = np.concatenate([R[c]["o_convs"].reshape(NSEQ, DSEQ, CONVD)[:, DSEQ - 3:] for c in range(NCORES)], axis=0)[None]
    ssm_s = np.concatenate([R[c]["o_ssms"] for c in range(NCORES)], axis=0)[None]
    sconv_s = np.concatenate([R[c]["o_sconvs"].reshape(NSEQ, DSEQ, D)[:, DSEQ - 2:] for c in range(NCORES)], axis=0)[None]
    return (y_prompt, y_sample, np.ascontiguousarray(conv_p), np.ascontiguousarray(ssm_p), np.ascontiguousarray(sconv_p),
            np.ascontiguousarray(conv_s), np.ascontiguousarray(ssm_s), np.ascontiguousarray(sconv_s))
```

```python
import numpy as np
from contextlib import ExitStack
import concourse.bass as bass
import concourse.mybir as mybir
from concourse.bass_utils import run_bass_kernel_spmd

F32 = mybir.dt.float32
BF16 = mybir.dt.bfloat16
ALU = mybir.AluOpType
AF = mybir.ActivationFunctionType

D = 2048
DI = 4096
NH = 64
HP = 64
NS = 128
NG = 8
CONVD = 6144
PROJ = 22592
EPS = 1e-6
META = 16
SEQ = 2048
NCORES = 8
Q = 115
NCHW = 9
WIN = Q * NCHW
HALF = 1032
BLK = 3
NBLK = NCHW // BLK
TMAX = BLK * Q
NSEQ = 16
DSEQ = 8
C_Z = 0
C_X = DI
C_B = DI + DI
C_C = DI + DI + 1024
C_DT = DI + CONVD
C_SB = C_DT + NH
C_SC = C_SB + D
C_SH = C_SC + D
C_SZ = C_SH + D
C_GA = C_SZ + D
C_GB = C_GA + D
K_ID, K_TM, K_UM, K_TMS, K_UMS, K_ONE, K_EM, K_OM, K_BLK, K_ROLE = 0, 128, 256, 384, 512, 640, 768, 896, 1024, 1040
NCST = 1041
PV_CW, PV_CB, PV_SCW, PV_SNW, PV_NW = 0, 192, 240, 288, 320
NPV = 336
BV_DTB, BV_ALOG, BV_DSK, BV_FNW = 0, 64, 128, 192
NBV = 192 + D


class Res:
    __slots__ = ("w", "r")

    def __init__(self):
        self.w = None
        self.r = []


class TT:
    def __init__(self, t):
        self.t = t
        self.res = Res()

    def __getitem__(self, k):
        return self.t[k]


class _Rec:
    def __init__(self):
        self.call = None

    def __getattr__(self, name):
        def f(*a, **k):
            self.call = (name, a, k)
            return self
        return f


def _record(fn):
    r = _Rec()
    fn(r)
    name, a, k = r.call
    return lambda e: getattr(e, name)(*a, **k)


class Prog:
    ENG = ("pe", "act", "dve", "pool", "sp")

    def __init__(self, nc, stack, n_dma_sems=56):
        self.nc = nc
        self.dry = False
        self.sem = {e: stack.enter_context(nc.semaphore("c_" + e)) for e in self.ENG}
        self.dsem = [stack.enter_context(nc.semaphore("d%d" % i)) for i in range(n_dma_sems)]
        self.dpool = {"pool": list(range(0, 32)), "sp": list(range(32, n_dma_sems))}
        self.reset()

    def reset(self):
        self.q = {e: [] for e in self.ENG}
        self.cnt = {e: 0 for e in self.ENG}
        self.seen = {e: {} for e in self.ENG}
        self.dval = [0] * len(self.dsem)
        self.dnext = {"pool": 0, "sp": 0}

    def _deps(self, reads, writes):
        deps = []
        for r in reads:
            if r.res.w is not None:
                deps.append(r.res.w)
        for w in writes:
            if w.res.w is not None:
                deps.append(w.res.w)
            deps.extend(w.res.r)
        return deps

    def _waits(self, eng, deps, skip_self):
        need = {}
        seen = self.seen[eng]
        for (k, v) in deps:
            if skip_self and k == eng:
                continue
            if seen.get(k, 0) >= v:
                continue
            if need.get(k, 0) < v:
                need[k] = v
        for k, v in need.items():
            seen[k] = v
        return list(need.items())

    def _mark(self, tok, reads, writes):
        for r in reads:
            r.res.r.append(tok)
        for w in writes:
            w.res.w = tok
            w.res.r = []

    def op(self, eng, fn, r=(), w=()):
        if self.dry:
            return
        waits = self._waits(eng, self._deps(r, w), eng == "pe")
        self.cnt[eng] += 1
        tok = (eng, self.cnt[eng])
        self.q[eng].append((waits, _record(fn), (eng, 1)))
        self._mark(tok, r, w)

    def dma(self, eng, fn, r=(), w=()):
        if self.dry:
            return
        deps = self._deps(r, w)
        pool = self.dpool[eng]
        i = pool[self.dnext[eng] % len(pool)]
        self.dnext[eng] += 1
        if self.dval[i] > 0:
            deps.append((i, self.dval[i]))
        waits = self._waits(eng, deps, False)
        self.dval[i] += 16
        tok = (i, self.dval[i])
        self.q[eng].append((waits, _record(fn), (i, 16)))
        self._mark(tok, r, w)

    def finish(self, eng, tts):
        deps = []
        for t in tts:
            if t.res.w is not None:
                deps.append(t.res.w)
            deps.extend(t.res.r)
        self.q[eng].append((self._waits(eng, deps, False), None, None))

    def _semobj(self, k):
        return self.sem[k] if isinstance(k, str) else self.dsem[k]

    def emit(self):
        import bisect
        sig = {e: set() for e in self.ENG}
        for name in self.ENG:
            for waits, fn, inc in self.q[name]:
                for k, v in waits:
                    if isinstance(k, str):
                        sig[k].add(v)
        sigl = {e: sorted(sig[e]) for e in self.ENG}

        def remap(k, v):
            if isinstance(k, str):
                return bisect.bisect_right(sigl[k], v)
            return v

        def run(name):
            def body(e):
                n = 0
                for waits, fn, inc in self.q[name]:
                    for k, v in waits:
                        e.wait_ge(self._semobj(k), remap(k, v))
                    if fn is not None:
                        ins = fn(e)
                        if isinstance(inc[0], str):
                            n += 1
                            if n in sig[name]:
                                ins.then_inc(self._semobj(inc[0]), 1)
                        else:
                            ins.then_inc(self._semobj(inc[0]), inc[1])
            return body
        with self.nc.Block() as block:
            block.tensor(run("pe"))
            block.scalar(run("act"))
            block.vector(run("dve"))
            block.gpsimd(run("pool"))
            block.sync(run("sp"))


class WStream:
    def __init__(self, P, slots, depth, nc):
        self.P = P
        self.slots = slots
        self.depth = depth
        self.nc = nc
        self.specs = []
        self.i = 0
        self.issued = 0

    def start(self):
        self.i = 0
        self.issued = 0
        keys = {}
        self.kidx = []
        self.firstuse = []
        for (src, k, n) in self.specs:
            key = (src.name, str(src.offset), k, n)
            self.firstuse.append(key not in keys)
            if key not in keys:
                keys[key] = len(keys)
            self.kidx.append(keys[key])
        self.scr = self.nc.dram_tensor("wscr", [len(keys), 128, 4096], BF16).ap()
        self.scr_res = [TT(None) for _ in keys]
        self.conv_ptr = 0

    def _view(self, slot, kcs, ncols):
        return slot.t[:, 0:kcs * ncols].rearrange("p (k n) -> p k n", n=ncols)

    def next(self, src, kcs, ncols):
        if self.P.dry:
            self.specs.append((src, kcs, ncols))
            return self.slots[0], self._view(self.slots[0], kcs, ncols)
        i = self.i
        assert self.specs[i][1:] == (kcs, ncols)
        while self.issued < len(self.specs) and self.issued <= i + self.depth:
            j = self.issued
            s, k, n = self.specs[j]
            sl = self.slots[j % len(self.slots)]
            ki = self.kidx[j]
            flat = sl.t[:, 0:k * n]
            if self.firstuse[j]:
                v = self._view(sl, k, n)
                self.P.dma("pool", lambda e: e.dma_start(out=v, in_=s), w=[sl])
                self.P.dma("sp", lambda e: e.dma_start(out=self.scr[ki, :, 0:k * n], in_=flat), r=[sl], w=[self.scr_res[ki]])
            else:
                self.P.dma("sp", lambda e: e.dma_start(out=flat, in_=self.scr[ki, :, 0:k * n]), r=[self.scr_res[ki]], w=[sl])
            self.issued += 1
        jc = max(self.conv_ptr, self.issued + 6)
        while jc < len(self.specs) and not self.firstuse[jc]:
            jc += 1
        if jc < len(self.specs):
            s2, k2, n2 = self.specs[jc]
            ki2 = self.kidx[jc]
            dst = self.scr[ki2, :, 0:k2 * n2].rearrange("p (k n) -> p k n", n=n2)
            self.P.dma("pool", lambda e: e.dma_start(out=dst, in_=s2), w=[self.scr_res[ki2]])
            self.firstuse[jc] = False
            self.conv_ptr = jc + 1
        self.i += 1
        sl = self.slots[i % len(self.slots)]
        return sl, self._view(sl, kcs, ncols)


def bcl(ap2, n):
    return ap2.unsqueeze(2).broadcast_to([ap2.shape[0], ap2.shape[1], n])


def bcm(ap2, n):
    return ap2.unsqueeze(1).broadcast_to([ap2.shape[0], n, ap2.shape[1]])


def build_program():
    nc = bass.Bass("TRN2", target_bir_lowering=False)
    dt_in = lambda n, s: nc.dram_tensor(n, s, F32, kind="ExternalInput").ap()
    dt_out = lambda n, s: nc.dram_tensor(n, s, F32, kind="ExternalOutput").ap()
    xw_main = dt_in("xw_main", [WIN, D])
    xw_warm = dt_in("xw_warm", [WIN, D])
    xsmp = dt_in("xsmp", [128, D])
    consts = dt_in("consts", [128, NCST])
    pvec = dt_in("pvec", [128, NPV])
    bvec = dt_in("bvec", [1, NBV])
    cst_conv = dt_in("cst_conv", [NSEQ * 3, CONVD])
    cst_ssm = dt_in("cst_ssm", [NSEQ, NH, HP, NS])
    cst_sconv = dt_in("cst_sconv", [NSEQ * 2, D])
    w_in = dt_in("w_in", [D, PROJ])
    w_ssd_out = dt_in("w_ssd_out", [DI, D])
    w_sconv_out = dt_in("w_sconv_out", [D, D])
    w_o = dt_in("w_o", [D, D])
    o_yp = dt_out("o_yp", [WIN, D])
    o_ys = dt_out("o_ys", [128, D])
    o_convp = dt_out("o_convp", [Q, CONVD])
    o_ssmTp = dt_out("o_ssmTp", [128, DI])
    o_sconvp = dt_out("o_sconvp", [Q, D])
    o_convs = dt_out("o_convs", [128, CONVD])
    o_ssms = dt_out("o_ssms", [NSEQ, NH, HP, NS])
    o_sconvs = dt_out("o_sconvs", [128, D])

    w_in_v = w_in.rearrange("(kc p) n -> p kc n", p=128)
    w_so_v = w_ssd_out.rearrange("(kc p) n -> p kc n", p=128)
    w_sc_v = w_sconv_out.rearrange("(kc p) n -> p kc n", p=128)
    w_o_v = w_o.rearrange("(kc p) n -> p kc n", p=128)

    with ExitStack() as st:
        P = Prog(nc, st)
        sb = lambda n, s, d: TT(st.enter_context(nc.sbuf_tensor(n, s, d)))
        psb = lambda n: TT(st.enter_context(nc.psum_tensor(n, [128, 512], F32)))
        cst = sb("cst", [128, NCST], F32)
        identb = sb("identb", [128, 128], BF16)
        pv = sb("pv", [128, NPV], F32)
        pvn = sb("pvn", [128, 48], F32)
        bv = sb("bv", [128, NBV], F32)
        negA = sb("negA", [128, NH], F32)
        hT = sb("hT", [128, 16, TMAX], BF16)
        ynT = sb("ynT", [128, 32, TMAX], BF16)
        mrg = sb("mrg", [128, 16, TMAX], BF16)
        xio = [sb("xio%d" % i, [128, D], F32) for i in range(3)]
        hb = sb("hb", [128, D], BF16)
        x1 = xio[1].t
        segr1 = TT(x1[:, 0:1024].rearrange("p (a b) -> p a b", b=128))
        wT1 = TT(x1[:, 1024:1536].bitcast(BF16).rearrange("p (a b) -> p a b", b=128))
        xdt1 = TT(x1[:, 1536:1792].bitcast(BF16))
        xs1 = TT(x1[:, 1792:2048].bitcast(BF16))
        xio1_alias = [segr1, wT1, xdt1, xs1]
        x2 = xio[2].t
        sst_extra = [TT(x2[:, i * 512:(i + 1) * 512].rearrange("p (a n) -> p a n", n=128)) for i in range(4)]
        stt = [sb("stt%d" % i, [128, 4], F32) for i in range(4)]
        wsl = [sb("wsl%d" % i, [128, 4096], BF16) for i in range(3)]
        dtt = sb("dtt", [128, BLK, NH], F32)
        dtmp = sb("dtmp", [128, NH], F32)
        la = sb("la", [128, BLK, NH], F32)
        toend = sb("toend", [128, BLK, NH], F32)
        eacum = sb("eacum", [128, BLK, NH], F32)
        cdb = sb("cdb", [128, BLK, NH], F32)
        raw = sb("raw", [128, 6, 3 + TMAX], BF16)
        acc = [sb("acc%d" % i, [128, TMAX], F32) for i in range(2)]
        xbcTs = [sb("xbcT%d" % i, [128, 6, TMAX], BF16) for i in range(2)]
        xbtoks = [sb("xbtok%d" % i, [128, 640], BF16) for i in range(3)]
        szS = [sb("sz%d" % i, [128, BLK, 512], BF16) for i in range(2)]
        carx = sb("carx", [128, 48, 3], BF16)
        caru = sb("caru", [128, 16, 2], F32)
        cbms = [sb("cbm%d" % i, [128, 128], BF16) for i in range(2)]
        segr0 = sb("segr0", [128, 8, 128], F32)
        dec = sb("dec", [128, 8, 128], BF16)
        wT0 = sb("wT0", [128, 8, 128], BF16)
        xdt0 = sb("xdt0", [128, 512], BF16)
        xs0 = sb("xs0", [128, 512], BF16)
        t1 = sb("t1", [128, 512], F32)
        t3 = sb("t3", [128, 512], F32)
        ydg = t3
        gns = [sb("gn%d" % i, [128, 512], BF16) for i in range(2)]
        segrs = [segr0, segr1]
        wTs = [wT0, wT1]
        xdts = [xdt0, xdt1]
        xss = [xs0, xs1]
        big = sb("big", [128, 4096], F32)
        Sst = [TT(big.t[:, g * 512:(g + 1) * 512]) for g in range(NG)]
        Sbf = sb("Sbf", [128, 512], BF16)
        sgts = [sb("sgt%d" % i, [128, TMAX], F32) for i in range(2)]
        tcc = [sb("tcc%d" % i, [128, TMAX], F32) for i in range(2)]
        szz = [sb("szz%d" % i, [128, TMAX], F32) for i in range(2)]
        rawu = [sb("rawu%d" % i, [128, 2 + TMAX], F32) for i in range(2)]
        crow = [sb("crow%d" % i, [128, 256], F32) for i in range(2)]
        tcr = sb("tcr", [128, 256], F32)
        cvst = sb("cvst", [48, 768], BF16)
        scst = sb("scst", [32, D], BF16)
        CTm = TT(big.t[:, 0:1024].bitcast(BF16).rearrange("p (s n) -> p s n", n=128))
        Btm = TT(big.t[:, 1024:2048].bitcast(BF16).rearrange("p (s n) -> p s n", n=128))
        Rv = TT(big.t[:, 2048:3072].rearrange("p (a n) -> p a n", n=512))
        cds = TT(big.t[:, 3072:3584].rearrange("p (s h) -> p s h", h=32))
        smp_alias = [CTm, Btm, Rv, cds]
        sst = [sb("sst%d" % i, [128, 4, 128], F32) for i in range(4)] + sst_extra
        sstb = [TT(big.t[:, 3584 + 256 * i:3584 + 256 * (i + 1)].bitcast(BF16).rearrange("p (a n) -> p a n", n=128)) for i in range(2)]
        smp_alias += sstb
        stb = [sb("stb%d" % i, [128, 512], BF16) for i in range(2)]
        A = [psb("pA0"), psb("pA1")]
        Sg = [psb("pS0"), psb("pS1")]
        Yb = psb("pY")
        YOb = psb("pYO")
        STb = psb("pST")
        Mb = psb("pM")
        tp_ap = Mb.t[:, 192:512].bitcast(BF16).rearrange("p (a b) -> p a b", b=128)
        Cb = Mb

        W = WStream(P, wsl, 2, nc)

        def op(eng, fn, r=(), w=()):
            P.op(eng, fn, r, w)

        def cK(k, n=128, rows=128):
            return cst[0:rows, k:k + n]

        def setup():
            P.dma("pool", lambda e: e.dma_start(out=cst[:], in_=consts), w=[cst])
            P.dma("pool", lambda e: e.dma_start(out=pv[:], in_=pvec), w=[pv])
            P.dma("pool", lambda e: e.dma_start(out=bv[:], in_=bvec.partition_broadcast(128)), w=[bv])
            op("dve", lambda e: e.tensor_copy(out=identb[:], in_=cst[:, K_ID:K_ID + 128]), r=[cst], w=[identb])
            op("dve", lambda e: e.tensor_scalar(out=pvn[:], in0=pv[:, PV_CB:PV_CB + 48], scalar1=-1.0, scalar2=None, op0=ALU.mult),
               r=[pv], w=[pvn])
            op("act", lambda e: e.activation(out=negA[:], in_=bv[:, BV_ALOG:BV_ALOG + NH], func=AF.Exp), r=[bv], w=[negA])
            op("dve", lambda e: e.tensor_scalar(out=negA[:], in0=negA[:], scalar1=-1.0, scalar2=None, op0=ALU.mult),
               r=[negA], w=[negA])

        def rms_rstd(src_tt, rows, sti, n_inv_sqrt, junk):
            s = stt[sti]
            op("act", lambda e: e.activation(out=junk[0:rows, :], in_=src_tt[0:rows, :], func=AF.Square,
                                             scale=float(n_inv_sqrt), accum_out=s[0:rows, 0:1]),
               r=[src_tt], w=[junk, s])
            op("act", lambda e: e.activation(out=s[0:rows, 1:2], in_=s[0:rows, 0:1], func=AF.Ln, bias=EPS, scale=1.0),
               r=[s], w=[s])
            op("act", lambda e: e.activation(out=s[0:rows, 2:3], in_=s[0:rows, 1:2], func=AF.Exp, scale=-0.5), r=[s], w=[s])
            return s

        def sig_chain(dst_tt, dst_ap, src_tt, src_ap, negb=None):
            if negb is None:
                op("act", lambda e: e.activation(out=dst_ap, in_=src_ap, func=AF.Exp, scale=-1.0), r=[src_tt], w=[dst_tt])
            else:
                op("act", lambda e: e.activation(out=dst_ap, in_=src_ap, func=AF.Exp, scale=-1.0, bias=negb), r=[src_tt, pvn], w=[dst_tt])
            op("act", lambda e: e.activation(out=dst_ap, in_=dst_ap, func=AF.Ln, bias=1.0, scale=1.0), r=[dst_tt], w=[dst_tt])
            op("act", lambda e: e.activation(out=dst_ap, in_=dst_ap, func=AF.Exp, scale=-1.0), r=[dst_tt], w=[dst_tt])

        def block(xsrc, nch, q, kind, first, last, ysink):
            T = nch * q
            smp = kind == "sample"
            warm = kind == "warm"
            nseq, L = (NSEQ, DSEQ) if smp else (1, T)
            kTM, kUM = (K_TMS, K_UMS) if smp else (K_TM, K_UM)
            emit_rows = last or smp
            rawv = raw.t[:, :, 0:nseq * (3 + L)].rearrange("p j (s l) -> p j s l", l=3 + L)
            rawuvs = [ru.t[:, 0:nseq * (2 + L)].rearrange("p (s l) -> p s l", l=2 + L) for ru in rawu]

            def v3(ap2):
                return ap2.rearrange("p (s l) -> p s l", l=L)

            for c in range(nch):
                xt = xio[c]
                P.dma("pool", lambda e, xt=xt, c=c: e.dma_start(out=xt[0:q, :], in_=xsrc[c * q:(c + 1) * q, :]), w=[xt])
                s = rms_rstd(xt, q, c, D ** -0.5, hb)
                op("dve", lambda e, xt=xt, s=s: e.tensor_scalar(out=hb[0:q, :], in0=xt[0:q, :], scalar1=s[0:q, 2:3],
                                                                scalar2=None, op0=ALU.mult), r=[xt, s], w=[hb])
                for g4 in range(4):
                    for j in range(4):
                        kc = g4 * 4 + j
                        op("pe", lambda e, kc=kc, j=j: e.transpose(out=tp_ap[:, j, 0:q], in_=hb[0:q, kc * 128:(kc + 1) * 128],
                                                                   identity=identb[0:q, 0:q]), r=[hb, identb], w=[Mb])
                    op("dve", lambda e, g4=g4, c=c: e.tensor_tensor(
                        out=hT[:, g4 * 4:(g4 + 1) * 4, c * q:(c + 1) * q], in0=tp_ap[:, 0:4, 0:q],
                        in1=bcl(pv[:, PV_NW + g4 * 4:PV_NW + (g4 + 1) * 4], q), op=ALU.mult), r=[Mb, pv], w=[hT])
            sl, wv = W.next(w_in_v[:, :, C_DT:C_DT + NH], 16, NH)
            for c in range(nch):
                for kc in range(16):
                    op("pe", lambda e, kc=kc, c=c, wv=wv: e.matmul(Mb[0:q, 128:192], lhsT=hT[:, kc, c * q:(c + 1) * q],
                                                                   rhs=wv[:, kc, :], start=(kc == 0), stop=(kc == 15)),
                       r=[hT, sl], w=[Mb])
                op("dve", lambda e: e.tensor_tensor(out=dtmp[0:q, :], in0=Mb[0:q, 128:192], in1=bv[0:q, BV_DTB:BV_DTB + NH],
                                                    op=ALU.add), r=[Mb, bv], w=[dtmp])
                op("act", lambda e: e.activation(out=dtmp[0:q, :], in_=dtmp[0:q, :], func=AF.Exp), r=[dtmp], w=[dtmp])
                op("act", lambda e, c=c: e.activation(out=dtt[0:q, c, :], in_=dtmp[0:q, :], func=AF.Ln, bias=1.0, scale=1.0),
                   r=[dtmp], w=[dtt])
                if first and c == 0:
                    op("dve", lambda e: e.memset(dtt[0:3, 0, :], 0.0), w=[dtt])
                op("dve", lambda e, c=c: e.tensor_tensor(out=la[0:q, c, :], in0=dtt[0:q, c, :], in1=negA[0:q, :], op=ALU.mult),
                   r=[dtt, negA], w=[la])
                pa = A[c % 2]
                op("pe", lambda e, c=c, pa=pa: e.matmul(pa[0:q, 0:64], lhsT=cK(kUM, q, q), rhs=la[0:q, c, :], start=True, stop=True),
                   r=[cst, la], w=[pa])
                op("pe", lambda e, c=c, pa=pa: e.matmul(pa[0:q, 64:128], lhsT=cK(kTM, q, q), rhs=la[0:q, c, :], start=True, stop=True),
                   r=[cst, la], w=[pa])
                op("pe", lambda e, c=c, pa=pa: e.matmul(pa[:, 128:192], lhsT=cK(K_ONE, 128, q), rhs=la[0:q, c, :], start=True, stop=True),
                   r=[cst, la], w=[pa])
                op("act", lambda e, c=c, pa=pa: e.activation(out=toend[0:q, c, :], in_=pa[0:q, 0:64], func=AF.Exp), r=[pa], w=[toend])
                op("act", lambda e, c=c, pa=pa: e.activation(out=eacum[0:q, c, :], in_=pa[0:q, 64:128], func=AF.Exp), r=[pa], w=[eacum])
                op("act", lambda e, c=c, pa=pa: e.activation(out=cdb[:, c, :], in_=pa[:, 128:192], func=AF.Exp), r=[pa], w=[cdb])

            if smp:
                for al in smp_alias:
                    for S_ in Sst:
                        if S_.res.w is not None:
                            al.res.r.append(S_.res.w)
                        al.res.r.extend(S_.res.r)
                op("dve", lambda e: e.memset(CTm[:], 0.0), w=[CTm])
                P.dma("pool", lambda e: e.dma_start(out=scst[:], in_=cst_sconv), w=[scst])
                lav = la.t[:, 0, :].rearrange("p (hh two) -> p hh two", two=2)
                for par in range(2):
                    op("dve", lambda e, par=par: e.tensor_tensor(
                        out=Rv.t[:, par, :].rearrange("p (s h) -> p s h", h=32), in0=bcl(cst[:, K_BLK:K_BLK + 16], 32),
                        in1=bcm(lav[:, :, par], 16), op=ALU.mult), r=[cst, la], w=[Rv])
                op("pe", lambda e: e.matmul(A[0][:, 0:512], lhsT=cK(K_EM), rhs=Rv[:, 0, :], start=True, stop=False), r=[cst, Rv], w=[A[0]])
                op("pe", lambda e: e.matmul(A[0][:, 0:512], lhsT=cK(K_OM), rhs=Rv[:, 1, :], start=False, stop=True), r=[cst, Rv], w=[A[0]])
                op("act", lambda e: e.activation(out=cds.t[:].rearrange("p s h -> p (s h)"), in_=A[0][:, 0:512], func=AF.Exp),
                   r=[A[0]], w=[cds])

            bank_i = [0]
            banks = [A[0], A[1]]
            slabref = [None]
            raws = [TT(raw.t[:, jj, :]) for jj in range(6)]

            def nbank():
                pa = banks[bank_i[0] % len(banks)]
                bank_i[0] += 1
                return pa

            def inproj_fm(wv, sl, j0, ncols_used=128):
                pa = nbank()
                for kc in range(16):
                    op("pe", lambda e, kc=kc, pa=pa: e.matmul(pa[:, 0:T], lhsT=wv[:, kc, j0:j0 + 128], rhs=hT[:, kc, 0:T],
                                                              start=(kc == 0), stop=(kc == 15)), r=[sl, hT], w=[pa])
                return pa

            def rows_tm(wv, sl, ncols, colbase, out_dram):
                pa = nbank()
                c = nch - 1
                for kc in range(16):
                    op("pe", lambda e, kc=kc, pa=pa: e.matmul(pa[0:q, 0:ncols], lhsT=hT[:, kc, c * q:(c + 1) * q], rhs=wv[:, kc, 0:ncols],
                                                              start=(kc == 0), stop=(kc == 15)), r=[sl, hT], w=[pa])
                cr = crow[bank_i[0] % 2]
                op("act", lambda e, pa=pa, cr=cr: e.activation(out=cr[0:q, 0:ncols], in_=pa[0:q, 0:ncols], func=AF.Copy), r=[pa], w=[cr])
                P.dma("pool", lambda e, cr=cr: e.dma_start(out=out_dram[0:q, colbase:colbase + ncols], in_=cr[0:q, 0:ncols]), r=[cr], w=[cr])

            def stage1(g, si):
                xb = xbcTs[si]
                szs = szS[si]
                pend = [None, None, None]

                def step(job):
                    if pend[2] is not None and pend[2][3] is not None:
                        pend[2][3]()
                    if pend[0] is not None:
                        pend[0][1]()
                    if pend[1] is not None and pend[1][2] is not None:
                        pend[1][2]()
                    if job is not None:
                        job[0]()
                    pend[2], pend[1], pend[0] = pend[1], pend[0], job

                if smp:
                    P.dma("pool", lambda e: e.dma_start(out=cvst[:, 0:512], in_=cst_conv[:, g * 512:(g + 1) * 512]), w=[cvst])
                    P.dma("pool", lambda e: e.dma_start(out=cvst[:, 512:640], in_=cst_conv[:, DI + g * 128:DI + (g + 1) * 128]), w=[cvst])
                    P.dma("pool", lambda e: e.dma_start(out=cvst[:, 640:768], in_=cst_conv[:, DI + 1024 + g * 128:DI + 1024 + (g + 1) * 128]), w=[cvst])
                specs = [(C_X + g * 512, 256, [0, 1]), (C_X + g * 512 + 256, 256, [2, 3]), (C_B + g * 128, 128, [4])]
                if not warm:
                    specs.append((C_C + g * 128, 128, [5]))
                njob = [0]
                for (col0, ncols, jjs) in specs:
                    for jn, jj in enumerate(jjs):
                        cc = (col0 - C_X) // 128 + jn
                        n = njob[0]
                        njob[0] += 1
                        st_ = {}
                        rw = raws[jj]
                        rv = rawv[:, jj]
                        ac = acc[n % 2]
                        acv = v3(ac[:, 0:T])
                        sg_ = sgts[n % 2]

                        def pe(col0=col0, ncols=ncols, jn=jn, jj=jj, st_=st_, last_of_slab=(jn == len(jjs) - 1)):
                            if jn == 0:
                                st_["slab"] = W.next(w_in_v[:, :, col0:col0 + ncols], 16, ncols)
                                slabref[0] = st_["slab"]
                            sl, wv = slabref[0]
                            st_["pa"] = inproj_fm(wv, sl, jn * 128)
                            if smp:
                                op("pe", lambda e: e.transpose(out=tp_ap[:, 0, 0:48], in_=cvst[0:48, jj * 128:(jj + 1) * 128],
                                                               identity=identb[0:48, 0:48]), r=[cvst, identb], w=[Mb])
                                op("act", lambda e: e.activation(out=rv[:, :, 0:3], in_=tp_ap[:, 0, 0:48].rearrange("p (s k) -> p s k", k=3),
                                                                 func=AF.Copy), r=[Mb], w=[rw])
                            if last_of_slab and emit_rows and not warm:
                                rows_tm(wv, sl, ncols, col0 - C_X, o_convs if smp else o_convp)

                        def pa_(jj=jj, cc=cc, st_=st_, rw=rw, rv=rv, ac=ac, acv=acv):
                            pa = st_["pa"]
                            if smp:
                                pass
                            elif first:
                                op("dve", lambda e: e.memset(rv[:, :, 0:3], 0.0), w=[rw])
                            else:
                                op("act", lambda e: e.activation(out=rv[:, 0, 0:3], in_=carx[:, cc, :], func=AF.Copy), r=[carx], w=[rw])
                            op("act", lambda e: e.activation(out=rv[:, :, 3:3 + L], in_=v3(pa[:, 0:T]), func=AF.Copy), r=[pa], w=[rw])
                            if not smp:
                                op("act", lambda e: e.activation(out=carx[:, cc, :], in_=rv[:, 0, L:L + 3], func=AF.Copy), r=[rw], w=[carx])
                            op("act", lambda e: e.activation(out=acv, in_=rv[:, :, 0:L], func=AF.Copy, scale=pv[:, PV_CW + cc * 4:PV_CW + cc * 4 + 1]),
                               r=[rw, pv], w=[ac])
                            for k in range(1, 4):
                                op("dve", lambda e, k=k: e.scalar_tensor_tensor(
                                    out=acv, in0=rv[:, :, k:k + L], scalar=pv[:, PV_CW + cc * 4 + k:PV_CW + cc * 4 + k + 1], in1=acv,
                                    op0=ALU.mult, op1=ALU.add), r=[rw, pv, ac], w=[ac])

                        def pb_(cc=cc, ac=ac, sg_=sg_):
                            sig_chain(sg_, sg_[:, 0:T], ac, ac[:, 0:T], negb=pvn[:, cc:cc + 1])

                        def pc_(jj=jj, cc=cc, ac=ac, sg_=sg_):
                            op("dve", lambda e: e.scalar_tensor_tensor(out=xb[:, jj, 0:T], in0=ac[:, 0:T], scalar=pv[:, PV_CB + cc:PV_CB + cc + 1],
                                                                       in1=sg_[:, 0:T], op0=ALU.add, op1=ALU.mult), r=[ac, pv, sg_], w=[xb])

                        step((pe, pa_, pb_, pc_))
                        yield
                if not warm:
                    for zs in range(2):
                        for c in range(nch):
                            n = njob[0]
                            njob[0] += 1
                            st_ = {}
                            sg_ = sgts[n % 2]

                            def pe(zs=zs, c=c, st_=st_):
                                if c == 0:
                                    slabref[0] = W.next(w_in_v[:, :, C_Z + g * 512 + zs * 256:C_Z + g * 512 + (zs + 1) * 256], 16, 256)
                                sl, wv = slabref[0]
                                pa = nbank()
                                st_["pa"] = pa
                                for kc in range(16):
                                    op("pe", lambda e, kc=kc: e.matmul(pa[0:q, 0:256], lhsT=hT[:, kc, c * q:(c + 1) * q], rhs=wv[:, kc, :],
                                                                       start=(kc == 0), stop=(kc == 15)), r=[sl, hT], w=[pa])

                            def pa_(st_=st_, sg_=sg_):
                                pa = st_["pa"]
                                sig_chain(sg_, sg_[0:q, 0:256], pa, pa[0:q, 0:256])

                            def pb_(zs=zs, c=c, st_=st_, sg_=sg_):
                                pa = st_["pa"]
                                op("dve", lambda e: e.tensor_tensor(out=szs[0:q, c, zs * 256:(zs + 1) * 256], in0=sg_[0:q, 0:256], in1=pa[0:q, 0:256],
                                                                    op=ALU.mult), r=[sg_, pa], w=[szs])

                            step((pe, pa_, pb_, None))
                            yield
                for _ in range(3):
                    step(None)
                    yield

            def pieceA(g, c):
                si = g % 2
                xb = xbcTs[si]
                k = g * nch + c
                xbt, xd, xs2, cb2, sr2 = xbtoks[k % 3], xdts[k % 2], xss[k % 2], cbms[k % 2], segrs[k % 2]
                tk = slice(c * q, (c + 1) * q)
                hs = slice(g * 8, g * 8 + 8)
                S = Sst[g]
                if c == 0 and not smp:
                    if first and warm:
                        op("dve", lambda e: e.memset(S[:], 0.0), w=[S])
                    if first and kind == "main":
                        op("dve", lambda e: e.tensor_scalar(out=S[:], in0=S[:], scalar1=cst[:, K_ROLE:K_ROLE + 1], scalar2=None,
                                                            op0=ALU.mult), r=[S, cst], w=[S])
                for j in range(5):
                    op("pe", lambda e, j=j: e.transpose(out=tp_ap[0:q, j, :], in_=xb[:, j, tk], identity=identb[:]),
                       r=[xb, identb], w=[Mb])
                op("act", lambda e: e.activation(out=xbt[0:q, :].rearrange("p (a b) -> p a b", b=128), in_=tp_ap[0:q, 0:5, :], func=AF.Copy),
                   r=[Mb], w=[xbt])
                xv = xbt[0:q, 0:512].rearrange("p (r d) -> p r d", d=HP)
                op("dve", lambda e: e.tensor_tensor(out=xd[0:q, :].rearrange("p (r d) -> p r d", d=HP), in0=xv,
                                                    in1=bcl(dtt[0:q, c, hs], HP), op=ALU.mult), r=[xbt, dtt], w=[xd])
                op("dve", lambda e: e.tensor_tensor(out=xs2[0:q, :].rearrange("p (r d) -> p r d", d=HP),
                                                    in0=xd[0:q, :].rearrange("p (r d) -> p r d", d=HP),
                                                    in1=bcl(toend[0:q, c, hs], HP), op=ALU.mult), r=[xd, toend], w=[xs2])
                if not warm:
                    op("pe", lambda e: e.matmul(Cb[0:q, 0:q], lhsT=xb[:, 4, tk], rhs=xb[:, 5, tk], start=True, stop=True), r=[xb], w=[Cb])
                    op("dve", lambda e: e.tensor_tensor(out=cb2[0:q, 0:q], in0=Cb[0:q, 0:q], in1=cK(kTM, q, q), op=ALU.mult),
                       r=[Cb, cst], w=[cb2])
                    op("dve", lambda e: e.tensor_tensor(out=sr2[0:q, :, 0:q], in0=bcm(cK(kTM, q, q), 8),
                                                        in1=bcl(la[0:q, c, hs], q), op=ALU.mult), r=[cst, la], w=[sr2])

            def pieceB(g, c):
                if warm:
                    return
                k = g * nch + c
                cb2, sr2, w2 = cbms[k % 2], segrs[k % 2], wTs[k % 2]
                for r in range(8):
                    sgb = Sg[r // 4]
                    op("pe", lambda e, r=r, sgb=sgb: e.matmul(sgb[0:q, (r % 4) * 128:(r % 4) * 128 + q], lhsT=cK(kUM, q, q),
                                                              rhs=sr2[0:q, r, 0:q], start=True, stop=True), r=[cst, sr2], w=[sgb])
                for h2 in range(2):
                    sgb = Sg[h2]
                    op("act", lambda e, h2=h2, sgb=sgb: e.activation(
                        out=dec[0:q, h2 * 4:(h2 + 1) * 4, 0:q], in_=sgb[0:q, :].rearrange("p (a b) -> p a b", b=128)[:, :, 0:q],
                        func=AF.Exp), r=[sgb], w=[dec])
                op("dve", lambda e: e.tensor_tensor(out=w2[0:q, :, 0:q], in0=dec[0:q, :, 0:q], in1=bcm(cb2[0:q, 0:q], 8), op=ALU.mult),
                   r=[dec, cb2], w=[w2])

            def pieceC(g, c):
                si = g % 2
                xb = xbcTs[si]
                szs = szS[si]
                k = g * nch + c
                xbt, xd, xs2, w2, gn2 = xbtoks[k % 3], xdts[k % 2], xss[k % 2], wTs[k % 2], gns[k % 2]
                tk = slice(c * q, (c + 1) * q)
                hs = slice(g * 8, g * 8 + 8)
                S = Sst[g]
                xv = xbt[0:q, 0:512].rearrange("p (r d) -> p r d", d=HP)
                if not warm:
                    for r in range(8):
                        op("pe", lambda e, r=r: e.matmul(Yb[0:q, r * HP:(r + 1) * HP], lhsT=w2[0:q, r, 0:q], rhs=xd[0:q, r * HP:(r + 1) * HP],
                                                         start=True, stop=True), r=[w2, xd], w=[Yb])
                    if not smp:
                        if c == 0 and kind == "main":
                            op("act", lambda e: e.activation(out=Sbf[:], in_=S[:], func=AF.Copy), r=[S], w=[Sbf])
                        op("pe", lambda e: e.matmul(YOb[0:q, :], lhsT=xb[:, 5, tk], rhs=Sbf[:], start=True, stop=True),
                           r=[xb, Sbf], w=[YOb])
                if not smp:
                    op("pe", lambda e: e.matmul(STb[:, :], lhsT=xbt[0:q, 512:640], rhs=xs2[0:q, :], start=True, stop=True),
                       r=[xbt, xs2], w=[STb])
                    Sv = S[:].rearrange("p (r d) -> p r d", d=HP)
                    op("dve", lambda e: e.tensor_tensor(out=Sv, in0=Sv, in1=bcl(cdb[:, c, hs], HP), op=ALU.mult), r=[S, cdb], w=[S])
                    op("dve", lambda e: e.tensor_tensor(out=S[:], in0=S[:], in1=STb[:, :], op=ALU.add), r=[S, STb], w=[S])
                    need_sbf = (not warm) and c + 1 < nch
                else:
                    op("act", lambda e: e.activation(out=ydg[:, :], in_=Yb[:, :], func=AF.Copy), r=[Yb], w=[ydg])
                    ctsrc = xb[:, 5, 0:128].rearrange("p (s t) -> p s t", t=DSEQ)
                    base = CTm.t[:, 0, 0:DSEQ]
                    ctdst = bass.AP(base.tensor, base.offset, [list(base.ap[0]), [128 + DSEQ, NSEQ], [1, DSEQ]])
                    op("dve", lambda e: e.tensor_copy(out=ctdst, in_=ctsrc), r=[xb], w=[CTm])
                    op("dve", lambda e: e.tensor_tensor(out=Btm[:], in0=bcm(xbt[:, 512:640], NSEQ), in1=bcl(cst[:, K_BLK:K_BLK + 16], 128),
                                                        op=ALU.mult), r=[xbt, cst], w=[Btm])

                    def sD(s):
                        it = g * NSEQ + s
                        so = sst[it % 8]
                        srcv = cst_ssm[s, g * 8:(g + 1) * 8].rearrange("(q two) p n -> (two p) q n", two=2)
                        P.dma("sp", lambda e: e.dma_start(out=so[:], in_=srcv), w=[so])

                    def sA(s):
                        it = g * NSEQ + s
                        so, sbb, stt_, tpb = sst[it % 8], sstb[it % 2], stb[it % 2], Sg[it % 2]
                        tpv = tpb.t[:, 0:256].bitcast(BF16).rearrange("p (a b) -> p a b", b=128)
                        op("act", lambda e: e.activation(out=sbb[:], in_=so[:], func=AF.Copy), r=[so], w=[sbb])
                        for qq in range(4):
                            op("pe", lambda e, qq=qq: e.transpose(out=tpv[:, qq, :], in_=sbb[:, qq, :], identity=identb[:]),
                               r=[sbb, identb], w=[tpb])
                        op("act", lambda e: e.activation(out=stt_[:].rearrange("p (a b) -> p a b", b=128), in_=tpv[:, 0:4, :], func=AF.Copy),
                           r=[tpb], w=[stt_])

                    def sB(s):
                        it = g * NSEQ + s
                        so, stt_ = sst[it % 8], stb[it % 2]
                        stp = (STb, Yb)[it % 2]
                        dstv = o_ssms[s, g * 8:(g + 1) * 8].rearrange("(q two) p n -> (two p) q n", two=2)
                        op("pe", lambda e: e.matmul(YOb[:, :], lhsT=CTm[:, s, :], rhs=stt_[:], start=(s == 0), stop=(s == NSEQ - 1)),
                           r=[CTm, stt_], w=[YOb])
                        for qq in range(4):
                            op("pe", lambda e, qq=qq: e.matmul(stp[:, qq * 128:(qq + 1) * 128], lhsT=xs2[:, qq * 128:(qq + 1) * 128],
                                                               rhs=Btm[:, s, :], start=True, stop=True), r=[xs2, Btm], w=[stp])
                        op("dve", lambda e: e.tensor_tensor(out=so[:], in0=so[:], in1=bcl(cds[:, s, g * 4:(g + 1) * 4], 128),
                                                            op=ALU.mult), r=[so, cds], w=[so])
                        op("dve", lambda e: e.tensor_tensor(out=so[:], in0=so[:], in1=stp[:, :].rearrange("p (a b) -> p a b", b=128),
                                                            op=ALU.add), r=[so, stp], w=[so])
                        P.dma("sp", lambda e: e.dma_start(out=dstv, in_=so[:]), r=[so], w=[so])

                    for s0 in range(6):
                        sD(s0)
                    sA(0)
                    for s in range(NSEQ):
                        if s + 6 < NSEQ:
                            sD(s + 6)
                        if s + 1 < NSEQ:
                            sA(s + 1)
                        sB(s)
                        if s % 2 == 1:
                            yield
                if not warm:
                    t1v = t1[0:q, :].rearrange("p (r d) -> p r d", d=HP)
                    op("dve", lambda e: e.tensor_tensor(out=t1v, in0=YOb[0:q, :].rearrange("p (r d) -> p r d", d=HP),
                                                        in1=bcl(eacum[0:q, c, hs], HP), op=ALU.mult), r=[YOb, eacum], w=[t1])
                    ysrc = ydg if smp else Yb
                    op("dve", lambda e: e.tensor_tensor(out=t1[0:q, :], in0=t1[0:q, :], in1=ysrc[0:q, :], op=ALU.add), r=[t1, ysrc], w=[t1])
                    op("dve", lambda e: e.tensor_tensor(out=t3[0:q, :].rearrange("p (r d) -> p r d", d=HP), in0=xv,
                                                        in1=bcl(bv[0:q, BV_DSK + g * 8:BV_DSK + g * 8 + 8], HP), op=ALU.mult),
                       r=[xbt, bv], w=[t3])
                    op("dve", lambda e: e.tensor_tensor(out=t1[0:q, :], in0=t1[0:q, :], in1=t3[0:q, :], op=ALU.add), r=[t1, t3], w=[t1])
                    op("dve", lambda e: e.tensor_tensor(out=t1[0:q, :], in0=t1[0:q, :], in1=szs[0:q, c, :], op=ALU.mult), r=[t1, szs], w=[t1])
                    yield "C2"
                    if (not smp) and need_sbf:
                        op("act", lambda e: e.activation(out=Sbf[:], in_=S[:], func=AF.Copy), r=[S], w=[Sbf])
                    s4 = rms_rstd(t1, q, 3, 512 ** -0.5, t3)
                    op("act", lambda e: e.activation(out=gn2[0:q, :], in_=t1[0:q, :], func=AF.Copy, scale=s4[0:q, 2:3]),
                       r=[t1, s4], w=[gn2])
                if last and kind == "main" and c == nch - 1:
                    P.dma("pool", lambda e: e.dma_start(out=o_ssmTp[:, g * 512:(g + 1) * 512], in_=S[:]), r=[S], w=[S])

            def pieceD(g, c):
                if warm:
                    return
                k = g * nch + c
                gn2 = gns[k % 2]
                tk = slice(c * q, (c + 1) * q)
                for j in range(4):
                    op("pe", lambda e, j=j: e.transpose(out=tp_ap[:, j, 0:q], in_=gn2[0:q, j * 128:(j + 1) * 128], identity=identb[0:q, 0:q]),
                       r=[gn2, identb], w=[Mb])
                op("dve", lambda e: e.tensor_tensor(out=ynT[:, g * 4:(g + 1) * 4, tk], in0=tp_ap[:, 0:4, 0:q],
                                                    in1=bcl(pv[:, PV_SNW + g * 4:PV_SNW + (g + 1) * 4], q), op=ALU.mult),
                   r=[Mb, pv], w=[ynT])

            if smp:
                for al in sst_extra:
                    if xio[2].res.w is not None:
                        al.res.r.append(xio[2].res.w)
                    al.res.r.extend(xio[2].res.r)
            for al in xio1_alias:
                if xio[1].res.w is not None:
                    al.res.r.append(xio[1].res.w)
                al.res.r.extend(xio[1].res.r)
            its = [(g, c) for g in range(NG) for c in range(nch)]
            NIT = len(its)
            s1 = {0: stage1(0, 0)}
            for _ in s1[0]:
                pass
            s1done = {0}

            def unit(gcur, n=1, flush_g=None):
                gg = flush_g if flush_g is not None else gcur + 1
                if gg >= NG or gg in s1done:
                    return
                if gg not in s1:
                    s1[gg] = stage1(gg, gg % 2)
                cnt = 0
                while flush_g is not None or cnt < n:
                    try:
                        next(s1[gg])
                    except StopIteration:
                        s1done.add(gg)
                        return
                    cnt += 1

            def drive(gen_or_none, gcur):
                if gen_or_none is None:
                    return
                for _ in gen_or_none:
                    unit(gcur)

            pieceA(*its[0])
            for t in range(NIT):
                g, c = its[t]
                if t + 1 < NIT:
                    gn_, cn_ = its[t + 1]
                    if gn_ != g:
                        unit(g, flush_g=gn_)
                    pieceA(gn_, cn_)
                unit(g)
                pieceB(g, c)
                unit(g)
                if t >= 1:
                    pieceD(*its[t - 1])
                    unit(g)
                drive(pieceC(g, c), g)
                unit(g)
            pieceD(*its[NIT - 1])
            if smp:
                for al in sst_extra:
                    if al.res.w is not None:
                        xio[2].res.r.append(al.res.w)
                    xio[2].res.r.extend(al.res.r)
            for al in xio1_alias:
                if al.res.w is not None:
                    xio[1].res.r.append(al.res.w)
                xio[1].res.r.extend(al.res.r)
            if warm:
                return
            banks[:] = [A[0], A[1], Sg[0], Sg[1], STb]
            for m2 in range(8):
                sl2, wv2 = W.next(w_in_v[:, :, C_GA + m2 * 256:C_GA + (m2 + 1) * 256], 16, 256)
                for hf in range(2):
                    p1 = nbank()
                    for kc in range(16):
                        op("pe", lambda e, kc=kc: e.matmul(p1[:, 0:T], lhsT=wv2[:, kc, hf * 128:(hf + 1) * 128], rhs=hT[:, kc, 0:T],
                                                           start=(kc == 0), stop=(kc == 15)), r=[sl2, hT], w=[p1])
                    sig_chain(sgts[hf], sgts[hf][:, 0:T], p1, p1[:, 0:T])
                for hf in range(2):
                    m = m2 * 2 + hf
                    sl1, wv1 = W.next(w_so_v[:, :, m * 128:(m + 1) * 128], 32, 128)
                    p0 = nbank()
                    for kc in range(32):
                        op("pe", lambda e, kc=kc: e.matmul(p0[:, 0:T], lhsT=wv1[:, kc, :], rhs=ynT[:, kc, 0:T], start=(kc == 0), stop=(kc == 31)),
                           r=[sl1, ynT], w=[p0])
                    op("dve", lambda e: e.tensor_tensor(out=mrg[:, m, 0:T], in0=sgts[hf][:, 0:T], in1=p0[:, 0:T], op=ALU.mult),
                       r=[sgts[hf], p0], w=[mrg])
            urow = xio[2]
            for j2 in range(8):
                c = nch - 1
                slc, wvc = W.next(w_in_v[:, :, C_SC + j2 * 256:C_SC + (j2 + 1) * 256], 16, 256)
                for hf in range(2):
                    pa = inproj_fm(wvc, slc, hf * 128)
                    op("act", lambda e, pa=pa, hf=hf: e.activation(out=tcc[hf][:, 0:T], in_=pa[:, 0:T], func=AF.Copy), r=[pa], w=[tcc[hf]])
                if emit_rows:
                    for kc in range(16):
                        op("pe", lambda e, kc=kc: e.matmul(Yb[0:q, 0:256], lhsT=hT[:, kc, c * q:(c + 1) * q], rhs=wvc[:, kc, :],
                                                           start=(kc == 0), stop=(kc == 15)), r=[slc, hT], w=[Yb])
                    op("act", lambda e: e.activation(out=tcr[0:q, :], in_=Yb[0:q, 0:256], func=AF.Copy), r=[Yb], w=[tcr])
                slh, wvh = W.next(w_in_v[:, :, C_SH + j2 * 256:C_SH + (j2 + 1) * 256], 16, 256)
                for hf in range(2):
                    j = j2 * 2 + hf
                    ru = rawu[hf]
                    rawuv = rawuvs[hf]
                    pa = inproj_fm(wvh, slh, hf * 128)
                    if smp:
                        op("pe", lambda e, j=j: e.transpose(out=tp_ap[:, 0, 0:32], in_=scst[0:32, j * 128:(j + 1) * 128], identity=identb[0:32, 0:32]),
                           r=[scst, identb], w=[Mb])
                        op("dve", lambda e: e.tensor_copy(out=rawuv[:, :, 0:2], in_=tp_ap[:, 0, 0:32].rearrange("p (s k) -> p s k", k=2)),
                           r=[Mb], w=[ru])
                    elif first:
                        op("dve", lambda e: e.memset(rawuv[:, :, 0:2], 0.0), w=[ru])
                    else:
                        op("dve", lambda e, j=j: e.tensor_copy(out=rawuv[:, 0, 0:2], in_=caru[:, j, :]), r=[caru], w=[ru])
                    op("dve", lambda e, pa=pa: e.tensor_tensor(out=rawuv[:, :, 2:2 + L], in0=v3(tcc[hf][:, 0:T]), in1=v3(pa[:, 0:T]), op=ALU.mult),
                       r=[tcc[hf], pa], w=[ru])
                    if not smp:
                        op("dve", lambda e, j=j: e.tensor_copy(out=caru[:, j, :], in_=rawuv[:, 0, L:L + 2]), r=[ru], w=[caru])
                    ac = acc[hf]
                    acv = v3(ac[:, 0:T])
                    op("dve", lambda e, j=j, acv=acv: e.tensor_scalar(out=acv, in0=rawuv[:, :, 0:L], scalar1=pv[:, PV_SCW + j * 3:PV_SCW + j * 3 + 1],
                                                                      scalar2=None, op0=ALU.mult), r=[ru, pv], w=[ac])
                    for k in range(1, 3):
                        op("dve", lambda e, j=j, acv=acv, k=k: e.scalar_tensor_tensor(
                            out=acv, in0=rawuv[:, :, k:k + L], scalar=pv[:, PV_SCW + j * 3 + k:PV_SCW + j * 3 + k + 1], in1=acv,
                            op0=ALU.mult, op1=ALU.add), r=[ru, pv, ac], w=[ac])
                if emit_rows:
                    for kc in range(16):
                        op("pe", lambda e, kc=kc: e.matmul(YOb[0:q, 0:256], lhsT=hT[:, kc, c * q:(c + 1) * q], rhs=wvh[:, kc, :],
                                                           start=(kc == 0), stop=(kc == 15)), r=[slh, hT], w=[YOb])
                    op("dve", lambda e: e.tensor_tensor(out=urow[0:q, j2 * 256:(j2 + 1) * 256], in0=tcr[0:q, :], in1=YOb[0:q, 0:256], op=ALU.mult),
                       r=[tcr, YOb], w=[urow])
                slz, wvz = W.next(w_in_v[:, :, C_SZ + j2 * 256:C_SZ + (j2 + 1) * 256], 16, 256)
                for hf in range(2):
                    pa = inproj_fm(wvz, slz, hf * 128)
                    sig_chain(szz[hf], szz[hf][:, 0:T], pa, pa[:, 0:T])
                    op("dve", lambda e, pa=pa: e.tensor_tensor(out=szz[hf][:, 0:T], in0=szz[hf][:, 0:T], in1=pa[:, 0:T], op=ALU.mult),
                       r=[szz[hf], pa], w=[szz[hf]])
                slb, wvb = W.next(w_in_v[:, :, C_SB + j2 * 256:C_SB + (j2 + 1) * 256], 16, 256)
                for hf in range(2):
                    j = j2 * 2 + hf
                    ac = acc[hf]
                    pa = inproj_fm(wvb, slb, hf * 128)
                    op("dve", lambda e, pa=pa, ac=ac: e.tensor_tensor(out=ac[:, 0:T], in0=ac[:, 0:T], in1=pa[:, 0:T], op=ALU.mult), r=[ac, pa], w=[ac])
                    op("dve", lambda e, j=j, ac=ac: e.tensor_tensor(out=ynT[:, j, 0:T], in0=ac[:, 0:T], in1=szz[hf][:, 0:T], op=ALU.mult),
                       r=[ac, szz[hf]], w=[ynT])
            if emit_rows:
                od = o_sconvs if smp else o_sconvp
                P.dma("pool", lambda e: e.dma_start(out=od[0:q, :], in_=urow[0:q, :]), r=[urow], w=[urow])
            for m2 in range(8):
                sl2, wv2 = W.next(w_in_v[:, :, C_GB + m2 * 256:C_GB + (m2 + 1) * 256], 16, 256)
                for hf in range(2):
                    p1 = nbank()
                    for kc in range(16):
                        op("pe", lambda e, kc=kc: e.matmul(p1[:, 0:T], lhsT=wv2[:, kc, hf * 128:(hf + 1) * 128], rhs=hT[:, kc, 0:T],
                                                           start=(kc == 0), stop=(kc == 15)), r=[sl2, hT], w=[p1])
                    sig_chain(sgts[hf], sgts[hf][:, 0:T], p1, p1[:, 0:T])
                sl1, wv1 = W.next(w_sc_v[:, :, m2 * 256:(m2 + 1) * 256], 16, 256)
                for hf in range(2):
                    m = m2 * 2 + hf
                    p0 = nbank()
                    for kc in range(16):
                        op("pe", lambda e, kc=kc: e.matmul(p0[:, 0:T], lhsT=wv1[:, kc, hf * 128:(hf + 1) * 128], rhs=ynT[:, kc, 0:T],
                                                           start=(kc == 0), stop=(kc == 15)), r=[sl1, ynT], w=[p0])
                    op("dve", lambda e: e.tensor_tensor(out=sgts[hf][:, 0:T], in0=sgts[hf][:, 0:T], in1=p0[:, 0:T], op=ALU.mult),
                       r=[sgts[hf], p0], w=[sgts[hf]])
                    op("dve", lambda e: e.tensor_tensor(out=mrg[:, m, 0:T], in0=mrg[:, m, 0:T], in1=sgts[hf][:, 0:T], op=ALU.add),
                       r=[mrg, sgts[hf]], w=[mrg])
            for c in range(nch):
                xt = xio[c]
                P.dma("pool", lambda e, xt=xt, c=c: e.dma_start(out=xt[0:q, :], in_=xsrc[c * q:(c + 1) * q, :]), w=[xt])
            for s8 in range(8):
                sl, wv = W.next(w_o_v[:, :, s8 * 256:(s8 + 1) * 256], 16, 256)
                for c in range(nch):
                    pa = nbank()
                    for kc in range(16):
                        op("pe", lambda e, kc=kc, c=c, pa=pa, wv=wv: e.matmul(pa[0:q, 0:256], lhsT=mrg[:, kc, c * q:(c + 1) * q], rhs=wv[:, kc, :],
                                                                              start=(kc == 0), stop=(kc == 15)), r=[sl, mrg], w=[pa])
                    xt = xio[c]
                    op("dve", lambda e, xt=xt, pa=pa, s8=s8: e.tensor_tensor(out=xt[0:q, s8 * 256:(s8 + 1) * 256], in0=xt[0:q, s8 * 256:(s8 + 1) * 256],
                                                                             in1=pa[0:q, 0:256], op=ALU.add), r=[xt, pa], w=[xt])
            for c in range(nch):
                xt = xio[c]
                s = rms_rstd(xt, q, c, D ** -0.5, hb)
                op("dve", lambda e, xt=xt, s=s: e.scalar_tensor_tensor(out=xt[0:q, :], in0=xt[0:q, :], scalar=s[0:q, 2:3], in1=bv[0:q, BV_FNW:BV_FNW + D],
                                                                       op0=ALU.mult, op1=ALU.mult), r=[xt, s, bv], w=[xt])
                P.dma("pool", lambda e, xt=xt, c=c: e.dma_start(out=ysink[c * q:(c + 1) * q, :], in_=xt[0:q, :]), r=[xt], w=[xt])

        def emit_all():
            setup()
            for b in range(NBLK):
                block(xw_warm[b * TMAX:(b + 1) * TMAX, :], BLK, Q, "warm", b == 0, b == NBLK - 1, None)
            for b in range(NBLK):
                block(xw_main[b * TMAX:(b + 1) * TMAX, :], BLK, Q, "main", b == 0, b == NBLK - 1, o_yp[b * TMAX:(b + 1) * TMAX, :])
            block(xsmp, 1, 128, "sample", False, False, o_ys)

        P.dry = True
        emit_all()
        P.dry = False
        W.start()
        emit_all()
        allt = xio + sst + crow + Sst
        P.finish("pool", allt)
        P.finish("sp", W.scr_res + sst)
        P.emit()
    return nc


_CACHE = {}


def _consts():
    k = np.arange(128)
    c = np.zeros((128, NCST), np.float32)
    c[:, K_ID:K_ID + 128] = np.eye(128)
    c[:, K_TM:K_TM + 128] = (k[:, None] <= k[None, :])
    c[:, K_UM:K_UM + 128] = (k[:, None] > k[None, :])
    same = (k[:, None] // DSEQ) == (k[None, :] // DSEQ)
    c[:, K_TMS:K_TMS + 128] = (k[:, None] <= k[None, :]) & same
    c[:, K_UMS:K_UMS + 128] = (k[:, None] > k[None, :]) & same
    c[:, K_ONE:K_ONE + 128] = 1.0
    c[:, K_EM:K_EM + 128] = (k[None, :] < 64)
    c[:, K_OM:K_OM + 128] = (k[None, :] >= 64)
    c[:, K_BLK:K_BLK + 16] = (k[:, None] // DSEQ) == np.arange(16)[None, :]
    return c


def kernel(x_prompt, x_sample, state_ssd_conv, state_ssm, state_sconv, meta_tokens, norm_w, w_in, ssd_conv_w,
           ssd_conv_b, dt_bias, a_log, d_skip, ssd_norm_w, w_ssd_out, sconv_w, w_sconv_out, w_o, final_norm_w):
    f = lambda a: np.ascontiguousarray(np.asarray(a, dtype=np.float32))
    x_prompt, x_sample = f(x_prompt), f(x_sample)
    if "nc" not in _CACHE:
        _CACHE["nc"] = build_program()
    nc = _CACHE["nc"]
    pvec = np.zeros((128, NPV), np.float32)
    cw = f(ssd_conv_w)[0]
    pvec[:, PV_CW:PV_CW + 192] = cw.reshape(4, 48, 128).transpose(2, 1, 0).reshape(128, 192)
    pvec[:, PV_CB:PV_CB + 48] = f(ssd_conv_b)[0].reshape(48, 128).T
    pvec[:, PV_SCW:PV_SCW + 48] = f(sconv_w)[0].reshape(3, 16, 128).transpose(2, 1, 0).reshape(128, 48)
    pvec[:, PV_SNW:PV_SNW + 32] = f(ssd_norm_w)[0].reshape(32, 128).T
    pvec[:, PV_NW:PV_NW + 16] = f(norm_w)[0].reshape(16, 128).T
    bvec = np.zeros((1, NBV), np.float32)
    bvec[0, BV_DTB:BV_DTB + 64] = f(dt_bias)[0]
    bvec[0, BV_ALOG:BV_ALOG + 64] = f(a_log)[0]
    bvec[0, BV_DSK:BV_DSK + 64] = f(d_skip)[0]
    bvec[0, BV_FNW:BV_FNW + D] = f(final_norm_w)
    cbase = _consts()
    w_in0, w_so0, w_sc0, w_o0 = f(w_in)[0], f(w_ssd_out)[0], f(w_sconv_out)[0], f(w_o)[0]
    meta = f(meta_tokens)
    zeros3 = np.zeros((3, D), np.float32)
    in_maps = []
    for c in range(NCORES):
        b, half = c // 2, c % 2
        full = np.concatenate([zeros3, meta, x_prompt[b]], axis=0)
        win0 = full[0:WIN]
        win1 = full[HALF:HALF + WIN]
        cc = cbase.copy()
        cc[:, K_ROLE] = float(half)
        sl = slice(c * NSEQ, (c + 1) * NSEQ)
        in_maps.append({
            "xw_main": np.ascontiguousarray(win1 if half else win0),
            "xw_warm": np.ascontiguousarray(win0) if half else np.zeros((WIN, D), np.float32),
            "xsmp": np.ascontiguousarray(x_sample[sl].reshape(128, D)),
            "consts": cc, "pvec": pvec, "bvec": bvec,
            "cst_conv": np.ascontiguousarray(f(state_ssd_conv)[0, sl].reshape(NSEQ * 3, CONVD)),
            "cst_ssm": np.ascontiguousarray(f(state_ssm)[0, sl]),
            "cst_sconv": np.ascontiguousarray(f(state_sconv)[0, sl].reshape(NSEQ * 2, D)),
            "w_in": w_in0, "w_ssd_out": w_so0, "w_sconv_out": w_sc0, "w_o": w_o0,
        })
    res = run_bass_kernel_spmd(nc, in_maps, core_ids=list(range(NCORES)))
    R = res.results
    nb = x_prompt.shape[0]
    y_prompt = np.zeros((nb, SEQ, D), np.float32)
    conv_p = np.zeros((1, nb, 3, CONVD), np.float32)
    ssm_p = np.zeros((1, nb, NH, HP, NS), np.float32)
    sconv_p = np.zeros((1, nb, 2, D), np.float32)
    for b in range(nb):
        y_prompt[b, 0:HALF - META] = R[2 * b]["o_yp"][3 + META:WIN]
        y_prompt[b, HALF - META:] = R[2 * b + 1]["o_yp"][3:WIN]
        conv_p[0, b] = R[2 * b + 1]["o_convp"][Q - 3:Q]
        ssm_p[0, b] = R[2 * b + 1]["o_ssmTp"].T.reshape(NH, HP, NS)
        sconv_p[0, b] = R[2 * b + 1]["o_sconvp"][Q - 2:Q]
    y_sample = np.concatenate([R[c]["o_ys"].reshape(NSEQ, DSEQ, D) for c in range(NCORES)], axis=0)
    conv_s = np.concatenate([R[c]["o_convs"].reshape(NSEQ, DSEQ, CONVD)[:, DSEQ - 3:] for c in range(NCORES)], axis=0)[None]
    ssm_s = np.concatenate([R[c]["o_ssms"] for c in range(NCORES)], axis=0)[None]
    sconv_s = np.concatenate([R[c]["o_sconvs"].reshape(NSEQ, DSEQ, D)[:, DSEQ - 2:] for c in range(NCORES)], axis=0)[None]
    return (y_prompt, y_sample, np.ascontiguousarray(conv_p), np.ascontiguousarray(ssm_p), np.ascontiguousarray(sconv_p),
            np.ascontiguousarray(conv_s), np.ascontiguousarray(ssm_s), np.ascontiguousarray(sconv_s))
```

```python
import numpy as np
from contextlib import ExitStack
import concourse.bass as bass
import concourse.mybir as mybir
from concourse.bass_utils import run_bass_kernel_spmd

F32 = mybir.dt.float32
BF16 = mybir.dt.bfloat16
ALU = mybir.AluOpType
AF = mybir.ActivationFunctionType

D = 2048
DI = 4096
NH = 64
HP = 64
NS = 128
NG = 8
CONVD = 6144
PROJ = 22592
EPS = 1e-6
META = 16
SEQ = 2048
NCORES = 8
Q = 115
NCHW = 9
WIN = Q * NCHW
HALF = 1032
BLK = 3
NBLK = NCHW // BLK
TMAX = BLK * Q
NSEQ = 16
DSEQ = 8
C_Z = 0
C_X = DI
C_B = DI + DI
C_C = DI + DI + 1024
C_DT = DI + CONVD
C_SB = C_DT + NH
C_SC = C_SB + D
C_SH = C_SC + D
C_SZ = C_SH + D
C_GA = C_SZ + D
C_GB = C_GA + D
K_ID, K_TM, K_UM, K_TMS, K_UMS, K_ONE, K_EM, K_OM, K_BLK, K_ROLE = 0, 128, 256, 384, 512, 640, 768, 896, 1024, 1040
NCST = 1041
PV_CW, PV_CB, PV_SCW, PV_SNW, PV_NW = 0, 192, 240, 288, 320
NPV = 336
BV_DTB, BV_ALOG, BV_DSK, BV_FNW = 0, 64, 128, 192
NBV = 192 + D


class Res:
    __slots__ = ("w", "r")

    def __init__(self):
        self.w = None
        self.r = []


class TT:
    def __init__(self, t):
        self.t = t
        self.res = Res()

    def __getitem__(self, k):
        return self.t[k]


class _Rec:
    def __init__(self):
        self.call = None

    def __getattr__(self, name):
        def f(*a, **k):
            self.call = (name, a, k)
            return self
        return f


def _record(fn):
    r = _Rec()
    fn(r)
    name, a, k = r.call
    return lambda e: getattr(e, name)(*a, **k)


class Prog:
    ENG = ("pe", "act", "dve", "pool", "sp")

    def __init__(self, nc, stack, n_dma_sems=56):
        self.nc = nc
        self.dry = False
        self.sem = {e: stack.enter_context(nc.semaphore("c_" + e)) for e in self.ENG}
        self.dsem = [stack.enter_context(nc.semaphore("d%d" % i)) for i in range(n_dma_sems)]
        self.dpool = {"pool": list(range(0, 32)), "sp": list(range(32, n_dma_sems))}
        self.reset()

    def reset(self):
        self.q = {e: [] for e in self.ENG}
        self.cnt = {e: 0 for e in self.ENG}
        self.seen = {e: {} for e in self.ENG}
        self.dval = [0] * len(self.dsem)
        self.dnext = {"pool": 0, "sp": 0}

    def _deps(self, reads, writes):
        deps = []
        for r in reads:
            if r.res.w is not None:
                deps.append(r.res.w)
        for w in writes:
            if w.res.w is not None:
                deps.append(w.res.w)
            deps.extend(w.res.r)
        return deps

    def _waits(self, eng, deps, skip_self):
        need = {}
        seen = self.seen[eng]
        for (k, v) in deps:
            if skip_self and k == eng:
                continue
            if seen.get(k, 0) >= v:
                continue
            if need.get(k, 0) < v:
                need[k] = v
        for k, v in need.items():
            seen[k] = v
        return list(need.items())

    def _mark(self, tok, reads, writes):
        for r in reads:
            r.res.r.append(tok)
        for w in writes:
            w.res.w = tok
            w.res.r = []

    def op(self, eng, fn, r=(), w=()):
        if self.dry:
            return
        waits = self._waits(eng, self._deps(r, w), eng == "pe")
        self.cnt[eng] += 1
        tok = (eng, self.cnt[eng])
        self.q[eng].append((waits, _record(fn), (eng, 1)))
        self._mark(tok, r, w)

    def dma(self, eng, fn, r=(), w=()):
        if self.dry:
            return
        deps = self._deps(r, w)
        pool = self.dpool[eng]
        i = pool[self.dnext[eng] % len(pool)]
        self.dnext[eng] += 1
        if self.dval[i] > 0:
            deps.append((i, self.dval[i]))
        waits = self._waits(eng, deps, False)
        self.dval[i] += 16
        tok = (i, self.dval[i])
        self.q[eng].append((waits, _record(fn), (i, 16)))
        self._mark(tok, r, w)

    def finish(self, eng, tts):
        deps = []
        for t in tts:
            if t.res.w is not None:
                deps.append(t.res.w)
            deps.extend(t.res.r)
        self.q[eng].append((self._waits(eng, deps, False), None, None))

    def _semobj(self, k):
        return self.sem[k] if isinstance(k, str) else self.dsem[k]

    def emit(self):
        import bisect
        sig = {e: set() for e in self.ENG}
        for name in self.ENG:
            for waits, fn, inc in self.q[name]:
                for k, v in waits:
                    if isinstance(k, str):
                        sig[k].add(v)
        sigl = {e: sorted(sig[e]) for e in self.ENG}

        def remap(k, v):
            if isinstance(k, str):
                return bisect.bisect_right(sigl[k], v)
            return v

        def run(name):
            def body(e):
                n = 0
                for waits, fn, inc in self.q[name]:
                    for k, v in waits:
                        e.wait_ge(self._semobj(k), remap(k, v))
                    if fn is not None:
                        ins = fn(e)
                        if isinstance(inc[0], str):
                            n += 1
                            if n in sig[name]:
                                ins.then_inc(self._semobj(inc[0]), 1)
                        else:
                            ins.then_inc(self._semobj(inc[0]), inc[1])
            return body
        with self.nc.Block() as block:
            block.tensor(run("pe"))
            block.scalar(run("act"))
            block.vector(run("dve"))
            block.gpsimd(run("pool"))
            block.sync(run("sp"))


class WStream:
    def __init__(self, P, slots, depth, nc):
        self.P = P
        self.slots = slots
        self.depth = depth
        self.nc = nc
        self.specs = []
        self.i = 0
        self.issued = 0

    def start(self):
        self.i = 0
        self.issued = 0
        keys = {}
        self.kidx = []
        self.firstuse = []
        for (src, k, n) in self.specs:
            key = (src.name, str(src.offset), k, n)
            self.firstuse.append(key not in keys)
            if key not in keys:
                keys[key] = len(keys)
            self.kidx.append(keys[key])
        self.scr = self.nc.dram_tensor("wscr", [len(keys), 128, 4096], BF16).ap()
        self.scr_res = [TT(None) for _ in keys]
        self.conv_ptr = 0

    def _view(self, slot, kcs, ncols):
        return slot.t[:, 0:kcs * ncols].rearrange("p (k n) -> p k n", n=ncols)

    def next(self, src, kcs, ncols):
        if self.P.dry:
            self.specs.append((src, kcs, ncols))
            return self.slots[0], self._view(self.slots[0], kcs, ncols)
        i = self.i
        assert self.specs[i][1:] == (kcs, ncols)
        while self.issued < len(self.specs) and self.issued <= i + self.depth:
            j = self.issued
            s, k, n = self.specs[j]
            sl = self.slots[j % len(self.slots)]
            ki = self.kidx[j]
            flat = sl.t[:, 0:k * n]
            if self.firstuse[j]:
                v = self._view(sl, k, n)
                self.P.dma("pool", lambda e: e.dma_start(out=v, in_=s), w=[sl])
                self.P.dma("sp", lambda e: e.dma_start(out=self.scr[ki, :, 0:k * n], in_=flat), r=[sl], w=[self.scr_res[ki]])
            else:
                self.P.dma("sp", lambda e: e.dma_start(out=flat, in_=self.scr[ki, :, 0:k * n]), r=[self.scr_res[ki]], w=[sl])
            self.issued += 1
        jc = max(self.conv_ptr, self.issued + 6)
        while jc < len(self.specs) and not self.firstuse[jc]:
            jc += 1
        if jc < len(self.specs):
            s2, k2, n2 = self.specs[jc]
            ki2 = self.kidx[jc]
            dst = self.scr[ki2, :, 0:k2 * n2].rearrange("p (k n) -> p k n", n=n2)
            self.P.dma("pool", lambda e: e.dma_start(out=dst, in_=s2), w=[self.scr_res[ki2]])
            self.firstuse[jc] = False
            self.conv_ptr = jc + 1
        self.i += 1
        sl = self.slots[i % len(self.slots)]
        return sl, self._view(sl, kcs, ncols)


def bcl(ap2, n):
    return ap2.unsqueeze(2).broadcast_to([ap2.shape[0], ap2.shape[1], n])


def bcm(ap2, n):
    return ap2.unsqueeze(1).broadcast_to([ap2.shape[0], n, ap2.shape[1]])


def build_program():
    nc = bass.Bass("TRN2", target_bir_lowering=False)
    dt_in = lambda n, s: nc.dram_tensor(n, s, F32, kind="ExternalInput").ap()
    dt_out = lambda n, s: nc.dram_tensor(n, s, F32, kind="ExternalOutput").ap()
    xw_main = dt_in("xw_main", [WIN, D])
    xw_warm = dt_in("xw_warm", [WIN, D])
    xsmp = dt_in("xsmp", [128, D])
    consts = dt_in("consts", [128, NCST])
    pvec = dt_in("pvec", [128, NPV])
    bvec = dt_in("bvec", [1, NBV])
    cst_conv = dt_in("cst_conv", [NSEQ * 3, CONVD])
    cst_ssm = dt_in("cst_ssm", [NSEQ, NH, HP, NS])
    cst_sconv = dt_in("cst_sconv", [NSEQ * 2, D])
    w_in = dt_in("w_in", [D, PROJ])
    w_ssd_out = dt_in("w_ssd_out", [DI, D])
    w_sconv_out = dt_in("w_sconv_out", [D, D])
    w_o = dt_in("w_o", [D, D])
    o_yp = dt_out("o_yp", [WIN, D])
    o_ys = dt_out("o_ys", [128, D])
    o_convp = dt_out("o_convp", [Q, CONVD])
    o_ssmTp = dt_out("o_ssmTp", [128, DI])
    o_sconvp = dt_out("o_sconvp", [Q, D])
    o_convs = dt_out("o_convs", [128, CONVD])
    o_ssms = dt_out("o_ssms", [NSEQ, NH, HP, NS])
    o_sconvs = dt_out("o_sconvs", [128, D])

    w_in_v = w_in.rearrange("(kc p) n -> p kc n", p=128)
    w_so_v = w_ssd_out.rearrange("(kc p) n -> p kc n", p=128)
    w_sc_v = w_sconv_out.rearrange("(kc p) n -> p kc n", p=128)
    w_o_v = w_o.rearrange("(kc p) n -> p kc n", p=128)

    with ExitStack() as st:
        P = Prog(nc, st)
        sb = lambda n, s, d: TT(st.enter_context(nc.sbuf_tensor(n, s, d)))
        psb = lambda n: TT(st.enter_context(nc.psum_tensor(n, [128, 512], F32)))
        cst = sb("cst", [128, NCST], F32)
        identb = sb("identb", [128, 128], BF16)
        pv = sb("pv", [128, NPV], F32)
        pvn = sb("pvn", [128, 48], F32)
        bv = sb("bv", [128, NBV], F32)
        negA = sb("negA", [128, NH], F32)
        hT = sb("hT", [128, 16, TMAX], BF16)
        ynT = sb("ynT", [128, 32, TMAX], BF16)
        mrg = sb("mrg", [128, 16, TMAX], BF16)
        xio = [sb("xio%d" % i, [128, D], F32) for i in range(3)]
        hb = sb("hb", [128, D], BF16)
        x1 = xio[1].t
        segr1 = TT(x1[:, 0:1024].rearrange("p (a b) -> p a b", b=128))
        wT1 = TT(x1[:, 1024:1536].bitcast(BF16).rearrange("p (a b) -> p a b", b=128))
        xdt1 = TT(x1[:, 1536:1792].bitcast(BF16))
        xs1 = TT(x1[:, 1792:2048].bitcast(BF16))
        xio1_alias = [segr1, wT1, xdt1, xs1]
        x2 = xio[2].t
        sst_extra = [TT(x2[:, i * 512:(i + 1) * 512].rearrange("p (a n) -> p a n", n=128)) for i in range(4)]
        stt = [sb("stt%d" % i, [128, 4], F32) for i in range(4)]
        wsl = [sb("wsl%d" % i, [128, 4096], BF16) for i in range(3)]
        dtt = sb("dtt", [128, BLK, NH], F32)
        dtmp = sb("dtmp", [128, NH], F32)
        la = sb("la", [128, BLK, NH], F32)
        toend = sb("toend", [128, BLK, NH], F32)
        eacum = sb("eacum", [128, BLK, NH], F32)
        cdb = sb("cdb", [128, BLK, NH], F32)
        raw = sb("raw", [128, 6, 3 + TMAX], BF16)
        acc = [sb("acc%d" % i, [128, TMAX], F32) for i in range(2)]
        xbcTs = [sb("xbcT%d" % i, [128, 6, TMAX], BF16) for i in range(2)]
        xbtoks = [sb("xbtok%d" % i, [128, 640], BF16) for i in range(3)]
        szS = [sb("sz%d" % i, [128, BLK, 512], BF16) for i in range(2)]
        carx = sb("carx", [128, 48, 3], BF16)
        caru = sb("caru", [128, 16, 2], F32)
        cbms = [sb("cbm%d" % i, [128, 128], BF16) for i in range(2)]
        segr0 = sb("segr0", [128, 8, 128], F32)
        dec = sb("dec", [128, 8, 128], BF16)
        wT0 = sb("wT0", [128, 8, 128], BF16)
        xdt0 = sb("xdt0", [128, 512], BF16)
        xs0 = sb("xs0", [128, 512], BF16)
        t1 = sb("t1", [128, 512], F32)
        t3 = sb("t3", [128, 512], F32)
        ydg = t3
        gns = [sb("gn%d" % i, [128, 512], BF16) for i in range(2)]
        segrs = [segr0, segr1]
        wTs = [wT0, wT1]
        xdts = [xdt0, xdt1]
        xss = [xs0, xs1]
        big = sb("big", [128, 4096], F32)
        Sst = [TT(big.t[:, g * 512:(g + 1) * 512]) for g in range(NG)]
        Sbf = sb("Sbf", [128, 512], BF16)
        sgts = [sb("sgt%d" % i, [128, TMAX], F32) for i in range(2)]
        tcc = [sb("tcc%d" % i, [128, TMAX], F32) for i in range(2)]
        szz = [sb("szz%d" % i, [128, TMAX], F32) for i in range(2)]
        rawu = [sb("rawu%d" % i, [128, 2 + TMAX], F32) for i in range(2)]
        crow = [sb("crow%d" % i, [128, 256], F32) for i in range(2)]
        tcr = sb("tcr", [128, 256], F32)
        cvst = sb("cvst", [48, 768], BF16)
        scst = sb("scst", [32, D], BF16)
        CTm = TT(big.t[:, 0:1024].bitcast(BF16).rearrange("p (s n) -> p s n", n=128))
        Btm = TT(big.t[:, 1024:2048].bitcast(BF16).rearrange("p (s n) -> p s n", n=128))
        Rv = TT(big.t[:, 2048:3072].rearrange("p (a n) -> p a n", n=512))
        cds = TT(big.t[:, 3072:3584].rearrange("p (s h) -> p s h", h=32))
        smp_alias = [CTm, Btm, Rv, cds]
        sst = [sb("sst%d" % i, [128, 4, 128], F32) for i in range(4)] + sst_extra
        sstb = [TT(big.t[:, 3584 + 256 * i:3584 + 256 * (i + 1)].bitcast(BF16).rearrange("p (a n) -> p a n", n=128)) for i in range(2)]
        smp_alias += sstb
        stb = [sb("stb%d" % i, [128, 512], BF16) for i in range(2)]
        A = [psb("pA0"), psb("pA1")]
        Sg = [psb("pS0"), psb("pS1")]
        Yb = psb("pY")
        YOb = psb("pYO")
        STb = psb("pST")
        Mb = psb("pM")
        tp_ap = Mb.t[:, 192:512].bitcast(BF16).rearrange("p (a b) -> p a b", b=128)
        Cb = Mb

        W = WStream(P, wsl, 2, nc)

        def op(eng, fn, r=(), w=()):
            P.op(eng, fn, r, w)

        def cK(k, n=128, rows=128):
            return cst[0:rows, k:k + n]

        def setup():
            P.dma("pool", lambda e: e.dma_start(out=cst[:], in_=consts), w=[cst])
            P.dma("pool", lambda e: e.dma_start(out=pv[:], in_=pvec), w=[pv])
            P.dma("pool", lambda e: e.dma_start(out=bv[:], in_=bvec.partition_broadcast(128)), w=[bv])
            op("dve", lambda e: e.tensor_copy(out=identb[:], in_=cst[:, K_ID:K_ID + 128]), r=[cst], w=[identb])
            op("dve", lambda e: e.tensor_scalar(out=pvn[:], in0=pv[:, PV_CB:PV_CB + 48], scalar1=-1.0, scalar2=None, op0=ALU.mult),
               r=[pv], w=[pvn])
            op("act", lambda e: e.activation(out=negA[:], in_=bv[:, BV_ALOG:BV_ALOG + NH], func=AF.Exp), r=[bv], w=[negA])
            op("dve", lambda e: e.tensor_scalar(out=negA[:], in0=negA[:], scalar1=-1.0, scalar2=None, op0=ALU.mult),
               r=[negA], w=[negA])

        def rms_rstd(src_tt, rows, sti, n_inv_sqrt, junk):
            s = stt[sti]
            op("act", lambda e: e.activation(out=junk[0:rows, :], in_=src_tt[0:rows, :], func=AF.Square,
                                             scale=float(n_inv_sqrt), accum_out=s[0:rows, 0:1]),
               r=[src_tt], w=[junk, s])
            op("act", lambda e: e.activation(out=s[0:rows, 1:2], in_=s[0:rows, 0:1], func=AF.Ln, bias=EPS, scale=1.0),
               r=[s], w=[s])
            op("act", lambda e: e.activation(out=s[0:rows, 2:3], in_=s[0:rows, 1:2], func=AF.Exp, scale=-0.5), r=[s], w=[s])
            return s

        def sig_chain(dst_tt, dst_ap, src_tt, src_ap, negb=None):
            if negb is None:
                op("act", lambda e: e.activation(out=dst_ap, in_=src_ap, func=AF.Exp, scale=-1.0), r=[src_tt], w=[dst_tt])
            else:
                op("act", lambda e: e.activation(out=dst_ap, in_=src_ap, func=AF.Exp, scale=-1.0, bias=negb), r=[src_tt, pvn], w=[dst_tt])
            op("act", lambda e: e.activation(out=dst_ap, in_=dst_ap, func=AF.Ln, bias=1.0, scale=1.0), r=[dst_tt], w=[dst_tt])
            op("act", lambda e: e.activation(out=dst_ap, in_=dst_ap, func=AF.Exp, scale=-1.0), r=[dst_tt], w=[dst_tt])

        def block(xsrc, nch, q, kind, first, last, ysink):
            T = nch * q
            smp = kind == "sample"
            warm = kind == "warm"
            nseq, L = (NSEQ, DSEQ) if smp else (1, T)
            kTM, kUM = (K_TMS, K_UMS) if smp else (K_TM, K_UM)
            emit_rows = last or smp
            rawv = raw.t[:, :, 0:nseq * (3 + L)].rearrange("p j (s l) -> p j s l", l=3 + L)
            rawuvs = [ru.t[:, 0:nseq * (2 + L)].rearrange("p (s l) -> p s l", l=2 + L) for ru in rawu]

            def v3(ap2):
                return ap2.rearrange("p (s l) -> p s l", l=L)

            for c in range(nch):
                xt = xio[c]
                P.dma("pool", lambda e, xt=xt, c=c: e.dma_start(out=xt[0:q, :], in_=xsrc[c * q:(c + 1) * q, :]), w=[xt])
                s = rms_rstd(xt, q, c, D ** -0.5, hb)
                op("dve", lambda e, xt=xt, s=s: e.tensor_scalar(out=hb[0:q, :], in0=xt[0:q, :], scalar1=s[0:q, 2:3],
                                                                scalar2=None, op0=ALU.mult), r=[xt, s], w=[hb])
                for g4 in range(4):
                    for j in range(4):
                        kc = g4 * 4 + j
                        op("pe", lambda e, kc=kc, j=j: e.transpose(out=tp_ap[:, j, 0:q], in_=hb[0:q, kc * 128:(kc + 1) * 128],
                                                                   identity=identb[0:q, 0:q]), r=[hb, identb], w=[Mb])
                    op("dve", lambda e, g4=g4, c=c: e.tensor_tensor(
                        out=hT[:, g4 * 4:(g4 + 1) * 4, c * q:(c + 1) * q], in0=tp_ap[:, 0:4, 0:q],
                        in1=bcl(pv[:, PV_NW + g4 * 4:PV_NW + (g4 + 1) * 4], q), op=ALU.mult), r=[Mb, pv], w=[hT])
            sl, wv = W.next(w_in_v[:, :, C_DT:C_DT + NH], 16, NH)
            for c in range(nch):
                for kc in range(16):
                    op("pe", lambda e, kc=kc, c=c, wv=wv: e.matmul(Mb[0:q, 128:192], lhsT=hT[:, kc, c * q:(c + 1) * q],
                                                                   rhs=wv[:, kc, :], start=(kc == 0), stop=(kc == 15)),
                       r=[hT, sl], w=[Mb])
                op("dve", lambda e: e.tensor_tensor(out=dtmp[0:q, :], in0=Mb[0:q, 128:192], in1=bv[0:q, BV_DTB:BV_DTB + NH],
                                                    op=ALU.add), r=[Mb, bv], w=[dtmp])
                op("act", lambda e: e.activation(out=dtmp[0:q, :], in_=dtmp[0:q, :], func=AF.Exp), r=[dtmp], w=[dtmp])
                op("act", lambda e, c=c: e.activation(out=dtt[0:q, c, :], in_=dtmp[0:q, :], func=AF.Ln, bias=1.0, scale=1.0),
                   r=[dtmp], w=[dtt])
                if first and c == 0:
                    op("dve", lambda e: e.memset(dtt[0:3, 0, :], 0.0), w=[dtt])
                op("dve", lambda e, c=c: e.tensor_tensor(out=la[0:q, c, :], in0=dtt[0:q, c, :], in1=negA[0:q, :], op=ALU.mult),
                   r=[dtt, negA], w=[la])
                pa = A[c % 2]
                op("pe", lambda e, c=c, pa=pa: e.matmul(pa[0:q, 0:64], lhsT=cK(kUM, q, q), rhs=la[0:q, c, :], start=True, stop=True),
                   r=[cst, la], w=[pa])
                op("pe", lambda e, c=c, pa=pa: e.matmul(pa[0:q, 64:128], lhsT=cK(kTM, q, q), rhs=la[0:q, c, :], start=True, stop=True),
                   r=[cst, la], w=[pa])
                op("pe", lambda e, c=c, pa=pa: e.matmul(pa[:, 128:192], lhsT=cK(K_ONE, 128, q), rhs=la[0:q, c, :], start=True, stop=True),
                   r=[cst, la], w=[pa])
                op("act", lambda e, c=c, pa=pa: e.activation(out=toend[0:q, c, :], in_=pa[0:q, 0:64], func=AF.Exp), r=[pa], w=[toend])
                op("act", lambda e, c=c, pa=pa: e.activation(out=eacum[0:q, c, :], in_=pa[0:q, 64:128], func=AF.Exp), r=[pa], w=[eacum])
                op("act", lambda e, c=c, pa=pa: e.activation(out=cdb[:, c, :], in_=pa[:, 128:192], func=AF.Exp), r=[pa], w=[cdb])

            if smp:
                for al in smp_alias:
                    for S_ in Sst:
                        if S_.res.w is not None:
                            al.res.r.append(S_.res.w)
                        al.res.r.extend(S_.res.r)
                op("dve", lambda e: e.memset(CTm[:], 0.0), w=[CTm])
                P.dma("pool", lambda e: e.dma_start(out=scst[:], in_=cst_sconv), w=[scst])
                lav = la.t[:, 0, :].rearrange("p (hh two) -> p hh two", two=2)
                for par in range(2):
                    op("dve", lambda e, par=par: e.tensor_tensor(
                        out=Rv.t[:, par, :].rearrange("p (s h) -> p s h", h=32), in0=bcl(cst[:, K_BLK:K_BLK + 16], 32),
                        in1=bcm(lav[:, :, par], 16), op=ALU.mult), r=[cst, la], w=[Rv])
                op("pe", lambda e: e.matmul(A[0][:, 0:512], lhsT=cK(K_EM), rhs=Rv[:, 0, :], start=True, stop=False), r=[cst, Rv], w=[A[0]])
                op("pe", lambda e: e.matmul(A[0][:, 0:512], lhsT=cK(K_OM), rhs=Rv[:, 1, :], start=False, stop=True), r=[cst, Rv], w=[A[0]])
                op("act", lambda e: e.activation(out=cds.t[:].rearrange("p s h -> p (s h)"), in_=A[0][:, 0:512], func=AF.Exp),
                   r=[A[0]], w=[cds])

            bank_i = [0]
            banks = [A[0], A[1]]
            slabref = [None]
            raws = [TT(raw.t[:, jj, :]) for jj in range(6)]

            def nbank():
                pa = banks[bank_i[0] % len(banks)]
                bank_i[0] += 1
                return pa

            def inproj_fm(wv, sl, j0, ncols_used=128):
                pa = nbank()
                for kc in range(16):
                    op("pe", lambda e, kc=kc, pa=pa: e.matmul(pa[:, 0:T], lhsT=wv[:, kc, j0:j0 + 128], rhs=hT[:, kc, 0:T],
                                                              start=(kc == 0), stop=(kc == 15)), r=[sl, hT], w=[pa])
                return pa

            def rows_tm(wv, sl, ncols, colbase, out_dram):
                pa = nbank()
                c = nch - 1
                for kc in range(16):
                    op("pe", lambda e, kc=kc, pa=pa: e.matmul(pa[0:q, 0:ncols], lhsT=hT[:, kc, c * q:(c + 1) * q], rhs=wv[:, kc, 0:ncols],
                                                              start=(kc == 0), stop=(kc == 15)), r=[sl, hT], w=[pa])
                cr = crow[bank_i[0] % 2]
                op("act", lambda e, pa=pa, cr=cr: e.activation(out=cr[0:q, 0:ncols], in_=pa[0:q, 0:ncols], func=AF.Copy), r=[pa], w=[cr])
                P.dma("pool", lambda e, cr=cr: e.dma_start(out=out_dram[0:q, colbase:colbase + ncols], in_=cr[0:q, 0:ncols]), r=[cr], w=[cr])

            def stage1(g, si):
                xb = xbcTs[si]
                szs = szS[si]
                pend = [None, None, None]

                def step(job):
                    if pend[2] is not None and pend[2][3] is not None:
                        pend[2][3]()
                    if pend[0] is not None:
                        pend[0][1]()
                    if pend[1] is not None and pend[1][2] is not None:
                        pend[1][2]()
                    if job is not None:
                        job[0]()
                    pend[2], pend[1], pend[0] = pend[1], pend[0], job

                if smp:
                    P.dma("pool", lambda e: e.dma_start(out=cvst[:, 0:512], in_=cst_conv[:, g * 512:(g + 1) * 512]), w=[cvst])
                    P.dma("pool", lambda e: e.dma_start(out=cvst[:, 512:640], in_=cst_conv[:, DI + g * 128:DI + (g + 1) * 128]), w=[cvst])
                    P.dma("pool", lambda e: e.dma_start(out=cvst[:, 640:768], in_=cst_conv[:, DI + 1024 + g * 128:DI + 1024 + (g + 1) * 128]), w=[cvst])
                specs = [(C_X + g * 512, 256, [0, 1]), (C_X + g * 512 + 256, 256, [2, 3]), (C_B + g * 128, 128, [4])]
                if not warm:
                    specs.append((C_C + g * 128, 128, [5]))
                njob = [0]
                for (col0, ncols, jjs) in specs:
                    for jn, jj in enumerate(jjs):
                        cc = (col0 - C_X) // 128 + jn
                        n = njob[0]
                        njob[0] += 1
                        st_ = {}
                        rw = raws[jj]
                        rv = rawv[:, jj]
                        ac = acc[n % 2]
                        acv = v3(ac[:, 0:T])
                        sg_ = sgts[n % 2]

                        def pe(col0=col0, ncols=ncols, jn=jn, jj=jj, st_=st_, last_of_slab=(jn == len(jjs) - 1)):
                            if jn == 0:
                                st_["slab"] = W.next(w_in_v[:, :, col0:col0 + ncols], 16, ncols)
                                slabref[0] = st_["slab"]
                            sl, wv = slabref[0]
                            st_["pa"] = inproj_fm(wv, sl, jn * 128)
                            if smp:
                                op("pe", lambda e: e.transpose(out=tp_ap[:, 0, 0:48], in_=cvst[0:48, jj * 128:(jj + 1) * 128],
                                                               identity=identb[0:48, 0:48]), r=[cvst, identb], w=[Mb])
                                op("act", lambda e: e.activation(out=rv[:, :, 0:3], in_=tp_ap[:, 0, 0:48].rearrange("p (s k) -> p s k", k=3),
                                                                 func=AF.Copy), r=[Mb], w=[rw])
                            if last_of_slab and emit_rows and not warm:
                                rows_tm(wv, sl, ncols, col0 - C_X, o_convs if smp else o_convp)

                        def pa_(jj=jj, cc=cc, st_=st_, rw=rw, rv=rv, ac=ac, acv=acv):
                            pa = st_["pa"]
                            if smp:
                                pass
                            elif first:
                                op("dve", lambda e: e.memset(rv[:, :, 0:3], 0.0), w=[rw])
                            else:
                                op("act", lambda e: e.activation(out=rv[:, 0, 0:3], in_=carx[:, cc, :], func=AF.Copy), r=[carx], w=[rw])
                            op("act", lambda e: e.activation(out=rv[:, :, 3:3 + L], in_=v3(pa[:, 0:T]), func=AF.Copy), r=[pa], w=[rw])
                            if not smp:
                                op("act", lambda e: e.activation(out=carx[:, cc, :], in_=rv[:, 0, L:L + 3], func=AF.Copy), r=[rw], w=[carx])
                            op("act", lambda e: e.activation(out=acv, in_=rv[:, :, 0:L], func=AF.Copy, scale=pv[:, PV_CW + cc * 4:PV_CW + cc * 4 + 1]),
                               r=[rw, pv], w=[ac])
                            for k in range(1, 4):
                                op("dve", lambda e, k=k: e.scalar_tensor_tensor(
                                    out=acv, in0=rv[:, :, k:k + L], scalar=pv[:, PV_CW + cc * 4 + k:PV_CW + cc * 4 + k + 1], in1=acv,
                                    op0=ALU.mult, op1=ALU.add), r=[rw, pv, ac], w=[ac])

                        def pb_(cc=cc, ac=ac, sg_=sg_):
                            sig_chain(sg_, sg_[:, 0:T], ac, ac[:, 0:T], negb=pvn[:, cc:cc + 1])

                        def pc_(jj=jj, cc=cc, ac=ac, sg_=sg_):
                            op("dve", lambda e: e.scalar_tensor_tensor(out=xb[:, jj, 0:T], in0=ac[:, 0:T], scalar=pv[:, PV_CB + cc:PV_CB + cc + 1],
                                                                       in1=sg_[:, 0:T], op0=ALU.add, op1=ALU.mult), r=[ac, pv, sg_], w=[xb])

                        step((pe, pa_, pb_, pc_))
                        yield
                if not warm:
                    for zs in range(2):
                        for c in range(nch):
                            n = njob[0]
                            njob[0] += 1
                            st_ = {}
                            sg_ = sgts[n % 2]

                            def pe(zs=zs, c=c, st_=st_):
                                if c == 0:
                                    slabref[0] = W.next(w_in_v[:, :, C_Z + g * 512 + zs * 256:C_Z + g * 512 + (zs + 1) * 256], 16, 256)
                                sl, wv = slabref[0]
                                pa = nbank()
                                st_["pa"] = pa
                                for kc in range(16):
                                    op("pe", lambda e, kc=kc: e.matmul(pa[0:q, 0:256], lhsT=hT[:, kc, c * q:(c + 1) * q], rhs=wv[:, kc, :],
                                                                       start=(kc == 0), stop=(kc == 15)), r=[sl, hT], w=[pa])

                            def pa_(st_=st_, sg_=sg_):
                                pa = st_["pa"]
                                sig_chain(sg_, sg_[0:q, 0:256], pa, pa[0:q, 0:256])

                            def pb_(zs=zs, c=c, st_=st_, sg_=sg_):
                                pa = st_["pa"]
                                op("dve", lambda e: e.tensor_tensor(out=szs[0:q, c, zs * 256:(zs + 1) * 256], in0=sg_[0:q, 0:256], in1=pa[0:q, 0:256],
                                                                    op=ALU.mult), r=[sg_, pa], w=[szs])

                            step((pe, pa_, pb_, None))
                            yield
                for _ in range(3):
                    step(None)
                    yield

            def pieceA(g, c):
                si = g % 2
                xb = xbcTs[si]
                k = g * nch + c
                xbt, xd, xs2, cb2, sr2 = xbtoks[k % 3], xdts[k % 2], xss[k % 2], cbms[k % 2], segrs[k % 2]
                tk = slice(c * q, (c + 1) * q)
                hs = slice(g * 8, g * 8 + 8)
                S = Sst[g]
                if c == 0 and not smp:
                    if first and warm:
                        op("dve", lambda e: e.memset(S[:], 0.0), w=[S])
                    if first and kind == "main":
                        op("dve", lambda e: e.tensor_scalar(out=S[:], in0=S[:], scalar1=cst[:, K_ROLE:K_ROLE + 1], scalar2=None,
                                                            op0=ALU.mult), r=[S, cst], w=[S])
                for j in range(5):
                    op("pe", lambda e, j=j: e.transpose(out=tp_ap[0:q, j, :], in_=xb[:, j, tk], identity=identb[:]),
                       r=[xb, identb], w=[Mb])
                op("act", lambda e: e.activation(out=xbt[0:q, :].rearrange("p (a b) -> p a b", b=128), in_=tp_ap[0:q, 0:5, :], func=AF.Copy),
                   r=[Mb], w=[xbt])
                xv = xbt[0:q, 0:512].rearrange("p (r d) -> p r d", d=HP)
                op("dve", lambda e: e.tensor_tensor(out=xd[0:q, :].rearrange("p (r d) -> p r d", d=HP), in0=xv,
                                                    in1=bcl(dtt[0:q, c, hs], HP), op=ALU.mult), r=[xbt, dtt], w=[xd])
                op("dve", lambda e: e.tensor_tensor(out=xs2[0:q, :].rearrange("p (r d) -> p r d", d=HP),
                                                    in0=xd[0:q, :].rearrange("p (r d) -> p r d", d=HP),
                                                    in1=bcl(toend[0:q, c, hs], HP), op=ALU.mult), r=[xd, toend], w=[xs2])
                if not warm:
                    op("pe", lambda e: e.matmul(Cb[0:q, 0:q], lhsT=xb[:, 4, tk], rhs=xb[:, 5, tk], start=True, stop=True), r=[xb], w=[Cb])
                    op("dve", lambda e: e.tensor_tensor(out=cb2[0:q, 0:q], in0=Cb[0:q, 0:q], in1=cK(kTM, q, q), op=ALU.mult),
                       r=[Cb, cst], w=[cb2])
                    op("dve", lambda e: e.tensor_tensor(out=sr2[0:q, :, 0:q], in0=bcm(cK(kTM, q, q), 8),
                                                        in1=bcl(la[0:q, c, hs], q), op=ALU.mult), r=[cst, la], w=[sr2])

            def pieceB(g, c):
                if warm:
                    return
                k = g * nch + c
                cb2, sr2, w2 = cbms[k % 2], segrs[k % 2], wTs[k % 2]
                for r in range(8):
                    sgb = Sg[r // 4]
                    op("pe", lambda e, r=r, sgb=sgb: e.matmul(sgb[0:q, (r % 4) * 128:(r % 4) * 128 + q], lhsT=cK(kUM, q, q),
                                                              rhs=sr2[0:q, r, 0:q], start=True, stop=True), r=[cst, sr2], w=[sgb])
                for h2 in range(2):
                    sgb = Sg[h2]
                    op("act", lambda e, h2=h2, sgb=sgb: e.activation(
                        out=dec[0:q, h2 * 4:(h2 + 1) * 4, 0:q], in_=sgb[0:q, :].rearrange("p (a b) -> p a b", b=128)[:, :, 0:q],
                        func=AF.Exp), r=[sgb], w=[dec])
                op("dve", lambda e: e.tensor_tensor(out=w2[0:q, :, 0:q], in0=dec[0:q, :, 0:q], in1=bcm(cb2[0:q, 0:q], 8), op=ALU.mult),
                   r=[dec, cb2], w=[w2])

            def pieceC(g, c):
                si = g % 2
                xb = xbcTs[si]
                szs = szS[si]
                k = g * nch + c
                xbt, xd, xs2, w2, gn2 = xbtoks[k % 3], xdts[k % 2], xss[k % 2], wTs[k % 2], gns[k % 2]
                tk = slice(c * q, (c + 1) * q)
                hs = slice(g * 8, g * 8 + 8)
                S = Sst[g]
                xv = xbt[0:q, 0:512].rearrange("p (r d) -> p r d", d=HP)
                if not warm:
                    for r in range(8):
                        op("pe", lambda e, r=r: e.matmul(Yb[0:q, r * HP:(r + 1) * HP], lhsT=w2[0:q, r, 0:q], rhs=xd[0:q, r * HP:(r + 1) * HP],
                                                         start=True, stop=True), r=[w2, xd], w=[Yb])
                    if not smp:
                        if c == 0 and kind == "main":
                            op("act", lambda e: e.activation(out=Sbf[:], in_=S[:], func=AF.Copy), r=[S], w=[Sbf])
                        op("pe", lambda e: e.matmul(YOb[0:q, :], lhsT=xb[:, 5, tk], rhs=Sbf[:], start=True, stop=True),
                           r=[xb, Sbf], w=[YOb])
                if not smp:
                    op("pe", lambda e: e.matmul(STb[:, :], lhsT=xbt[0:q, 512:640], rhs=xs2[0:q, :], start=True, stop=True),
                       r=[xbt, xs2], w=[STb])
                    Sv = S[:].rearrange("p (r d) -> p r d", d=HP)
                    op("dve", lambda e: e.tensor_tensor(out=Sv, in0=Sv, in1=bcl(cdb[:, c, hs], HP), op=ALU.mult), r=[S, cdb], w=[S])
                    op("dve", lambda e: e.tensor_tensor(out=S[:], in0=S[:], in1=STb[:, :], op=ALU.add), r=[S, STb], w=[S])
                    need_sbf = (not warm) and c + 1 < nch
                else:
                    op("act", lambda e: e.activation(out=ydg[:, :], in_=Yb[:, :], func=AF.Copy), r=[Yb], w=[ydg])
                    ctsrc = xb[:, 5, 0:128].rearrange("p (s t) -> p s t", t=DSEQ)
                    base = CTm.t[:, 0, 0:DSEQ]
                    ctdst = bass.AP(base.tensor, base.offset, [list(base.ap[0]), [128 + DSEQ, NSEQ], [1, DSEQ]])
                    op("dve", lambda e: e.tensor_copy(out=ctdst, in_=ctsrc), r=[xb], w=[CTm])
                    op("dve", lambda e: e.tensor_tensor(out=Btm[:], in0=bcm(xbt[:, 512:640], NSEQ), in1=bcl(cst[:, K_BLK:K_BLK + 16], 128),
                                                        op=ALU.mult), r=[xbt, cst], w=[Btm])

                    def sD(s):
                        it = g * NSEQ + s
                        so = sst[it % 8]
                        srcv = cst_ssm[s, g * 8:(g + 1) * 8].rearrange("(q two) p n -> (two p) q n", two=2)
                        P.dma("sp", lambda e: e.dma_start(out=so[:], in_=srcv), w=[so])

                    def sA(s):
                        it = g * NSEQ + s
                        so, sbb, stt_, tpb = sst[it % 8], sstb[it % 2], stb[it % 2], Sg[it % 2]
                        tpv = tpb.t[:, 0:256].bitcast(BF16).rearrange("p (a b) -> p a b", b=128)
                        op("act", lambda e: e.activation(out=sbb[:], in_=so[:], func=AF.Copy), r=[so], w=[sbb])
                        for qq in range(4):
                            op("pe", lambda e, qq=qq: e.transpose(out=tpv[:, qq, :], in_=sbb[:, qq, :], identity=identb[:]),
                               r=[sbb, identb], w=[tpb])
                        op("act", lambda e: e.activation(out=stt_[:].rearrange("p (a b) -> p a b", b=128), in_=tpv[:, 0:4, :], func=AF.Copy),
                           r=[tpb], w=[stt_])

                    def sB(s):
                        it = g * NSEQ + s
                        so, stt_ = sst[it % 8], stb[it % 2]
                        stp = (STb, Yb)[it % 2]
                        dstv = o_ssms[s, g * 8:(g + 1) * 8].rearrange("(q two) p n -> (two p) q n", two=2)
                        op("pe", lambda e: e.matmul(YOb[:, :], lhsT=CTm[:, s, :], rhs=stt_[:], start=(s == 0), stop=(s == NSEQ - 1)),
                           r=[CTm, stt_], w=[YOb])
                        for qq in range(4):
                            op("pe", lambda e, qq=qq: e.matmul(stp[:, qq * 128:(qq + 1) * 128], lhsT=xs2[:, qq * 128:(qq + 1) * 128],
                                                               rhs=Btm[:, s, :], start=True, stop=True), r=[xs2, Btm], w=[stp])
                        op("dve", lambda e: e.tensor_tensor(out=so[:], in0=so[:], in1=bcl(cds[:, s, g * 4:(g + 1) * 4], 128),
                                                            op=ALU.mult), r=[so, cds], w=[so])
                        op("dve", lambda e: e.tensor_tensor(out=so[:], in0=so[:], in1=stp[:, :].rearrange("p (a b) -> p a b", b=128),
                                                            op=ALU.add), r=[so, stp], w=[so])
                        P.dma("sp", lambda e: e.dma_start(out=dstv, in_=so[:]), r=[so], w=[so])

                    for s0 in range(6):
                        sD(s0)
                    sA(0)
                    for s in range(NSEQ):
                        if s + 6 < NSEQ:
                            sD(s + 6)
                        if s + 1 < NSEQ:
                            sA(s + 1)
                        sB(s)
                        if s % 8 == 7:
                            yield
                if not warm:
                    t1v = t1[0:q, :].rearrange("p (r d) -> p r d", d=HP)
                    op("dve", lambda e: e.tensor_tensor(out=t1v, in0=YOb[0:q, :].rearrange("p (r d) -> p r d", d=HP),
                                                        in1=bcl(eacum[0:q, c, hs], HP), op=ALU.mult), r=[YOb, eacum], w=[t1])
                    ysrc = ydg if smp else Yb
                    op("dve", lambda e: e.tensor_tensor(out=t1[0:q, :], in0=t1[0:q, :], in1=ysrc[0:q, :], op=ALU.add), r=[t1, ysrc], w=[t1])
                    op("dve", lambda e: e.tensor_tensor(out=t3[0:q, :].rearrange("p (r d) -> p r d", d=HP), in0=xv,
                                                        in1=bcl(bv[0:q, BV_DSK + g * 8:BV_DSK + g * 8 + 8], HP), op=ALU.mult),
                       r=[xbt, bv], w=[t3])
                    op("dve", lambda e: e.tensor_tensor(out=t1[0:q, :], in0=t1[0:q, :], in1=t3[0:q, :], op=ALU.add), r=[t1, t3], w=[t1])
                    op("dve", lambda e: e.tensor_tensor(out=t1[0:q, :], in0=t1[0:q, :], in1=szs[0:q, c, :], op=ALU.mult), r=[t1, szs], w=[t1])
                    yield "C2"
                    if (not smp) and need_sbf:
                        op("act", lambda e: e.activation(out=Sbf[:], in_=S[:], func=AF.Copy), r=[S], w=[Sbf])
                    s4 = rms_rstd(t1, q, 3, 512 ** -0.5, t3)
                    op("act", lambda e: e.activation(out=gn2[0:q, :], in_=t1[0:q, :], func=AF.Copy, scale=s4[0:q, 2:3]),
                       r=[t1, s4], w=[gn2])
                if last and kind == "main" and c == nch - 1:
                    P.dma("pool", lambda e: e.dma_start(out=o_ssmTp[:, g * 512:(g + 1) * 512], in_=S[:]), r=[S], w=[S])

            def pieceD(g, c):
                if warm:
                    return
                k = g * nch + c
                gn2 = gns[k % 2]
                tk = slice(c * q, (c + 1) * q)
                for j in range(4):
                    op("pe", lambda e, j=j: e.transpose(out=tp_ap[:, j, 0:q], in_=gn2[0:q, j * 128:(j + 1) * 128], identity=identb[0:q, 0:q]),
                       r=[gn2, identb], w=[Mb])
                op("dve", lambda e: e.tensor_tensor(out=ynT[:, g * 4:(g + 1) * 4, tk], in0=tp_ap[:, 0:4, 0:q],
                                                    in1=bcl(pv[:, PV_SNW + g * 4:PV_SNW + (g + 1) * 4], q), op=ALU.mult),
                   r=[Mb, pv], w=[ynT])

            if smp:
                for al in sst_extra:
                    if xio[2].res.w is not None:
                        al.res.r.append(xio[2].res.w)
                    al.res.r.extend(xio[2].res.r)
            for al in xio1_alias:
                if xio[1].res.w is not None:
                    al.res.r.append(xio[1].res.w)
                al.res.r.extend(xio[1].res.r)
            its = [(g, c) for g in range(NG) for c in range(nch)]
            NIT = len(its)
            s1 = {0: stage1(0, 0)}
            for _ in s1[0]:
                pass
            s1done = {0}

            def unit(gcur, n=1, flush_g=None):
                gg = flush_g if flush_g is not None else gcur + 1
                if gg >= NG or gg in s1done:
                    return
                if gg not in s1:
                    s1[gg] = stage1(gg, gg % 2)
                cnt = 0
                while flush_g is not None or cnt < n:
                    try:
                        next(s1[gg])
                    except StopIteration:
                        s1done.add(gg)
                        return
                    cnt += 1

            def drive(gen_or_none, gcur):
                if gen_or_none is None:
                    return
                for _ in gen_or_none:
                    unit(gcur)

            pieceA(*its[0])
            for t in range(NIT):
                g, c = its[t]
                if t + 1 < NIT:
                    gn_, cn_ = its[t + 1]
                    if gn_ != g:
                        unit(g, flush_g=gn_)
                    pieceA(gn_, cn_)
                unit(g)
                pieceB(g, c)
                unit(g)
                if t >= 1:
                    pieceD(*its[t - 1])
                    unit(g)
                drive(pieceC(g, c), g)
                unit(g)
            pieceD(*its[NIT - 1])
            if smp:
                for al in sst_extra:
                    if al.res.w is not None:
                        xio[2].res.r.append(al.res.w)
                    xio[2].res.r.extend(al.res.r)
            for al in xio1_alias:
                if al.res.w is not None:
                    xio[1].res.r.append(al.res.w)
                xio[1].res.r.extend(al.res.r)
            if warm:
                return
            banks[:] = [A[0], A[1], Sg[0], Sg[1], STb]
            for m2 in range(8):
                sl2, wv2 = W.next(w_in_v[:, :, C_GA + m2 * 256:C_GA + (m2 + 1) * 256], 16, 256)
                for hf in range(2):
                    p1 = nbank()
                    for kc in range(16):
                        op("pe", lambda e, kc=kc: e.matmul(p1[:, 0:T], lhsT=wv2[:, kc, hf * 128:(hf + 1) * 128], rhs=hT[:, kc, 0:T],
                                                           start=(kc == 0), stop=(kc == 15)), r=[sl2, hT], w=[p1])
                    sig_chain(sgts[hf], sgts[hf][:, 0:T], p1, p1[:, 0:T])
                for hf in range(2):
                    m = m2 * 2 + hf
                    sl1, wv1 = W.next(w_so_v[:, :, m * 128:(m + 1) * 128], 32, 128)
                    p0 = nbank()
                    for kc in range(32):
                        op("pe", lambda e, kc=kc: e.matmul(p0[:, 0:T], lhsT=wv1[:, kc, :], rhs=ynT[:, kc, 0:T], start=(kc == 0), stop=(kc == 31)),
                           r=[sl1, ynT], w=[p0])
                    op("dve", lambda e: e.tensor_tensor(out=mrg[:, m, 0:T], in0=sgts[hf][:, 0:T], in1=p0[:, 0:T], op=ALU.mult),
                       r=[sgts[hf], p0], w=[mrg])
            urow = xio[2]
            for j2 in range(8):
                c = nch - 1
                slc, wvc = W.next(w_in_v[:, :, C_SC + j2 * 256:C_SC + (j2 + 1) * 256], 16, 256)
                for hf in range(2):
                    pa = inproj_fm(wvc, slc, hf * 128)
                    op("act", lambda e, pa=pa, hf=hf: e.activation(out=tcc[hf][:, 0:T], in_=pa[:, 0:T], func=AF.Copy), r=[pa], w=[tcc[hf]])
                if emit_rows:
                    for kc in range(16):
                        op("pe", lambda e, kc=kc: e.matmul(Yb[0:q, 0:256], lhsT=hT[:, kc, c * q:(c + 1) * q], rhs=wvc[:, kc, :],
                                                           start=(kc == 0), stop=(kc == 15)), r=[slc, hT], w=[Yb])
                    op("act", lambda e: e.activation(out=tcr[0:q, :], in_=Yb[0:q, 0:256], func=AF.Copy), r=[Yb], w=[tcr])
                slh, wvh = W.next(w_in_v[:, :, C_SH + j2 * 256:C_SH + (j2 + 1) * 256], 16, 256)
                for hf in range(2):
                    j = j2 * 2 + hf
                    ru = rawu[hf]
                    rawuv = rawuvs[hf]
                    pa = inproj_fm(wvh, slh, hf * 128)
                    if smp:
                        op("pe", lambda e, j=j: e.transpose(out=tp_ap[:, 0, 0:32], in_=scst[0:32, j * 128:(j + 1) * 128], identity=identb[0:32, 0:32]),
                           r=[scst, identb], w=[Mb])
                        op("dve", lambda e: e.tensor_copy(out=rawuv[:, :, 0:2], in_=tp_ap[:, 0, 0:32].rearrange("p (s k) -> p s k", k=2)),
                           r=[Mb], w=[ru])
                    elif first:
                        op("dve", lambda e: e.memset(rawuv[:, :, 0:2], 0.0), w=[ru])
                    else:
                        op("dve", lambda e, j=j: e.tensor_copy(out=rawuv[:, 0, 0:2], in_=caru[:, j, :]), r=[caru], w=[ru])
                    op("dve", lambda e, pa=pa: e.tensor_tensor(out=rawuv[:, :, 2:2 + L], in0=v3(tcc[hf][:, 0:T]), in1=v3(pa[:, 0:T]), op=ALU.mult),
                       r=[tcc[hf], pa], w=[ru])
                    if not smp:
                        op("dve", lambda e, j=j: e.tensor_copy(out=caru[:, j, :], in_=rawuv[:, 0, L:L + 2]), r=[ru], w=[caru])
                    ac = acc[hf]
                    acv = v3(ac[:, 0:T])
                    op("dve", lambda e, j=j, acv=acv: e.tensor_scalar(out=acv, in0=rawuv[:, :, 0:L], scalar1=pv[:, PV_SCW + j * 3:PV_SCW + j * 3 + 1],
                                                                      scalar2=None, op0=ALU.mult), r=[ru, pv], w=[ac])
                    for k in range(1, 3):
                        op("dve", lambda e, j=j, acv=acv, k=k: e.scalar_tensor_tensor(
                            out=acv, in0=rawuv[:, :, k:k + L], scalar=pv[:, PV_SCW + j * 3 + k:PV_SCW + j * 3 + k + 1], in1=acv,
                            op0=ALU.mult, op1=ALU.add), r=[ru, pv, ac], w=[ac])
                if emit_rows:
                    for kc in range(16):
                        op("pe", lambda e, kc=kc: e.matmul(YOb[0:q, 0:256], lhsT=hT[:, kc, c * q:(c + 1) * q], rhs=wvh[:, kc, :],
                                                           start=(kc == 0), stop=(kc == 15)), r=[slh, hT], w=[YOb])
                    op("dve", lambda e: e.tensor_tensor(out=urow[0:q, j2 * 256:(j2 + 1) * 256], in0=tcr[0:q, :], in1=YOb[0:q, 0:256], op=ALU.mult),
                       r=[tcr, YOb], w=[urow])
                slz, wvz = W.next(w_in_v[:, :, C_SZ + j2 * 256:C_SZ + (j2 + 1) * 256], 16, 256)
                for hf in range(2):
                    pa = inproj_fm(wvz, slz, hf * 128)
                    sig_chain(szz[hf], szz[hf][:, 0:T], pa, pa[:, 0:T])
                    op("dve", lambda e, pa=pa: e.tensor_tensor(out=szz[hf][:, 0:T], in0=szz[hf][:, 0:T], in1=pa[:, 0:T], op=ALU.mult),
                       r=[szz[hf], pa], w=[szz[hf]])
                slb, wvb = W.next(w_in_v[:, :, C_SB + j2 * 256:C_SB + (j2 + 1) * 256], 16, 256)
                for hf in range(2):
                    j = j2 * 2 + hf
                    ac = acc[hf]
                    pa = inproj_fm(wvb, slb, hf * 128)
                    op("dve", lambda e, pa=pa, ac=ac: e.tensor_tensor(out=ac[:, 0:T], in0=ac[:, 0:T], in1=pa[:, 0:T], op=ALU.mult), r=[ac, pa], w=[ac])
                    op("dve", lambda e, j=j, ac=ac: e.tensor_tensor(out=ynT[:, j, 0:T], in0=ac[:, 0:T], in1=szz[hf][:, 0:T], op=ALU.mult),
                       r=[ac, szz[hf]], w=[ynT])
            if emit_rows:
                od = o_sconvs if smp else o_sconvp
                P.dma("pool", lambda e: e.dma_start(out=od[0:q, :], in_=urow[0:q, :]), r=[urow], w=[urow])
            for m2 in range(8):
                sl2, wv2 = W.next(w_in_v[:, :, C_GB + m2 * 256:C_GB + (m2 + 1) * 256], 16, 256)
                for hf in range(2):
                    p1 = nbank()
                    for kc in range(16):
                        op("pe", lambda e, kc=kc: e.matmul(p1[:, 0:T], lhsT=wv2[:, kc, hf * 128:(hf + 1) * 128], rhs=hT[:, kc, 0:T],
                                                           start=(kc == 0), stop=(kc == 15)), r=[sl2, hT], w=[p1])
                    sig_chain(sgts[hf], sgts[hf][:, 0:T], p1, p1[:, 0:T])
                sl1, wv1 = W.next(w_sc_v[:, :, m2 * 256:(m2 + 1) * 256], 16, 256)
                for hf in range(2):
                    m = m2 * 2 + hf
                    p0 = nbank()
                    for kc in range(16):
                        op("pe", lambda e, kc=kc: e.matmul(p0[:, 0:T], lhsT=wv1[:, kc, hf * 128:(hf + 1) * 128], rhs=ynT[:, kc, 0:T],
                                                           start=(kc == 0), stop=(kc == 15)), r=[sl1, ynT], w=[p0])
                    op("dve", lambda e: e.tensor_tensor(out=sgts[hf][:, 0:T], in0=sgts[hf][:, 0:T], in1=p0[:, 0:T], op=ALU.mult),
                       r=[sgts[hf], p0], w=[sgts[hf]])
                    op("dve", lambda e: e.tensor_tensor(out=mrg[:, m, 0:T], in0=mrg[:, m, 0:T], in1=sgts[hf][:, 0:T], op=ALU.add),
                       r=[mrg, sgts[hf]], w=[mrg])
            for c in range(nch):
                xt = xio[c]
                P.dma("pool", lambda e, xt=xt, c=c: e.dma_start(out=xt[0:q, :], in_=xsrc[c * q:(c + 1) * q, :]), w=[xt])
            for s8 in range(8):
                sl, wv = W.next(w_o_v[:, :, s8 * 256:(s8 + 1) * 256], 16, 256)
                for c in range(nch):
                    pa = nbank()
                    for kc in range(16):
                        op("pe", lambda e, kc=kc, c=c, pa=pa, wv=wv: e.matmul(pa[0:q, 0:256], lhsT=mrg[:, kc, c * q:(c + 1) * q], rhs=wv[:, kc, :],
                                                                              start=(kc == 0), stop=(kc == 15)), r=[sl, mrg], w=[pa])
                    xt = xio[c]
                    op("dve", lambda e, xt=xt, pa=pa, s8=s8: e.tensor_tensor(out=xt[0:q, s8 * 256:(s8 + 1) * 256], in0=xt[0:q, s8 * 256:(s8 + 1) * 256],
                                                                             in1=pa[0:q, 0:256], op=ALU.add), r=[xt, pa], w=[xt])
            for c in range(nch):
                xt = xio[c]
                s = rms_rstd(xt, q, c, D ** -0.5, hb)
                op("dve", lambda e, xt=xt, s=s: e.scalar_tensor_tensor(out=xt[0:q, :], in0=xt[0:q, :], scalar=s[0:q, 2:3], in1=bv[0:q, BV_FNW:BV_FNW + D],
                                                                       op0=ALU.mult, op1=ALU.mult), r=[xt, s, bv], w=[xt])
                P.dma("pool", lambda e, xt=xt, c=c: e.dma_start(out=ysink[c * q:(c + 1) * q, :], in_=xt[0:q, :]), r=[xt], w=[xt])

        def emit_all():
            setup()
            for b in range(NBLK):
                block(xw_warm[b * TMAX:(b + 1) * TMAX, :], BLK, Q, "warm", b == 0, b == NBLK - 1, None)
            for b in range(NBLK):
                block(xw_main[b * TMAX:(b + 1) * TMAX, :], BLK, Q, "main", b == 0, b == NBLK - 1, o_yp[b * TMAX:(b + 1) * TMAX, :])
            block(xsmp, 1, 128, "sample", False, False, o_ys)

        P.dry = True
        emit_all()
        P.dry = False
        W.start()
        emit_all()
        allt = xio + sst + crow + Sst
        P.finish("pool", allt)
        P.finish("sp", W.scr_res + sst)
        P.emit()
    return nc


_CACHE = {}


def _consts():
    k = np.arange(128)
    c = np.zeros((128, NCST), np.float32)
    c[:, K_ID:K_ID + 128] = np.eye(128)
    c[:, K_TM:K_TM + 128] = (k[:, None] <= k[None, :])
    c[:, K_UM:K_UM + 128] = (k[:, None] > k[None, :])
    same = (k[:, None] // DSEQ) == (k[None, :] // DSEQ)
    c[:, K_TMS:K_TMS + 128] = (k[:, None] <= k[None, :]) & same
    c[:, K_UMS:K_UMS + 128] = (k[:, None] > k[None, :]) & same
    c[:, K_ONE:K_ONE + 128] = 1.0
    c[:, K_EM:K_EM + 128] = (k[None, :] < 64)
    c[:, K_OM:K_OM + 128] = (k[None, :] >= 64)
    c[:, K_BLK:K_BLK + 16] = (k[:, None] // DSEQ) == np.arange(16)[None, :]
    return c


def kernel(x_prompt, x_sample, state_ssd_conv, state_ssm, state_sconv, meta_tokens, norm_w, w_in, ssd_conv_w,
           ssd_conv_b, dt_bias, a_log, d_skip, ssd_norm_w, w_ssd_out, sconv_w, w_sconv_out, w_o, final_norm_w):
    f = lambda a: np.ascontiguousarray(np.asarray(a, dtype=np.float32))
    x_prompt, x_sample = f(x_prompt), f(x_sample)
    if "nc" not in _CACHE:
        _CACHE["nc"] = build_program()
    nc = _CACHE["nc"]
    pvec = np.zeros((128, NPV), np.float32)
    cw = f(ssd_conv_w)[0]
    pvec[:, PV_CW:PV_CW + 192] = cw.reshape(4, 48, 128).transpose(2, 1, 0).reshape(128, 192)
    pvec[:, PV_CB:PV_CB + 48] = f(ssd_conv_b)[0].reshape(48, 128).T
    pvec[:, PV_SCW:PV_SCW + 48] = f(sconv_w)[0].reshape(3, 16, 128).transpose(2, 1, 0).reshape(128, 48)
    pvec[:, PV_SNW:PV_SNW + 32] = f(ssd_norm_w)[0].reshape(32, 128).T
    pvec[:, PV_NW:PV_NW + 16] = f(norm_w)[0].reshape(16, 128).T
    bvec = np.zeros((1, NBV), np.float32)
    bvec[0, BV_DTB:BV_DTB + 64] = f(dt_bias)[0]
    bvec[0, BV_ALOG:BV_ALOG + 64] = f(a_log)[0]
    bvec[0, BV_DSK:BV_DSK + 64] = f(d_skip)[0]
    bvec[0, BV_FNW:BV_FNW + D] = f(final_norm_w)
    cbase = _consts()
    w_in0, w_so0, w_sc0, w_o0 = f(w_in)[0], f(w_ssd_out)[0], f(w_sconv_out)[0], f(w_o)[0]
    meta = f(meta_tokens)
    zeros3 = np.zeros((3, D), np.float32)
    in_maps = []
    for c in range(NCORES):
        b, half = c // 2, c % 2
        full = np.concatenate([zeros3, meta, x_prompt[b]], axis=0)
        win0 = full[0:WIN]
        win1 = full[HALF:HALF + WIN]
        cc = cbase.copy()
        cc[:, K_ROLE] = float(half)
        sl = slice(c * NSEQ, (c + 1) * NSEQ)
        in_maps.append({
            "xw_main": np.ascontiguousarray(win1 if half else win0),
            "xw_warm": np.ascontiguousarray(win0) if half else np.zeros((WIN, D), np.float32),
            "xsmp": np.ascontiguousarray(x_sample[sl].reshape(128, D)),
            "consts": cc, "pvec": pvec, "bvec": bvec,
            "cst_conv": np.ascontiguousarray(f(state_ssd_conv)[0, sl].reshape(NSEQ * 3, CONVD)),
            "cst_ssm": np.ascontiguousarray(f(state_ssm)[0, sl]),
            "cst_sconv": np.ascontiguousarray(f(state_sconv)[0, sl].reshape(NSEQ * 2, D)),
            "w_in": w_in0, "w_ssd_out": w_so0, "w_sconv_out": w_sc0, "w_o": w_o0,
        })
    res = run_bass_kernel_spmd(nc, in_maps, core_ids=list(range(NCORES)))
    R = res.results
    nb = x_prompt.shape[0]
    y_prompt = np.zeros((nb, SEQ, D), np.float32)
    conv_p = np.zeros((1, nb, 3, CONVD), np.float32)
    ssm_p = np.zeros((1, nb, NH, HP, NS), np.float32)
    sconv_p = np.zeros((1, nb, 2, D), np.float32)
    for b in range(nb):
        y_prompt[b, 0:HALF - META] = R[2 * b]["o_yp"][3 + META:WIN]
        y_prompt[b, HALF - META:] = R[2 * b + 1]["o_yp"][3:WIN]
        conv_p[0, b] = R[2 * b + 1]["o_convp"][Q - 3:Q]
        ssm_p[0, b] = R[2 * b + 1]["o_ssmTp"].T.reshape(NH, HP, NS)
        sconv_p[0, b] = R[2 * b + 1]["o_sconvp"][Q - 2:Q]
    y_sample = np.concatenate([R[c]["o_ys"].reshape(NSEQ, DSEQ, D) for c in range(NCORES)], axis=0)
    conv_s = np.concatenate([R[c]["o_convs"].reshape(NSEQ, DSEQ, CONVD)[:, DSEQ - 3:] for c in range(NCORES)], axis=0)[None]
    ssm_s = np.concatenate([R[c]["o_ssms"] for c in range(NCORES)], axis=0)[None]
    sconv_s = np.concatenate([R[c]["o_sconvs"].reshape(NSEQ, DSEQ, D)[:, DSEQ - 2:] for c in range(NCORES)], axis=0)[None]
    return (y_prompt, y_sample, np.ascontiguousarray(conv_p), np.ascontiguousarray(ssm_p), np.ascontiguousarray(sconv_p),
            np.ascontiguousarray(conv_s), np.ascontiguousarray(ssm_s), np.ascontiguousarray(sconv_s))
```
